# Optimizing a Trainium2 kernel written in Bass

```python
import math
import jax, jax.numpy as jnp
from jax import lax
import numpy as np

D_MODEL = 1024
BATCH = 8
SEQ = 8192
DEPTH = 2
DEC_BATCH = 16
DEC_SEQ = 16
PAST_LEN = 2048

CHUNK = 64
EPS = 1e-6
CONV_W = 4
A_WIDTH = D_MODEL // 4
B_WIDTH = D_MODEL // 2
C_WIDTH = D_MODEL - A_WIDTH - B_WIDTH
A_HEADS = 4
A_HEAD_DIM = A_WIDTH // A_HEADS
LRU_C = 8.0
B_HEAD_DIM = 64
B_HEADS = B_WIDTH // B_HEAD_DIM
B_GROUPS = 2
B_HPG = B_HEADS // B_GROUPS
B_STATE = 128
B_CONV_DIM = B_WIDTH + 2 * B_GROUPS * B_STATE
C_HEADS = 4
C_KDIM = C_WIDTH // C_HEADS
C_VDIM = C_WIDTH // C_HEADS
D_FF = ((-(-8 * D_MODEL // 3)) + 255) // 256 * 256
SPLIT_SIZES = (A_WIDTH, A_WIDTH, B_WIDTH, B_CONV_DIM, B_HEADS, C_WIDTH, C_WIDTH, C_WIDTH, C_WIDTH)
SPLIT_IDX = tuple(int(v) for v in np.cumsum(SPLIT_SIZES)[:-1])
IN_COLS = sum(SPLIT_SIZES)

kernel_name = "hymba_rglru_ssd_hgrn2_stream_step"


def rmsnorm(x, w):
    xf = x.astype(jnp.float32)
    y = xf * lax.rsqrt(jnp.mean(xf * xf, axis=-1, keepdims=True) + EPS)
    return (y * w.astype(jnp.float32)).astype(x.dtype)


def causal_conv(x, prev, w, b):
    t = x.shape[1]
    xp = jnp.concatenate([prev.astype(x.dtype), x], axis=1)
    out = b
    for k in range(CONV_W):
        out = out + xp[:, k:k + t] * w[k]
    return out, xp[:, t:]


def chunk_size(t):
    cs = min(CHUNK, t)
    assert t % cs == 0
    return cs


def to_chunks(a, cs):
    b, t = a.shape[:2]
    return a.reshape(b, t // cs, cs, *a.shape[2:]).swapaxes(0, 1)


def from_chunks(a):
    nc, b, cs = a.shape[:3]
    return a.swapaxes(0, 1).reshape(b, nc * cs, *a.shape[3:])


def causal_mask(l):
    return jnp.tril(jnp.ones((l, l), dtype=bool))


def rglru(x, h0, wa, ba, wx, bx, lam):
    f32 = jnp.float32
    b_, t = x.shape[:2]
    xh = x.reshape(b_, t, A_HEADS, A_HEAD_DIM)
    r = jax.nn.sigmoid((jnp.einsum("bthi,hij->bthj", xh, wa).reshape(b_, t, A_WIDTH) + ba).astype(f32))
    i = jax.nn.sigmoid((jnp.einsum("bthi,hij->bthj", xh, wx).reshape(b_, t, A_WIDTH) + bx).astype(f32))
    log_a = -LRU_C * r * jax.nn.softplus(-lam.astype(f32))
    a = jnp.exp(log_a)
    u = jnp.sqrt(-jnp.expm1(2.0 * log_a)) * (i * x.astype(f32))
    u = u.at[:, 0].add(a[:, 0] * h0.astype(f32))

    def combine(left, right):
        return (left[0] * right[0], right[0] * left[1] + right[1])

    _, h = lax.associative_scan(combine, (a, u), axis=1)
    return h, h[:, -1]


def ssd_chunk(s, inp):
    x, da, bm, cm = inp
    l = x.shape[1]
    acs = jnp.cumsum(da, axis=1)
    diff = acs[:, :, None] - acs[:, None]
    decay = jnp.exp(jnp.where(causal_mask(l)[None, :, :, None, None], diff, -jnp.inf))
    cb = jnp.einsum("btgn,bsgn->btsg", cm, bm)
    y = jnp.einsum("btsg,btsgh,bsghp->btghp", cb, decay, x)
    y = y + jnp.einsum("btgn,bghpn->btghp", cm, s) * jnp.exp(acs)[..., None]
    tail = jnp.exp(acs[:, -1:] - acs)
    s_new = jnp.exp(acs[:, -1])[..., None, None] * s + jnp.einsum("bsgn,bsghp->bghpn", bm, x * tail[..., None])
    return s_new, y


def ssd_mixer(z, xbc, dt, conv_prev, s0, conv_w, conv_b, dt_bias, a_log, d_skip, norm_w):
    f32 = jnp.float32
    b_, t = z.shape[:2]
    xbc, conv_new = causal_conv(xbc, conv_prev, conv_w, conv_b)
    xbc = jax.nn.silu(xbc.astype(f32))
    xs, bm, cm = jnp.split(xbc, [B_WIDTH, B_WIDTH + B_GROUPS * B_STATE], axis=-1)
    xs = xs.reshape(b_, t, B_GROUPS, B_HPG, B_HEAD_DIM)
    bm = bm.reshape(b_, t, B_GROUPS, B_STATE)
    cm = cm.reshape(b_, t, B_GROUPS, B_STATE)
    dt = jax.nn.softplus(dt.astype(f32) + dt_bias.astype(f32)).reshape(b_, t, B_GROUPS, B_HPG)
    a = -jnp.exp(a_log.astype(f32)).reshape(B_GROUPS, B_HPG)
    cs = chunk_size(t)
    s0 = s0.astype(f32).reshape(b_, B_GROUPS, B_HPG, B_HEAD_DIM, B_STATE)
    inp = (to_chunks(xs * dt[..., None], cs), to_chunks(dt * a, cs), to_chunks(bm, cs), to_chunks(cm, cs))
    s_t, y = lax.scan(ssd_chunk, s0, inp)
    y = from_chunks(y) + xs * d_skip.astype(f32).reshape(B_GROUPS, B_HPG)[:, :, None]
    v = (y.reshape(b_, t, B_WIDTH) * jax.nn.silu(z.astype(f32))).reshape(b_, t, B_GROUPS, B_WIDTH // B_GROUPS)
    v = v * lax.rsqrt(jnp.mean(v * v, axis=-1, keepdims=True) + EPS)
    out = v.reshape(b_, t, B_WIDTH) * norm_w.astype(f32)
    return out, conv_new, s_t.reshape(b_, B_HEADS, B_HEAD_DIM, B_STATE)


def hgrn_chunk(s, inp):
    q, lf, k, v = inp
    l = q.shape[1]
    bc = jnp.cumsum(lf, axis=1)
    diff = bc[:, :, None] - bc[:, None]
    decay = jnp.exp(jnp.where(causal_mask(l)[None, :, :, None, None], diff, -jnp.inf))
    att = jnp.einsum("bthk,bshk,btshk->bths", q, k, decay)
    o = jnp.einsum("bths,bshv->bthv", att, v) + jnp.einsum("bthk,bhkv->bthv", q * jnp.exp(bc), s)
    bl = bc[:, -1]
    s_new = jnp.exp(bl)[..., None] * s + jnp.einsum("bshk,bshv->bhkv", k * jnp.exp(bl[:, None] - bc), v)
    return s_new, o


def hgrn_mixer(q, fpre, ig, og, s0, lb, norm_w):
    f32 = jnp.float32
    b_, t = q.shape[:2]
    shp = (b_, t, C_HEADS, C_KDIM)
    q = jax.nn.silu(q.astype(f32)).reshape(shp)
    fpre = fpre.astype(f32)
    log_f = jnp.logaddexp(jnp.log(lb), jnp.log1p(-lb) + jax.nn.log_sigmoid(fpre)).reshape(shp)
    k = ((1.0 - lb) * jax.nn.sigmoid(-fpre)).reshape(shp)
    v = ig.astype(f32).reshape(b_, t, C_HEADS, C_VDIM)
    cs = chunk_size(t)
    inp = (to_chunks(q, cs), to_chunks(log_f, cs), to_chunks(k, cs), to_chunks(v, cs))
    s_t, o = lax.scan(hgrn_chunk, s0.astype(f32), inp)
    o = from_chunks(o).reshape(b_, t, C_WIDTH)
    o = o * lax.rsqrt(jnp.mean(o * o, axis=-1, keepdims=True) + EPS) * norm_w.astype(f32)
    return o * jax.nn.silu(og.astype(f32)), s_t


def trunk_layer(x, conv_a0, h_a0, conv_b0, s_b0, s_c0, lb, p):
    f32 = jnp.float32
    h = rmsnorm(x, p["norm1_w"])
    proj = jnp.einsum("btd,de->bte", h, p["w_in"])
    xa, ga, z, xbc, dt, q, fpre, ig, og = jnp.split(proj, SPLIT_IDX, axis=-1)
    xa, conv_a = causal_conv(xa, conv_a0, p["rglru_conv_w"], p["rglru_conv_b"])
    ha, h_a = rglru(xa, h_a0, p["rglru_wa"], p["rglru_ba"], p["rglru_wx"], p["rglru_bx"], p["rglru_lambda"])
    ya = ha * jax.nn.gelu(ga.astype(f32))
    yb, conv_b, s_b = ssd_mixer(z, xbc, dt, conv_b0, s_b0, p["ssd_conv_w"], p["ssd_conv_b"],
                                p["ssd_dt_bias"], p["ssd_a_log"], p["ssd_d"], p["ssd_norm_w"])
    yc, s_c = hgrn_mixer(q, fpre, ig, og, s_c0, lb, p["hgrn_norm_w"])
    mix = jnp.concatenate([ya, yb, yc], axis=-1).astype(x.dtype)
    x = x + jnp.einsum("bte,ed->btd", mix, p["w_out"])
    h2 = rmsnorm(x, p["norm2_w"])
    ff = jax.nn.silu(jnp.einsum("btd,df->btf", h2, p["w_ffn_gate"])) * jnp.einsum("btd,df->btf", h2, p["w_ffn_up"])
    x = x + jnp.einsum("btf,fd->btd", ff, p["w_ffn_down"])
    return x, conv_a, h_a, conv_b, s_b, s_c


def setup_inputs(seed: int = 0) -> dict:
    key = jax.random.key(seed)
    ks = iter(jax.random.split(key, 40))
    f32 = jnp.float32

    def nrm(shape, scale):
        return scale * jax.random.normal(next(ks), shape, f32)

    def gain(shape):
        return 1.0 + 0.02 * jax.random.normal(next(ks), shape, f32)

    a0 = jax.random.uniform(next(ks), (DEPTH, A_WIDTH), f32, 0.9, 0.999)
    dt0 = jnp.exp(jax.random.uniform(next(ks), (DEPTH, B_HEADS), f32, math.log(1e-3), math.log(1e-1)))
    a_init = jax.random.uniform(next(ks), (DEPTH, B_HEADS), f32, 1.0, 16.0)
    return {
        "x_prompt": nrm((BATCH, SEQ, D_MODEL), 1.0),
        "x_sample": nrm((DEC_BATCH, DEC_SEQ, D_MODEL), 1.0),
        "state_rglru_conv": nrm((DEPTH, DEC_BATCH, CONV_W - 1, A_WIDTH), 1.0),
        "state_rglru_h": nrm((DEPTH, DEC_BATCH, A_WIDTH), 0.5),
        "state_ssd_conv": nrm((DEPTH, DEC_BATCH, CONV_W - 1, B_CONV_DIM), 1.0),
        "state_ssd": nrm((DEPTH, DEC_BATCH, B_HEADS, B_HEAD_DIM, B_STATE), 0.1),
        "state_hgrn": nrm((DEPTH, DEC_BATCH, C_HEADS, C_KDIM, C_VDIM), 0.5),
        "norm1_w": gain((DEPTH, D_MODEL)),
        "w_in": nrm((DEPTH, D_MODEL, IN_COLS), D_MODEL ** -0.5),
        "rglru_conv_w": nrm((DEPTH, CONV_W, A_WIDTH), CONV_W ** -0.5),
        "rglru_conv_b": nrm((DEPTH, A_WIDTH), 0.02),
        "rglru_wa": nrm((DEPTH, A_HEADS, A_HEAD_DIM, A_HEAD_DIM), A_HEAD_DIM ** -0.5),
        "rglru_ba": nrm((DEPTH, A_WIDTH), 0.02),
        "rglru_wx": nrm((DEPTH, A_HEADS, A_HEAD_DIM, A_HEAD_DIM), A_HEAD_DIM ** -0.5),
        "rglru_bx": nrm((DEPTH, A_WIDTH), 0.02),
        "rglru_lambda": jnp.log(a0) - jnp.log1p(-a0),
        "ssd_conv_w": nrm((DEPTH, CONV_W, B_CONV_DIM), CONV_W ** -0.5),
        "ssd_conv_b": nrm((DEPTH, B_CONV_DIM), 0.02),
        "ssd_dt_bias": dt0 + jnp.log(-jnp.expm1(-dt0)),
        "ssd_a_log": jnp.log(a_init),
        "ssd_d": gain((DEPTH, B_HEADS)),
        "ssd_norm_w": gain((DEPTH, B_WIDTH)),
        "hgrn_lb": nrm((DEPTH, C_WIDTH), 1.0),
        "hgrn_norm_w": gain((DEPTH, C_WIDTH)),
        "w_out": nrm((DEPTH, D_MODEL, D_MODEL), D_MODEL ** -0.5),
        "norm2_w": gain((DEPTH, D_MODEL)),
        "w_ffn_gate": nrm((DEPTH, D_MODEL, D_FF), D_MODEL ** -0.5),
        "w_ffn_up": nrm((DEPTH, D_MODEL, D_FF), D_MODEL ** -0.5),
        "w_ffn_down": nrm((DEPTH, D_FF, D_MODEL), D_FF ** -0.5),
        "final_norm_w": gain((D_MODEL,)),
    }


def reference(x_prompt, x_sample, state_rglru_conv, state_rglru_h, state_ssd_conv, state_ssd, state_hgrn,
              norm1_w, w_in, rglru_conv_w, rglru_conv_b, rglru_wa, rglru_ba, rglru_wx, rglru_bx, rglru_lambda,
              ssd_conv_w, ssd_conv_b, ssd_dt_bias, ssd_a_log, ssd_d, ssd_norm_w, hgrn_lb, hgrn_norm_w,
              w_out, norm2_w, w_ffn_gate, w_ffn_up, w_ffn_down, final_norm_w):
    f32 = jnp.float32
    lbs = jnp.cumsum(jax.nn.softmax(hgrn_lb.astype(f32), axis=0), axis=0)
    lbs = lbs - lbs[0]
    bp = x_prompt.shape[0]
    xp, xs = x_prompt, x_sample
    p_acc = ([], [], [], [], [])
    s_acc = ([], [], [], [], [])
    for l in range(DEPTH):
        p = {
            "norm1_w": norm1_w[l], "w_in": w_in[l],
            "rglru_conv_w": rglru_conv_w[l], "rglru_conv_b": rglru_conv_b[l],
            "rglru_wa": rglru_wa[l], "rglru_ba": rglru_ba[l], "rglru_wx": rglru_wx[l], "rglru_bx": rglru_bx[l],
            "rglru_lambda": rglru_lambda[l],
            "ssd_conv_w": ssd_conv_w[l], "ssd_conv_b": ssd_conv_b[l], "ssd_dt_bias": ssd_dt_bias[l],
            "ssd_a_log": ssd_a_log[l], "ssd_d": ssd_d[l], "ssd_norm_w": ssd_norm_w[l],
            "hgrn_norm_w": hgrn_norm_w[l], "w_out": w_out[l], "norm2_w": norm2_w[l],
            "w_ffn_gate": w_ffn_gate[l], "w_ffn_up": w_ffn_up[l], "w_ffn_down": w_ffn_down[l],
        }
        xp, *new_p = trunk_layer(
            xp,
            jnp.zeros((bp, CONV_W - 1, A_WIDTH), x_prompt.dtype),
            jnp.zeros((bp, A_WIDTH), f32),
            jnp.zeros((bp, CONV_W - 1, B_CONV_DIM), x_prompt.dtype),
            jnp.zeros((bp, B_HEADS, B_HEAD_DIM, B_STATE), f32),
            jnp.zeros((bp, C_HEADS, C_KDIM, C_VDIM), f32),
            lbs[l], p)
        xs, *new_s = trunk_layer(
            xs, state_rglru_conv[l], state_rglru_h[l], state_ssd_conv[l], state_ssd[l], state_hgrn[l],
            lbs[l], p)
        for acc, v in zip(p_acc, new_p):
            acc.append(v)
        for acc, v in zip(s_acc, new_s):
            acc.append(v)
    y_prompt = rmsnorm(xp, final_norm_w)
    y_sample = rmsnorm(xs, final_norm_w)
    p_rglru_conv, p_rglru_h, p_ssd_conv, p_ssd, p_hgrn = (jnp.stack(v) for v in p_acc)
    s_rglru_conv, s_rglru_h, s_ssd_conv, s_ssd, s_hgrn = (jnp.stack(v) for v in s_acc)
    return (y_prompt, y_sample, p_rglru_conv, p_rglru_h, p_ssd_conv, p_ssd, p_hgrn,
            s_rglru_conv, s_rglru_h, s_ssd_conv, s_ssd, s_hgrn)
```

```python
import contextlib
import numpy as np
import concourse.bass as bass
import concourse.mybir as mybir
from concourse.bass_utils import run_bass_kernel_spmd

F32 = mybir.dt.float32
BF16 = mybir.dt.bfloat16
AF = mybir.ActivationFunctionType
ALU = mybir.AluOpType

D = 1024
DEPTH = 2
DFF = 2816
EPS = 1e-6
WB = 2048
NSLOT = 4
NBLK_L = 59
BLK_SIZES = [1536, 1536, 2048] + [1536] * 8 + [2048, 2048] + [2048] * 4 + [2048] * 4 + [2048] * 22 + [1408] * 16
TILE = 512
NCORES = 8

C_XA, C_GA, C_Z, C_XS, C_B, C_C, C_DT, C_Q, C_F, C_I, C_G = 0, 256, 512, 1024, 1536, 1792, 2048, 2056, 2312, 2568, 2824

PF_N1, PF_N2, PF_CW, PF_CB, PF_BA, PF_BX, PF_LAM = 0, 8, 16, 56, 66, 68, 70
PF_SNW, PF_HNW = 72, 76
PF_L = 78
PF_FN = 2 * PF_L
PF_LB = PF_FN + 8
PF_TOT = PF_LB + 4
PB_DTB, PB_ALOG, PB_DSK, PB_SNW, PB_HNW = 0, 8, 16, 24, 536
PB_L = 792
PB_TOT = 2 * PB_L


class Buf:
    __slots__ = ("name", "lo", "hi", "space", "writers", "readers", "alias")

    def __init__(self, name, space, lo, hi):
        self.name, self.space, self.lo, self.hi = name, space, lo, hi
        self.writers = {}
        self.readers = {}
        self.alias = []


class V:
    __slots__ = ("ap", "bufs")

    def __init__(self, ap, bufs):
        self.ap = ap
        self.bufs = list(bufs)

    def __getitem__(self, k):
        return V(self.ap[k], self.bufs)

    def re(self, pattern_, **kw):
        return V(self.ap.rearrange(pattern_, **kw), self.bufs)

    def bc(self, shape):
        return V(self.ap.broadcast_to(list(shape)), self.bufs)

    def un(self, axis):
        return V(self.ap.unsqueeze(axis), self.bufs)

    def cast(self, dt):
        return V(self.ap.bitcast(dt), self.bufs)

    def only(self, i):
        return V(self.ap, [self.bufs[i]])


EPOCH = 1000
import os as _os
_STOP = [False]


def stage(n):
    if int(_os.environ.get("KSTOP", "-1")) == n:
        _STOP[0] = True
        print(f"[kernel] debug: stopping emission at stage {n}")


NDMASEM = 24


class Prog:
    ENGS = ("pe", "act", "dve", "pool", "sp")

    def __init__(self, nc, stack):
        self.nc = nc
        self.stack = stack
        self.ops = {e: [] for e in self.ENGS}
        self.cnt = {e: 0 for e in self.ENGS}
        self.esems = {e: [] for e in self.ENGS}
        self.seen = {e: {} for e in self.ENGS}
        self.dsems = [stack.enter_context(nc.semaphore(f"dma{i}")) for i in range(NDMASEM)]
        self.dval = [0] * NDMASEM
        self.dnext = 0
        self.sem_of = {}
        for i, s in enumerate(self.dsems):
            self.sem_of[("d", i)] = s
        self.nops = 0

    def _esem(self, eng):
        ep = self.cnt[eng] // EPOCH
        while len(self.esems[eng]) <= ep:
            s = self.stack.enter_context(self.nc.semaphore(f"{eng}{len(self.esems[eng])}"))
            self.sem_of[(eng, len(self.esems[eng]))] = s
            self.esems[eng].append(s)
        return (eng, ep), self.cnt[eng] % EPOCH + 1

    def add(self, eng, emits, reads=(), writes=(), dma=False):
        if _STOP[0]:
            return
        if not isinstance(emits, (list, tuple)):
            emits = [emits]
        preads = [b for b in reads if b.space == "psum"]
        if preads:
            reads = [b for b in reads if b.space != "psum"]
            writes = list(writes) + [b for b in preads if b not in writes]
        need = {}

        def want(k, v):
            if need.get(k, 0) < v:
                need[k] = v

        for b in reads:
            for bb in [b] + b.alias:
                for k, v in bb.writers.items():
                    want(k, v)
        for b in writes:
            for bb in [b] + b.alias:
                for k, v in bb.writers.items():
                    want(k, v)
                for k, v in bb.readers.items():
                    want(k, v)
        if dma:
            j = self.dnext
            self.dnext = (self.dnext + 1) % NDMASEM
            key = ("d", j)
            if self.dval[j] > 0:
                want(key, self.dval[j])
            self.dval[j] += 16
            val = self.dval[j]
            inc = (key, 16)
        else:
            key, val = self._esem(eng)
            self.cnt[eng] += 1
            inc = (key, 1)
        waits = []
        seen = self.seen[eng]
        for k, v in need.items():
            if k[0] == "pe" and eng == "pe":
                continue
            if seen.get(k, 0) >= v:
                continue
            seen[k] = v
            waits.append((k, v))
        self.ops[eng].append((waits, emits, inc))
        for b in writes:
            b.writers = {key: val}
            b.readers = {}
        for b in reads:
            if b.readers.get(key, 0) < val:
                b.readers[key] = val
        self.nops += len(emits)

    def replay(self, eng, e):
        for waits, emits, inc in self.ops[eng]:
            for k, v in waits:
                e.wait_ge(self.sem_of[k], v)
            last = None
            for f in emits:
                last = f(e)
            last.then_inc(self.sem_of[inc[0]], inc[1])

    def finish(self, eng, e):
        for j in range(NDMASEM):
            if self.dval[j] > 0:
                e.wait_ge(self.dsems[j], self.dval[j])


class KB:
    def __init__(self, nc, stack, seq, nsamp=2, lsamp=16, debug=False):
        self.nc = nc
        self.stack = stack
        self.pg = Prog(nc, stack)
        self.seq = seq
        self.nt = seq // TILE
        self.nsamp = nsamp
        self.lsamp = lsamp
        self.debug = debug
        self.sbufs = []
        self.off = 0
        self.maxoff = 0
        self.CAP = 212000
        self.arena = stack.enter_context(nc.sbuf_tensor("arena", [128, self.CAP // 4], F32))
        self.psum = stack.enter_context(nc.psum_tensor("psum", [128, 4096], F32))
        self.pbufs = [Buf(f"bank{i}", "psum", i, i + 1) for i in range(8)]
        self.pnext = 0
        self.ntiles_total = self.nt + 1
        self.wissued = 0
        self.wtotal = self.ntiles_total * DEPTH * NBLK_L
        self.wused = 0

    def alloc(self, name, cols, dt=F32, nbufs=1):
        esz = 4 if dt == F32 else 2
        nbytes = (cols * esz + 3) // 4 * 4
        lo = self.off
        self.off += nbytes
        self.maxoff = max(self.maxoff, self.off)
        assert self.off <= self.CAP, f"SBUF overflow at {name}: {self.off}"
        ap = self.arena[:, lo // 4:(lo + nbytes) // 4]
        if dt != F32:
            ap = ap.bitcast(dt)
        ap = ap[:, 0:cols]
        bufs = []
        per = nbytes // nbufs
        for i in range(nbufs):
            b = Buf(f"{name}{i}", "sb", lo + i * per, lo + (i + 1) * per if i < nbufs - 1 else lo + nbytes)
            for o in self.sbufs:
                if o.lo < b.hi and b.lo < o.hi:
                    o.alias.append(b)
                    b.alias.append(o)
            self.sbufs.append(b)
            bufs.append(b)
        return V(ap, bufs)

    def ps(self, n=1):
        if n == 2 and self.pnext % 2 == 1:
            self.pnext = (self.pnext + 1) % 8
        i = self.pnext
        self.pnext = (self.pnext + n) % 8
        return V(self.psum[:, i * 512:(i + n) * 512], self.pbufs[i:i + n])

    @staticmethod
    def _bufs(*vs):
        out = []
        for v in vs:
            if isinstance(v, V):
                for b in v.bufs:
                    if b not in out:
                        out.append(b)
        return out

    @staticmethod
    def _a(v):
        return v.ap if isinstance(v, V) else v

    def mm(self, out, pairs, extra_reads=(), fp32=False):
        n = len(pairs)
        emits = []
        for i, (l, r) in enumerate(pairs):
            emits.append(lambda e, l=l, r=r, i=i: e.matmul(out.ap, l.ap, r.ap, start=(i == 0), stop=(i == n - 1)))
        rd = self._bufs(*[p[0] for p in pairs], *[p[1] for p in pairs], *extra_reads)
        self.pg.add("pe", emits, reads=rd, writes=out.bufs)

    def mms(self, items):
        emits = []
        rd, wr = [], []
        for (o, l, r, st, sp) in items:
            emits.append(lambda e, o=o, l=l, r=r, st=st, sp=sp: e.matmul(o.ap, l.ap, r.ap, start=st, stop=sp))
            rd += self._bufs(l, r)
            wr += self._bufs(o)
        self.pg.add("pe", emits, reads=list(dict.fromkeys(rd)), writes=list(dict.fromkeys(wr)))

    def trs(self, items):
        emits = []
        rd, wr = [], []
        for (o, i, idn) in items:
            emits.append(lambda e, o=o, i=i, idn=idn: e.transpose(o.ap, i.ap, idn.ap))
            rd += self._bufs(i, idn)
            wr += self._bufs(o)
        self.pg.add("pe", emits, reads=list(dict.fromkeys(rd)), writes=list(dict.fromkeys(wr)))

    def act(self, out, in_, func, bias=None, scale=None, accum=None, eng="act"):
        kw = {}
        if bias is not None:
            kw["bias"] = self._a(bias)
        if scale is not None:
            kw["scale"] = self._a(scale)
        if accum is not None:
            kw["accum_out"] = accum.ap
        self.pg.add("act", lambda e: e.activation(out.ap, in_.ap, func, **kw),
                    reads=self._bufs(in_, bias, scale), writes=self._bufs(out, accum))

    def tt(self, eng, out, a, b, op):
        self.pg.add(eng, lambda e: e.tensor_tensor(out.ap, a.ap, b.ap, op),
                    reads=self._bufs(a, b), writes=out.bufs)

    def ts(self, eng, out, a, s1, s2, op0, op1=None):
        if op1 is None:
            f = lambda e: e.tensor_scalar(out.ap, a.ap, self._a(s1), None, op0)
        else:
            f = lambda e: e.tensor_scalar(out.ap, a.ap, self._a(s1), self._a(s2), op0, op1)
        self.pg.add(eng, f, reads=self._bufs(a, s1, s2), writes=out.bufs)

    def stt(self, out, a, s, b, op0, op1):
        self.pg.add("dve", lambda e: e.scalar_tensor_tensor(out.ap, a.ap, self._a(s), b.ap, op0, op1),
                    reads=self._bufs(a, s, b), writes=out.bufs)

    def copy(self, eng, out, in_):
        if eng == "act":
            self.pg.add("act", lambda e: e.copy(out.ap, in_.ap), reads=in_.bufs, writes=out.bufs)
        else:
            self.pg.add(eng, lambda e: e.tensor_copy(out.ap, in_.ap), reads=in_.bufs, writes=out.bufs)

    def scan(self, out, d0, d1, init):
        self.pg.add("dve", lambda e: e.tensor_tensor_scan(out.ap, d0.ap, d1.ap, self._a(init), ALU.mult, ALU.add),
                    reads=self._bufs(d0, d1, init), writes=out.bufs)

    def recip(self, out, in_):
        self.pg.add("dve", lambda e: e.reciprocal(out.ap, in_.ap), reads=in_.bufs, writes=out.bufs)

    def memset(self, eng, out, val):
        self.pg.add(eng, lambda e: e.memset(out.ap, val), reads=(), writes=out.bufs)

    def dma(self, q, out, in_, **kw):
        self.pg.add(q, lambda e: e.dma_start(out=out.ap, in_=in_.ap, **kw),
                    reads=self._bufs(in_), writes=self._bufs(out), dma=True)


class TileCfg:
    def __init__(self, T, P, NSEG, L, CH):
        self.T, self.P, self.NSEG, self.L, self.CH = T, P, NSEG, L, CH
        self.NB = T // P
        self.NCH = T // CH
        self.NCB = P // CH
        self.BPS = L // P


def build_program(nc, stack, seq, nsamp=2, lsamp=16):
    kb = KB(nc, stack, seq, nsamp, lsamp)
    pg = kb.pg
    NT = seq // TILE
    TS = nsamp * lsamp
    NSEQ = 1 + nsamp
    def din(name, shape, dt=F32):
        return V(nc.dram_tensor(name, shape, dt, kind="ExternalInput").ap(), [])

    def dout(name, shape):
        return V(nc.dram_tensor(name, shape, F32, kind="ExternalOutput").ap(), [Buf(name, "dram", 0, 1)])

    xT_p = din("xT_p", [D, seq])
    xT_s = din("xT_s", [D, TS])
    wblk = din("wblk", [DEPTH * NBLK_L, 128, WB])
    pfm_d = din("pfm", [128, PF_TOT])
    pbc_d = din("pbc", [1, PB_TOT])
    wab_d = din("wab", [128, DEPTH * 4 * 128])
    wdt_d = din("wdt", [128, DEPTH * 64])
    sconv_d = din("s_conv", [128, DEPTH * nsamp * 30])
    shA_d = din("s_hA", [128, DEPTH * nsamp * 2])
    sssd_d = din("s_ssd", [128, DEPTH * nsamp * 512])
    shg_d = din("s_hgrn", [128, DEPTH * nsamp * 128])
    yT_p = dout("yT_p", [D, seq])
    yT_s = dout("yT_s", [D, TS])
    o_conv = dout("o_conv", [128, NSEQ * DEPTH * 30])
    o_hA = dout("o_hA", [128, NSEQ * DEPTH * 2])
    o_ssd = dout("o_ssd", [128, NSEQ * DEPTH * 512])
    o_hg = dout("o_hgrn", [128, NSEQ * DEPTH * 128])
    wscr_t = nc.dram_tensor("wscr", [DEPTH * NBLK_L, 128, WB], BF16, kind="Internal").ap()
    wscr = [V(wscr_t[i], [Buf(f"wscr{i}", "dram", 0, 1)]) for i in range(DEPTH * NBLK_L)]

    A = kb.alloc
    x_fm = A("x_fm", 8 * TILE, nbufs=8)
    h_bf = A("h_bf", 8 * TILE, BF16, nbufs=8)
    wslots = [A(f"wslot{i}", WB, BF16) for i in range(NSLOT)]
    U = A("U", 128)
    SL = A("SL", 128)
    MBD = A("MBD", 128)
    IDN = A("IDN", 128)
    ONF = A("ONF", 128)
    IDB = A("IDB", 128, BF16)
    ONB = A("ONB", 128, BF16)
    RMK = A("RMK", TILE)
    pfm = A("pfm", PF_TOT)
    pbc = A("pbc", PB_TOT)
    drv = A("drv", 64)
    aneg = A("aneg", 2 * 8)
    rowm = A("rowm", 2)
    wab = A("wab", DEPTH * 4 * 128, BF16)
    wdt = A("wdt", DEPTH * 64, BF16)
    NSLOTS = max(1, nsamp)
    convst = [[A(f"convst{l}_{s}", 30) for s in range(NSLOTS)] for l in range(DEPTH)]
    hAst = [[A(f"hA{l}_{s}", 2) for s in range(NSLOTS)] for l in range(DEPTH)]
    STf = [[A(f"ST{l}_{s}", 512) for s in range(NSLOTS)] for l in range(DEPTH)]
    STb = [[A(f"STb{l}_{s}", 512, BF16) for s in range(NSLOTS)] for l in range(DEPTH)]
    Scur = [[A(f"Sc{l}_{s}", 128) for s in range(NSLOTS)] for l in range(DEPTH)]
    rstd = A("rstd", TILE)
    sqt = [A(f"sqt{i}", TILE, BF16) for i in range(2)]

    DV_HBA, DV_HBX, DV_CA, DV_HCA, DV_LBH, DV_HOML = 0, 2, 4, 6, 8, 10
    DVL = 16

    ov0 = kb.off
    xpre = [A(f"xpre{i}", 4 + TILE, BF16) for i in range(3)]
    xa_c = A("xa_c", 2 * TILE, nbufs=2)
    xa_b = A("xa_b", 2 * TILE, BF16, nbufs=2)
    thr_b = A("thr_b", 2 * TILE, nbufs=2)
    thi_b = A("thi_b", 2 * TILE, nbufs=2)
    xs_f = A("xs_f", 4 * TILE, nbufs=4)
    B_b = A("B_b", 2 * TILE, BF16, nbufs=2)
    C_b = A("C_b", 2 * TILE, BF16, nbufs=2)
    xg = A("xg", 2 * TILE, nbufs=2)
    q_s = A("q_s", 2 * TILE, nbufs=2)
    f_f = A("f_f", 2 * TILE, nbufs=2)
    kk = A("kk", 2 * TILE, nbufs=2)
    lf = A("lf", 2 * TILE, nbufs=2)
    bcm = A("bcm", 2 * TILE, nbufs=2)
    tmpA = [A(f"tmpA{i}", 2 * TILE, nbufs=2) for i in range(4)]
    qk6 = [A(f"qk{i}", 2 * TILE, BF16, nbufs=2) for i in range(6)]
    ebl = A("ebl", 2 * 8)
    zs = A("zs", 4 * 512)
    v_tm = A("v_tm", 4 * 256, BF16)
    v_msk = A("v_msk", 2 * 256, BF16)
    gs = A("gs", 4 * 256)
    dtp = A("dtp", 32)
    dtv = A("dtv", 32)
    da = A("da", 8)
    dtt = A("dtt", 8)
    eat = A("eat", 16)
    elast = A("elast", 8)
    xs_tm = A("xs_tm", 512)
    B_tm = A("B_tm", 256, BF16)
    kh_tm = A("kh_tm", 256, BF16)
    Lm = A("Lm", 1024)
    Em = A("Em", 1024)
    cbm = A("cbm", 256)
    Mm = A("Mm", 1024, BF16)
    xdt = A("xdt", 512, BF16)
    xdtt = A("xdtt", 512, BF16)
    y1 = A("y1", 512)
    y2s = [A("y2a", 512), A("y2b", 512)]
    y3 = A("y3", 512)
    junk = A("junk", 768, BF16)
    ssS = A("ssS", 2)
    rsS = A("rsS", 2)
    ssH = A("ssH", 2)
    rsH = A("rsH", 2)
    att_m = A("att_m", 512, BF16)
    Sbf = A("Sbf", 8 * 128, BF16)
    oc = A("oc", 256)
    ov1 = kb.off
    kb.off = ov0
    act_b = A("act_b", 22 * TILE, BF16, nbufs=22)
    sil = [A(f"sil{i}", TILE) for i in range(2)]
    y_out = A("y_out", 8 * TILE, nbufs=8)
    pre32 = [A(f"pre32_{i}", WB) for i in range(4)]
    pre16 = [A(f"pre16_{i}", WB, BF16) for i in range(4)]
    kb.off = max(kb.off, ov1)
    print(f"[kernel] SBUF bytes/partition used: {kb.maxoff} (overlay1 {ov1 - ov0})")

    kb.dma("sp", pfm, pfm_d)
    kb.dma("sp", pbc, V(pbc_d.ap[0:1, :].partition_broadcast(128) if False else pbc_d.ap.broadcast_to([128, PB_TOT]), []))
    wab32 = xa_c[:, 0:DEPTH * 4 * 128]
    kb.dma("sp", wab32, wab_d)
    wdt32 = xg[:, 0:DEPTH * 64]
    kb.dma("sp", wdt32, wdt_d)
    kb.copy("dve", wab, wab32)
    kb.copy("dve", wdt, wdt32)
    kb.memset("pool", ONF, 1.0)
    kb.memset("pool", ONB, 1.0)
    kb.memset("pool", RMK, 1.0)
    kb.memset("pool", RMK.re("p (j c) -> p j c", c=64)[:, :, 0:1], 0.0)
    pg.add("pool", lambda e: e.affine_select(U.ap, ONF.ap, [[1, 128]], ALU.is_ge, 0.0, base=0, channel_multiplier=-1),
           reads=ONF.bufs, writes=U.bufs)
    pg.add("pool", lambda e: e.affine_select(SL.ap, ONF.ap, [[-1, 128]], ALU.is_ge, 0.0, base=-1, channel_multiplier=1),
           reads=ONF.bufs, writes=SL.bufs)
    pg.add("pool", lambda e: e.affine_select(IDN.ap, ONF.ap, [[1, 128]], ALU.is_equal, 0.0, base=0, channel_multiplier=-1),
           reads=ONF.bufs, writes=IDN.bufs)
    kb.copy("pool", IDB, IDN)
    kb.copy("pool", MBD, U)
    kb.memset("pool", MBD[0:64, 64:128], 0.0)
    kb.memset("pool", rowm, 1.0)
    kb.memset("pool", rowm[64:128, 0:1], 0.0)
    kb.memset("pool", rowm[0:64, 1:2], 0.0)
    for l in range(DEPTH):
        b = DVL * l
        pl = PF_L * l
        kb.ts("dve", drv[:, b + DV_HBA:b + DV_HBA + 2], pfm[:, pl + PF_BA:pl + PF_BA + 2], 0.5, None, ALU.mult)
        kb.ts("dve", drv[:, b + DV_HBX:b + DV_HBX + 2], pfm[:, pl + PF_BX:pl + PF_BX + 2], 0.5, None, ALU.mult)
        t = drv[:, 60:62]
        kb.act(t, pfm[:, pl + PF_LAM:pl + PF_LAM + 2], AF.Exp, scale=-1.0)
        kb.act(t, t, AF.Ln, bias=1.0)
        kb.ts("dve", drv[:, b + DV_CA:b + DV_CA + 2], t, -8.0, None, ALU.mult)
        kb.ts("dve", drv[:, b + DV_HCA:b + DV_HCA + 2], t, -4.0, None, ALU.mult)
        lbv = drv[:, 62:64]
        if l == 0:
            kb.memset("dve", lbv, 0.0)
        else:
            kb.tt("dve", lbv, pfm[:, PF_LB:PF_LB + 2], pfm[:, PF_LB + 2:PF_LB + 4], ALU.subtract)
            kb.act(lbv, lbv, AF.Exp)
            kb.ts("dve", lbv, lbv, 1.0, None, ALU.add)
            kb.recip(lbv, lbv)
        kb.ts("dve", drv[:, b + DV_HOML:b + DV_HOML + 2], lbv, -0.5, 0.5, ALU.mult, ALU.add)
        kb.tt("dve", drv[:, b + DV_LBH:b + DV_LBH + 2], lbv, drv[:, b + DV_HOML:b + DV_HOML + 2], ALU.add)
        kb.act(aneg[:, 8 * l:8 * l + 8], pbc[:, PB_L * l + PB_ALOG:PB_L * l + PB_ALOG + 8], AF.Exp)
        kb.ts("dve", aneg[:, 8 * l:8 * l + 8], aneg[:, 8 * l:8 * l + 8], -1.0, None, ALU.mult)

    stage(0)
    NB_ALL = DEPTH * NBLK_L
    cast_engs = ["act", "dve"]
    NPRE = len(pre32)
    stage(1)
    def blk_size(j):
        return BLK_SIZES[j % NBLK_L]

    def wissue():
        g = kb.wissued
        kb.wissued += 1
        j = g % NB_ALL
        n = blk_size(j)
        slot = wslots[g % NSLOT]
        if g < NB_ALL:
            s32 = pre32[g % NPRE]
            kb.dma("sp", s32[:, 0:n], V(wblk.ap[j][:, 0:n], []))
            kb.copy(cast_engs[g % 2], slot[:, 0:n], s32[:, 0:n])
            kb.dma("pool", V(wscr[j].ap[:, 0:n], wscr[j].bufs), slot[:, 0:n])
        else:
            kb.dma("sp", slot[:, 0:n], V(wscr[j].ap[:, 0:n], wscr[j].bufs))

    def wget():
        g = kb.wused
        kb.wused += 1
        while kb.wissued < min(g + NSLOT - 1, kb.wtotal):
            wissue()
        return wslots[g % NSLOT]

    def rmsnorm(T, wcol0, out_is_h=True, out_v=None):
        p = kb.ps()
        for k in range(8):
            s = sqt[k % 2]
            kb.act(s[:, 0:T], xk(k, T), AF.Square)
            pg.add("pe", lambda e, k=k, s=s: e.matmul(p.ap[:, 0:T], ONB.ap, s.ap[:, 0:T], start=(k == 0), stop=(k == 7)),
                   reads=ONB.bufs + s.bufs, writes=p.bufs)
        kb.act(rstd[:, 0:T], p[:, 0:T], AF.Ln, bias=EPS, scale=1.0 / D)
        kb.act(rstd[:, 0:T], rstd[:, 0:T], AF.Exp, scale=-0.5)
        for k in range(8):
            if out_is_h:
                o = V(h_bf.ap[:, k * TILE:k * TILE + T], [h_bf.bufs[k]])
            else:
                o = V(out_v.ap[:, k * TILE:k * TILE + T], [out_v.bufs[k]])
            kb.stt(o, xk(k, T), pfm[:, wcol0 + k:wcol0 + k + 1], rstd[:, 0:T], ALU.mult, ALU.mult)

    def xk(k, T):
        return V(x_fm.ap[:, k * TILE:k * TILE + T], [x_fm.bufs[k]])

    def hk(k, c0, n):
        return V(h_bf.ap[:, k * TILE + c0:k * TILE + c0 + n], [h_bf.bufs[k]])

    def mixw(k, c0, n):
        return hk(k, c0, n)

    evq = [0]

    def ev_eng():
        evq[0] += 1
        return "act" if evq[0] % 2 == 0 else "dve"

    class BankPool:
        def __init__(self, banks):
            self.banks = banks
            self.i = 0

        def get(self, n=1):
            b = self.banks[self.i % len(self.banks)]
            self.i += 1
            return V(kb.psum[:, b * 512:(b + n) * 512], kb.pbufs[b:b + n])

    def interleave(*gens):
        active = [g for g in gens if g is not None]
        while active:
            for g in list(active):
                if g not in active:
                    continue
                try:
                    tok = next(g)
                except StopIteration:
                    active.remove(g)
                    continue
                if tok == "drain":
                    for o in list(active):
                        if o is not g:
                            for _ in o:
                                pass
                            active.remove(o)
                yield

    def pipelined(n, front, back):
        prev = None
        for b in range(n):
            f = front(b)
            if prev is None:
                yield from f
            else:
                yield from interleave(f, prev)
            prev = back(b)
        yield from prev

    def chunkv(v, c, T):
        return V(v.ap[:, c * TILE:c * TILE + T], [v.bufs[c]])

    def layer(l, cfg, slots):
        T, P, NB, NSEG, L, CH = cfg.T, cfg.P, cfg.NB, cfg.NSEG, cfg.L, cfg.CH
        pl = PF_L * l
        pb = PB_L * l
        dv = DVL * l
        NCH = cfg.NCH

        def seg3(v):
            return v.re("p (s l) -> p s l", s=NSEG)

        rmsnorm(T, pl + PF_N1)

        stage(2)
        conv_i = [0]

        def a_pre():
            for c in range(2):
                pr = kb.ps()
                pi = kb.ps()
                kb.mm(pr[:, 0:T], [(wab[:, (l * 4 + 0 + c) * 128:(l * 4 + 0 + c + 1) * 128], chunkv(xa_b, c, T))])
                kb.mm(pi[:, 0:T], [(wab[:, (l * 4 + 2 + c) * 128:(l * 4 + 2 + c + 1) * 128], chunkv(xa_b, c, T))])
                kb.act(chunkv(thr_b, c, T), pr[:, 0:T], AF.Tanh, bias=drv[:, dv + DV_HBA + c:dv + DV_HBA + c + 1], scale=0.5)
                kb.act(chunkv(thi_b, c, T), pi[:, 0:T], AF.Tanh, bias=drv[:, dv + DV_HBX + c:dv + DV_HBX + c + 1], scale=0.5)
            G = [chunkv(lf, c, T) for c in range(2)]
            X = [chunkv(xg, c, T) for c in range(2)]
            for c in range(2):
                kb.act(G[c], X[c], AF.Square)
            for c in range(2):
                kb.ts("dve", G[c], G[c], 0.044715, 1.0, ALU.mult, ALU.add)
            for c in range(2):
                kb.tt("dve", G[c], G[c], X[c], ALU.mult)
            for c in range(2):
                kb.act(G[c], G[c], AF.Tanh, scale=0.7978845608028654)
            for c in range(2):
                kb.stt(X[c], G[c], 1.0, X[c], ALU.add, ALU.mult)

        def proj_chunk(wv, off):
            p = kb.ps()
            pv = p[:, 0:T]
            kb.mm(pv, [(wv[:, off + kc * 128:off + (kc + 1) * 128], hk(kc, 0, T)) for kc in range(8)])
            return pv

        cbias = lambda ci: pfm[:, pl + PF_CB + ci:pl + PF_CB + ci + 1]

        def conv_evac(ci, pv):
            xp = xpre[conv_i[0] % 3]
            conv_i[0] += 1
            W = 4 + L
            xpv = xp[:, 0:NSEG * W].re("p (s w) -> p s w", s=NSEG)
            for s in range(NSEG):
                kb.copy("pool", xp[:, s * W:s * W + 3], convst[l][slots[s]][:, ci * 3:ci * 3 + 3])
            kb.copy("act", xpv[:, :, 3:3 + L], seg3(pv))
            for s in range(NSEG):
                kb.copy("dve", convst[l][slots[s]][:, ci * 3:ci * 3 + 3], pv[:, s * L + L - 3:s * L + L])
            return xp

        def conv_mm(wv, xp):
            W = 4 + L
            p2 = kb.ps()
            for s in range(NSEG):
                kb.mm(p2[:, s * L:(s + 1) * L],
                      [(wv[:, 1024 + k * 128:1024 + (k + 1) * 128], xp[:, s * W + k:s * W + k + L]) for k in range(4)])
            return p2[:, 0:T]

        units = []

        def mk_conv(ci, fin):
            st = {}

            def A_():
                st["wv"] = wget()
                st["xp"] = conv_evac(ci, proj_chunk(st["wv"], 0))

            def B_():
                fin(conv_mm(st["wv"], st["xp"]))
            return (A_, B_)

        def fin_xa(c):
            def f_(pc):
                kb.act(chunkv(xa_c, c, T), pc, AF.Identity, bias=cbias(c))
                kb.act(chunkv(xa_b, c, T), pc, AF.Identity, bias=cbias(c))
            return f_

        def fin_xbc(j):
            def f_(pc):
                if j < 4:
                    dst = V(xs_f.ap[:, j * TILE:j * TILE + T], [xs_f.bufs[j]])
                elif j < 6:
                    dst = chunkv(B_b, j - 4, T)
                else:
                    dst = chunkv(C_b, j - 6, T)
                kb.act(dst, pc, AF.Silu, bias=cbias(2 + j))
            return f_

        def mk_pair(kind):
            def A_():
                wv = wget()
                for j in range(2):
                    pv = proj_chunk(wv, j * 1024)
                    if kind == "ga":
                        kb.copy("act", chunkv(xg, j, T), pv)
                    elif kind == "q":
                        kb.act(chunkv(q_s, j, T), pv, AF.Silu)
                    else:
                        kb.act(chunkv(f_f, j, T), pv, AF.Tanh, scale=0.5)

            def B_():
                if kind == "ga":
                    a_pre_flag[0] = True
            return (A_, B_)

        a_pre_flag = [False]
        units.append(mk_conv(0, fin_xa(0)))
        units.append(mk_conv(1, fin_xa(1)))
        units.append(mk_pair("ga"))
        for j in range(8):
            units.append(mk_conv(2 + j, fin_xbc(j)))
        units.append(mk_pair("q"))
        units.append(mk_pair("f"))
        did_pre = False
        for i in range(len(units) + 1):
            if i < len(units):
                units[i][0]()
            if i > 0:
                units[i - 1][1]()
            if a_pre_flag[0] and not did_pre and i >= 4:
                a_pre()
                did_pre = True

        stage(4)
        pSF = BankPool([2, 3])
        pSD = BankPool([0])
        pTR = BankPool([4])
        pHF = BankPool([5, 6])
        pHO = BankPool([7])

        def fl(v):
            return V(v.ap.rearrange("p (c t) -> p c t", c=2)[:, :, 0:T], v.bufs)

        def ch4(v):
            return V(v.ap.rearrange("p (c t) -> p c t", c=2)[:, :, 0:T].rearrange("p c (j i) -> p c j i", i=CH), v.bufs)

        qt_e, qt_o, qh_e, qh_o, kt, kh = qk6
        d1, r1, d2, e3 = tmpA[0], tmpA[1], tmpA[2], tmpA[3]

        def c_prep():
            for c in range(2):
                kb.ts("dve", chunkv(f_f, c, T), chunkv(f_f, c, T), drv[:, dv + DV_HOML + c:dv + DV_HOML + c + 1],
                      drv[:, dv + DV_LBH + c:dv + DV_LBH + c + 1], ALU.mult, ALU.add)
                kb.ts("pool", chunkv(kk, c, T), chunkv(f_f, c, T), -1.0, 1.0, ALU.mult, ALU.add)
            for c in range(2):
                kb.act(chunkv(lf, c, T), chunkv(f_f, c, T), AF.Ln)
                for s in range(NSEG):
                    ss_ = slice(c * TILE + s * L, c * TILE + s * L + L)
                    kb.scan(V(bcm.ap[:, ss_], [bcm.bufs[c]]), RMK[:, 0:L], V(lf.ap[:, ss_], [lf.bufs[c]]), 0.0)
                yield
            b4 = ch4(bcm)
            kb.tt("dve", ch4(d1), b4, b4[:, :, :, CH // 2 - 1:CH // 2].bc([128, 2, NCH, CH]), ALU.subtract)
            kb.tt("pool", ch4(d2), b4[:, :, :, CH - 1:CH].bc([128, 2, NCH, CH]), b4, ALU.subtract)
            yield
            kb.act(fl(r1), fl(d1), AF.Exp, scale=-1.0)
            kb.act(fl(d1), fl(d1), AF.Exp)
            yield
            kb.act(fl(e3), fl(bcm), AF.Exp)
            kb.act(V(ebl.ap.rearrange("p (c j) -> p c j", c=2)[:, :, 0:NCH], ebl.bufs),
                   V(b4.ap[:, :, :, CH - 1], b4.bufs), AF.Exp)
            kb.tt("dve", fl(qt_o), fl(q_s), fl(d1), ALU.mult)
            yield
            kb.act(fl(d2), fl(d2), AF.Exp)
            kb.tt("dve", fl(qh_o), fl(q_s), fl(e3), ALU.mult)
            kb.tt("pool", fl(kt), fl(kk), fl(r1), ALU.mult)
            yield
            kb.act(fl(qt_e), fl(qt_o), AF.Copy, scale=rowm[:, 0:1])
            kb.act(fl(qt_o), fl(qt_o), AF.Copy, scale=rowm[:, 1:2])
            kb.tt("dve", fl(kh), fl(kk), fl(d2), ALU.mult)
            yield
            kb.act(fl(qh_e), fl(qh_o), AF.Copy, scale=rowm[:, 0:1])
            kb.act(fl(qh_o), fl(qh_o), AF.Copy, scale=rowm[:, 1:2])
            yield

        def a_post(c):
            cs = slice(c * TILE, c * TILE + T)
            thr, thi = chunkv(thr_b, c, T), chunkv(thi_b, c, T)
            av, sv = chunkv(tmpA[2], c, T), chunkv(tmpA[3], c, T)
            cA = drv[:, dv + DV_CA + c:dv + DV_CA + c + 1]
            hcA = drv[:, dv + DV_HCA + c:dv + DV_HCA + c + 1]
            kb.act(av, thr, AF.Exp, bias=hcA, scale=hcA)
            kb.act(sv, thr, AF.Exp, bias=cA, scale=cA)
            kb.stt(thi, thi, 1.0, chunkv(xa_c, c, T), ALU.add, ALU.mult)
            yield
            kb.act(sv, sv, AF.Ln, bias=1.0, scale=-1.0)
            kb.act(sv, sv, AF.Exp, scale=0.5)
            yield
            kb.stt(thi, thi, 0.5, sv, ALU.mult, ALU.mult)
            yield
            for s in range(NSEG):
                ss_ = slice(c * TILE + s * L, c * TILE + s * L + L)
                hst = hAst[l][slots[s]][:, c:c + 1]
                kb.scan(V(thr_b.ap[:, ss_], [thr_b.bufs[c]]), V(tmpA[2].ap[:, ss_], [tmpA[2].bufs[c]]),
                        V(thi_b.ap[:, ss_], [thi_b.bufs[c]]), hst)
                kb.copy("act", hst, V(thr_b.ap[:, c * TILE + s * L + L - 1:c * TILE + s * L + L], [thr_b.bufs[c]]))
            yield
            kb.stt(mixw(c, 0, T), thr, 0.5, chunkv(xg, c, T), ALU.mult, ALU.mult)
            yield

        def ssd_front(b):
            s = b // cfg.BPS
            slot = slots[s]
            c0 = b * P
            y2 = y2s[b % 2]
            if b == 0:
                kb.act(dtv[0:P, 0:NB * 8], dtp[0:P, 0:NB * 8], AF.Exp)
                kb.act(dtv[0:P, 0:NB * 8], dtv[0:P, 0:NB * 8], AF.Ln, bias=1.0)
            dtb_ = dtv[0:P, b * 8:b * 8 + 8]
            kb.tt("dve", da[0:P, :], dtb_, aneg[0:P, 8 * l:8 * l + 8], ALU.mult)
            L3 = Lm[0:P, 0:8 * P].re("p (h s) -> p h s", h=8)
            kb.tt("pool", L3, SL[0:P, 0:P].un(1).bc([P, 8, P]), da[0:P, :].un(2).bc([P, 8, P]), ALU.mult)
            pT = pSF.get()
            kb.trs([(pT[0:P, j * 128:(j + 1) * 128], V(xs_f.ap[:, j * TILE + c0:j * TILE + c0 + P], [xs_f.bufs[j]]), IDN)
                    for j in range(4)])
            kb.copy("act", xs_tm[0:P, :], pT[0:P, 0:512])
            yield
            pD = pSD.get(2)
            kb.mms([(pD[0:P, h * P:(h + 1) * P], Lm[0:P, h * P:(h + 1) * P], U[0:P, 0:P], True, True) for h in range(8)])
            pS = pSF.get()
            kb.mms([(pS[0:P, 0:8], U[0:P, 0:P], da[0:P, :], True, True),
                    (pS[0:P, 8:16], SL[0:P, 0:P], da[0:P, :], True, True),
                    (pS[:, 16:24], ONF[0:P, :], da[0:P, :], True, True)])
            kb.act(eat[0:P, :], pS[0:P, 0:16], AF.Exp)
            kb.act(elast, pS[:, 16:24], AF.Exp)
            yield
            pT2 = pSF.get()
            pT2b = pT2.cast(BF16)
            kb.trs([(pT2b[0:P, g * 128:(g + 1) * 128], chunkv(B_b, g, T)[:, c0:c0 + P], IDB) for g in range(2)])
            kb.copy("act", B_tm[0:P, :], pT2b[0:P, 0:256])
            kb.act(Em[0:P, 0:8 * P], pD[0:P, 0:8 * P], AF.Exp)
            kb.tt("dve", dtt[0:P, :], dtb_, eat[0:P, 8:16], ALU.mult)
            yield
            pC = pSF.get()
            kb.mms([(pC[0:P, g * P:(g + 1) * P], chunkv(B_b, g, T)[:, c0:c0 + P], chunkv(C_b, g, T)[:, c0:c0 + P], True, True)
                    for g in range(2)])
            kb.tt("dve", cbm[0:P, 0:2 * P].re("p (g t) -> p g t", g=2), pC[0:P, 0:2 * P].re("p (g t) -> p g t", g=2),
                  U[0:P, 0:P].un(1).bc([P, 2, P]), ALU.mult)
            x3 = xs_tm[0:P, :].re("p (h d) -> p h d", h=8)
            kb.tt("dve", xdt[0:P, :].re("p (h d) -> p h d", h=8), x3, dtb_.un(2).bc([P, 8, 64]), ALU.mult)
            kb.tt("pool", xdtt[0:P, :].re("p (h d) -> p h d", h=8), x3, dtt[0:P, :].un(2).bc([P, 8, 64]), ALU.mult)
            yield
            kb.tt("dve", Mm[0:P, 0:8 * P].re("p (g h t) -> p g h t", g=2, h=4), Em[0:P, 0:8 * P].re("p (g h t) -> p g h t", g=2, h=4),
                  cbm[0:P, 0:2 * P].re("p (g t) -> p g t", g=2).un(2).bc([P, 2, 4, P]), ALU.mult)
            kb.tt("pool", y2[0:P, :].re("p (h d) -> p h d", h=8), x3, pbc[0:P, pb + PB_DSK:pb + PB_DSK + 8].un(2).bc([P, 8, 64]), ALU.mult)
            yield
            stage(7)
            pI = pSF.get()
            kb.mms([(pI[0:P, g * 256:(g + 1) * 256], chunkv(C_b, g, T)[:, c0:c0 + P], STb[l][slot][:, g * 256:(g + 1) * 256], True, True)
                    for g in range(2)])
            kb.tt("dve", y1[0:P, :].re("p (h d) -> p h d", h=8), pI[0:P, 0:512].re("p (h d) -> p h d", h=8),
                  eat[0:P, 0:8].un(2).bc([P, 8, 64]), ALU.mult)
            pY = pSF.get()
            kb.mms([(pY[0:P, h * 64:(h + 1) * 64], Mm[0:P, h * P:(h + 1) * P], xdt[0:P, h * 64:(h + 1) * 64], True, True) for h in range(8)])
            yield
            kb.tt("dve", y1[0:P, :], y1[0:P, :], pY[0:P, 0:512], ALU.add)
            pN = pSF.get()
            kb.mms([(pN[:, g * 256:(g + 1) * 256], B_tm[0:P, g * 128:(g + 1) * 128], xdtt[0:P, g * 256:(g + 1) * 256], True, True)
                    for g in range(2)])
            st = STf[l][slot]
            kb.tt("dve", st.re("p (h d) -> p h d", h=8), st.re("p (h d) -> p h d", h=8), elast.un(2).bc([128, 8, 64]), ALU.mult)
            yield
            kb.tt("dve", st, st, pN[:, 0:512], ALU.add)
            kb.copy("act", STb[l][slot], st)
            kb.tt("pool", y2[0:P, :], y2[0:P, :], y1[0:P, :], ALU.add)
            kb.tt("pool", y2[0:P, :], y2[0:P, :], zs[0:P, b * 512:(b + 1) * 512], ALU.mult)
            yield

        def ssd_back(b):
            c0 = b * P
            y2 = y2s[b % 2]
            for g in range(2):
                kb.act(junk[0:P, g * 256:(g + 1) * 256], y2[0:P, g * 256:(g + 1) * 256], AF.Square, accum=ssS[0:P, g:g + 1])
            kb.act(rsS[0:P, 0:2], ssS[0:P, 0:2], AF.Ln, bias=EPS, scale=1.0 / 256)
            kb.act(rsS[0:P, 0:2], rsS[0:P, 0:2], AF.Exp, scale=-0.5)
            yield
            kb.tt("dve", y3[0:P, :].re("p (g d) -> p g d", g=2), y2[0:P, :].re("p (g d) -> p g d", g=2),
                  rsS[0:P, 0:2].un(2).bc([P, 2, 256]), ALU.mult)
            yield
            stage(9)
            pM = pTR.get()
            kb.trs([(pM[:, j * P:(j + 1) * P], y3[0:P, j * 128:(j + 1) * 128], IDN[0:P, 0:P]) for j in range(4)])
            for j in range(4):
                kb.act(mixw(2 + j, c0, P), pM[:, j * P:(j + 1) * P], AF.Copy, scale=pfm[:, pl + PF_SNW + j:pl + PF_SNW + j + 1])
            yield

        def hg_front(b):
            s = b // cfg.BPS
            slot = slots[s]
            c0 = b * P
            stage(8)
            pT2 = pHF.get()
            pT2b = pT2.cast(BF16)
            kb.trs([(pT2b[0:P, c * 128:(c + 1) * 128], chunkv(kh, c, T)[:, c0:c0 + P], IDB) for c in range(2)])
            kb.copy("act", kh_tm[0:P, :], pT2b[0:P, 0:256])
            yield
            pA = pHF.get()
            items = []
            for h in range(4):
                hp, c = h % 2, h // 2
                qt_ = qt_e if hp == 0 else qt_o
                items.append((pA[0:P, (hp * 2 + c) * P:(hp * 2 + c + 1) * P], chunkv(kt, c, T)[:, c0:c0 + P],
                              chunkv(qt_, c, T)[:, c0:c0 + P], True, True))
            kb.mms(items)
            kb.tt("dve", att_m[0:P, 0:4 * P].re("p (q t) -> p q t", q=4), pA[0:P, 0:4 * P].re("p (q t) -> p q t", q=4),
                  MBD[0:P, 0:P].un(1).bc([P, 4, P]), ALU.mult)
            yield
            sc = Scur[l][slot]
            pDs = pHF.get()
            for jj in range(cfg.NCB):
                if cfg.NCB == 1:
                    vm = v_tm[0:P, b * 256:(b + 1) * 256]
                else:
                    vm = v_msk[0:P, jj * 256:(jj + 1) * 256]
                    kb.act(vm, v_tm[0:P, b * 256:(b + 1) * 256], AF.Copy, scale=rowm[0:P, jj:jj + 1])
                kb.mms([(pDs[hp * 64:(hp + 1) * 64, (jj * 2 + c) * 64:(jj * 2 + c + 1) * 64],
                         kh_tm[0:P, c * 128 + hp * 64:c * 128 + (hp + 1) * 64],
                         vm[:, c * 128 + hp * 64:c * 128 + (hp + 1) * 64], True, True) for c in range(2) for hp in range(2)])
            yield
            for jj in range(cfg.NCB):
                j = b * cfg.NCB + jj
                kb.copy("act", Sbf[:, jj * 128:(jj + 1) * 128], sc)
                for c in range(2):
                    kb.stt(sc[:, c * 64:(c + 1) * 64], sc[:, c * 64:(c + 1) * 64], ebl[:, c * 8 + j:c * 8 + j + 1],
                           pDs[:, (jj * 2 + c) * 64:(jj * 2 + c + 1) * 64], ALU.mult, ALU.add)
                yield
            if b > 0:
                yield "drain"
            pO = pHO.get()
            for h in range(4):
                hp, c = h % 2, h // 2
                qh_ = qh_e if hp == 0 else qh_o
                items = [(pO[0:P, h * 64:(h + 1) * 64], att_m[0:P, (hp * 2 + c) * P:(hp * 2 + c + 1) * P],
                          v_tm[0:P, b * 256 + h * 64:b * 256 + (h + 1) * 64], True, False)]
                for jj in range(cfg.NCB):
                    items.append((pO[jj * CH:(jj + 1) * CH, h * 64:(h + 1) * 64],
                                  chunkv(qh_, c, T)[:, c0 + jj * CH:c0 + (jj + 1) * CH],
                                  Sbf[:, jj * 128 + c * 64:jj * 128 + (c + 1) * 64], False, jj == cfg.NCB - 1))
                kb.mms(items)
            hg_po[0] = pO
            yield

        hg_po = [None]

        def hg_back(b):
            c0 = b * P
            pO = hg_po[0]
            kb.act(junk[0:P, 512:768], pO[0:P, 0:256], AF.Square, accum=ssH[0:P, 0:1])
            kb.act(rsH[0:P, 0:1], ssH[0:P, 0:1], AF.Ln, bias=EPS, scale=1.0 / 256)
            kb.act(rsH[0:P, 0:1], rsH[0:P, 0:1], AF.Exp, scale=-0.5)
            yield
            kb.act(oc[0:P, :], pO[0:P, 0:256], AF.Copy, scale=rsH[0:P, 0:1])
            kb.tt("pool", oc[0:P, :], oc[0:P, :], gs[0:P, b * 256:(b + 1) * 256], ALU.mult)
            yield
            pM2 = pTR.get()
            kb.trs([(pM2[:, j * P:(j + 1) * P], oc[0:P, j * 128:(j + 1) * 128], IDN[0:P, 0:P]) for j in range(2)])
            for j in range(2):
                kb.act(mixw(6 + j, c0, P), pM2[:, j * P:(j + 1) * P], AF.Copy, scale=pfm[:, pl + PF_HNW + j:pl + PF_HNW + j + 1])
            yield

        stage(3)
        cg = c_prep()
        wz = [wget(), wget()]
        for b in range(NB):
            p = kb.ps()
            kb.mm(p[0:P, 0:512], [(hk(kc, b * P, P), wz[kc // 4][:, (kc % 4) * 512:(kc % 4 + 1) * 512]) for kc in range(8)])
            kb.act(zs[0:P, b * 512:(b + 1) * 512], p[0:P, 0:512], AF.Silu)
        stage(31)
        for _ in cg:
            pass
        wg = [wget(), wget()]
        for b in range(NB):
            p = kb.ps()
            kb.mm(p[0:P, 0:512], [(hk(kc, b * P, P), wg[kc // 4][:, (kc % 4) * 512:(kc % 4 + 1) * 512]) for kc in range(8)])
            kb.copy("dve", v_tm[0:P, b * 256:(b + 1) * 256], p[0:P, 0:256])
            kb.act(gs[0:P, b * 256:(b + 1) * 256], p[0:P, 256:512], AF.Silu)
        stage(32)
        for b in range(NB):
            p = kb.ps()
            kb.mm(p[0:P, 0:8], [(hk(kc, b * P, P), wdt[:, l * 64 + kc * 8:l * 64 + kc * 8 + 8]) for kc in range(8)])
            kb.tt("dve", dtp[0:P, b * 8:b * 8 + 8], p[0:P, 0:8], pbc[0:P, pb + PB_DTB:pb + PB_DTB + 8], ALU.add)
        for _ in cg:
            pass

        def chain_x():
            yield from interleave(pipelined(NB, hg_front, hg_back), a_post(0), a_post(1))

        def chain_y():
            yield from pipelined(NB, ssd_front, ssd_back)

        for _ in interleave(chain_y(), chain_x()):
            pass

        stage(10)
        for bi in range(4):
            wv = wget()
            for j in range(2):
                dc = 2 * bi + j
                p = kb.ps()
                kb.mm(p[:, 0:T], [(wv[:, (j * 8 + ec) * 128:(j * 8 + ec + 1) * 128], mixw(ec, 0, T)) for ec in range(8)])
                kb.tt("dve", xk(dc, T), xk(dc, T), p[:, 0:T], ALU.add)

        stage(11)
        rmsnorm(T, pl + PF_N2)
        for fc in range(22):
            wv = wget()
            pg_ = kb.ps()
            pu_ = kb.ps()
            kb.mm(pg_[:, 0:T], [(wv[:, kc * 128:(kc + 1) * 128], hk(kc, 0, T)) for kc in range(8)])
            kb.mm(pu_[:, 0:T], [(wv[:, 1024 + kc * 128:1024 + (kc + 1) * 128], hk(kc, 0, T)) for kc in range(8)])
            s_ = sil[fc % 2]
            kb.act(s_[:, 0:T], pg_[:, 0:T], AF.Silu)
            kb.tt("dve", V(act_b.ap[:, fc * TILE:fc * TILE + T], [act_b.bufs[fc]]), s_[:, 0:T], pu_[:, 0:T], ALU.mult)
        for dc in range(8):
            w0 = wget()
            w1 = wget()
            p = kb.ps()
            pairs = []
            for fc in range(22):
                wv = w0 if fc < 11 else w1
                o = (fc % 11) * 128
                pairs.append((wv[:, o:o + 128], V(act_b.ap[:, fc * TILE:fc * TILE + T], [act_b.bufs[fc]])))
            kb.mm(p[:, 0:T], pairs)
            kb.tt("dve", xk(dc, T), xk(dc, T), p[:, 0:T], ALU.add)
    def zero_states():
        for l in range(DEPTH):
            kb.memset("pool", convst[l][0], 0.0)
            kb.memset("pool", hAst[l][0], 0.0)
            kb.memset("pool", STf[l][0], 0.0)
            kb.memset("pool", STb[l][0], 0.0)
            kb.memset("pool", Scur[l][0], 0.0)

    def store_states(seq_i, slot):
        for l in range(DEPTH):
            base = (seq_i * DEPTH + l)
            kb.dma("pool", o_conv[:, base * 30:(base + 1) * 30], convst[l][slot])
            kb.dma("pool", o_hA[:, base * 2:(base + 1) * 2], hAst[l][slot])
            kb.dma("pool", o_ssd[:, base * 512:(base + 1) * 512], STf[l][slot])
            kb.dma("pool", o_hg[:, base * 128:(base + 1) * 128], Scur[l][slot])

    def load_states(si, slot):
        for l in range(DEPTH):
            base = (l * nsamp + si)
            kb.dma("pool", convst[l][slot], sconv_d[:, base * 30:(base + 1) * 30])
            kb.dma("pool", hAst[l][slot], shA_d[:, base * 2:(base + 1) * 2])
            kb.dma("pool", STf[l][slot], sssd_d[:, base * 512:(base + 1) * 512])
            kb.copy("pool", STb[l][slot], STf[l][slot])
            kb.dma("pool", Scur[l][slot], shg_d[:, base * 128:(base + 1) * 128])

    def load_x(cfg, x_src):
        T = cfg.T
        for k in range(8):
            kb.dma("pool", xk(k, T), V(x_src.ap[:, k, :], []))

    def run_tile(cfg, y_dst, slots, nxt=None):
        T = cfg.T
        for l in range(DEPTH):
            layer(l, cfg, slots)
        rmsnorm(T, PF_FN, out_is_h=False, out_v=y_out)
        if nxt is not None:
            load_x(*nxt)
        for k in range(8):
            kb.dma("pool", V(y_dst.ap[:, k, :], y_dst.bufs), V(y_out.ap[:, k * TILE:k * TILE + T], [y_out.bufs[k]]))

    cfgP = TileCfg(TILE, 128, 1, TILE, 64)
    xTp3 = xT_p.ap.rearrange("(k p) t -> p k t", p=128)
    yTp3 = yT_p.ap.rearrange("(k p) t -> p k t", p=128)
    cfgS = TileCfg(TS, lsamp, nsamp, lsamp, lsamp) if nsamp > 0 else None
    xTs3 = xT_s.ap.rearrange("(k p) t -> p k t", p=128)
    yTs3 = yT_s.ap.rearrange("(k p) t -> p k t", p=128)

    def xsrc(ti):
        return V(xTp3[:, :, ti * TILE:(ti + 1) * TILE], [])

    assert nsamp > 0, "the first (sample) tile carries the weight conversion"
    load_x(cfgS, V(xTs3, []))
    for si in range(nsamp):
        load_states(si, si)
    run_tile(cfgS, V(yTs3, yT_s.bufs), list(range(nsamp)), (cfgP, xsrc(0)) if NT > 0 else None)
    for si in range(nsamp):
        store_states(1 + si, si)
    if NT > 0:
        zero_states()
    for ti in range(NT):
        nxt = (cfgP, xsrc(ti + 1)) if ti + 1 < NT else None
        run_tile(cfgP, V(yTp3[:, :, ti * TILE:(ti + 1) * TILE], yT_p.bufs), [0], nxt)
    if NT > 0:
        store_states(0, 0)

    with nc.Block() as block:
        @block.tensor
        def _(e):
            pg.replay("pe", e)

        @block.scalar
        def _(e):
            pg.replay("act", e)

        @block.vector
        def _(e):
            pg.replay("dve", e)

        @block.gpsimd
        def _(e):
            pg.replay("pool", e)

        @block.sync
        def _(e):
            pg.replay("sp", e)
            pg.finish("sp", e)
    print(f"[kernel] ops: { {k: len(v) for k, v in pg.ops.items()} } instrs~{pg.nops}")
    return nc


def _fm_chunk(W, c0):
    sub = W[:, c0:c0 + 128]
    K = sub.shape[0]
    return sub.reshape(K // 128, 128, 128).transpose(1, 0, 2).reshape(128, -1)


def _layer_blocks(w_in, w_out, w_g, w_u, w_d, cw):
    blocks = []

    def conv_blk(c0, ci):
        blk = np.zeros((128, WB), np.float32)
        blk[:, 0:1024] = _fm_chunk(w_in, c0)
        for k in range(4):
            blk[np.arange(128), 1024 + k * 128 + np.arange(128)] = cw[ci, k]
        return blk

    def pair_blk(c0):
        return np.concatenate([_fm_chunk(w_in, c0), _fm_chunk(w_in, c0 + 128)], axis=1)

    blocks.append(conv_blk(C_XA, 0))
    blocks.append(conv_blk(C_XA + 128, 1))
    blocks.append(pair_blk(C_GA))
    for j in range(8):
        blocks.append(conv_blk(C_XS + j * 128, 2 + j))
    blocks.append(pair_blk(C_Q))
    blocks.append(pair_blk(C_F))
    for c0 in (C_Z, C_I):
        r = w_in[:, c0:c0 + 512].reshape(8, 128, 512).transpose(1, 0, 2)
        blocks.append(r[:, 0:4].reshape(128, -1))
        blocks.append(r[:, 4:8].reshape(128, -1))
    for i in range(0, 8, 2):
        blocks.append(np.concatenate([_fm_chunk(w_out, i * 128), _fm_chunk(w_out, (i + 1) * 128)], axis=1))
    for fc in range(22):
        blocks.append(np.concatenate([_fm_chunk(w_g, fc * 128), _fm_chunk(w_u, fc * 128)], axis=1))
    for dc in range(8):
        r = _fm_chunk(w_d, dc * 128).reshape(128, 22, 128)
        for half in range(2):
            blk = np.zeros((128, WB), np.float32)
            blk[:, 0:1408] = r[:, half * 11:(half + 1) * 11].reshape(128, -1)
            blocks.append(blk)
    assert len(blocks) == NBLK_L
    return blocks


def _col(v, n):
    return np.asarray(v).reshape(n, 128).T


_PROG_CACHE = {}


def kernel(x_prompt, x_sample, state_rglru_conv, state_rglru_h, state_ssd_conv, state_ssd, state_hgrn,
           norm1_w, w_in, rglru_conv_w, rglru_conv_b, rglru_wa, rglru_ba, rglru_wx, rglru_bx, rglru_lambda,
           ssd_conv_w, ssd_conv_b, ssd_dt_bias, ssd_a_log, ssd_d, ssd_norm_w, hgrn_lb, hgrn_norm_w,
           w_out, norm2_w, w_ffn_gate, w_ffn_up, w_ffn_down, final_norm_w):
    f = lambda a: np.ascontiguousarray(np.asarray(a, dtype=np.float32))
    x_prompt, x_sample = f(x_prompt), f(x_sample)
    ncores, seq, _ = x_prompt.shape
    nsamp = x_sample.shape[0] // ncores
    lsamp = x_sample.shape[1]
    TS = nsamp * lsamp
    w_in, w_out, w_g, w_u, w_d = f(w_in), f(w_out), f(w_ffn_gate), f(w_ffn_up), f(w_ffn_down)
    blocks = []
    pfm = np.zeros((128, PF_TOT), np.float32)
    pbc = np.zeros((1, PB_TOT), np.float32)
    wab = np.zeros((128, DEPTH * 4 * 128), np.float32)
    wdt = np.zeros((128, DEPTH * 64), np.float32)
    rcw, rcb, scw, scb = f(rglru_conv_w), f(rglru_conv_b), f(ssd_conv_w), f(ssd_conv_b)
    for l in range(DEPTH):
        cwl = np.concatenate([rcw[l].reshape(4, 2, 128), scw[l].reshape(4, 8, 128)], axis=1).transpose(1, 0, 2)
        blocks += _layer_blocks(w_in[l], w_out[l], w_g[l], w_u[l], w_d[l], cwl)
    wblk = np.ascontiguousarray(np.stack(blocks, 0))
    wa, wx = f(rglru_wa), f(rglru_wx)
    for l in range(DEPTH):
        o = PF_L * l
        pfm[:, o + PF_N1:o + PF_N1 + 8] = _col(f(norm1_w)[l], 8)
        pfm[:, o + PF_N2:o + PF_N2 + 8] = _col(f(norm2_w)[l], 8)
        for ci in range(10):
            for k in range(4):
                src = rcw[l, k, ci * 128:(ci + 1) * 128] if ci < 2 else scw[l, k, (ci - 2) * 128:(ci - 1) * 128]
                pfm[:, o + PF_CW + ci * 4 + k] = src
            pfm[:, o + PF_CB + ci] = rcb[l, ci * 128:(ci + 1) * 128] if ci < 2 else scb[l, (ci - 2) * 128:(ci - 1) * 128]
        pfm[:, o + PF_BA:o + PF_BA + 2] = _col(f(rglru_ba)[l], 2)
        pfm[:, o + PF_BX:o + PF_BX + 2] = _col(f(rglru_bx)[l], 2)
        pfm[:, o + PF_LAM:o + PF_LAM + 2] = _col(f(rglru_lambda)[l], 2)
        pfm[:, o + PF_SNW:o + PF_SNW + 4] = _col(f(ssd_norm_w)[l], 4)
        pfm[:, o + PF_HNW:o + PF_HNW + 2] = _col(f(hgrn_norm_w)[l], 2)
        pfm[:, PF_LB + 2 * l:PF_LB + 2 * l + 2] = _col(f(hgrn_lb)[l], 2)
        ob = PB_L * l
        pbc[0, ob + PB_DTB:ob + PB_DTB + 8] = f(ssd_dt_bias)[l]
        pbc[0, ob + PB_ALOG:ob + PB_ALOG + 8] = f(ssd_a_log)[l]
        pbc[0, ob + PB_DSK:ob + PB_DSK + 8] = f(ssd_d)[l]
        pbc[0, ob + PB_SNW:ob + PB_SNW + 512] = f(ssd_norm_w)[l]
        pbc[0, ob + PB_HNW:ob + PB_HNW + 256] = f(hgrn_norm_w)[l]
        for which, W in enumerate((wa, wx)):
            for c in range(2):
                base = (l * 4 + which * 2 + c) * 128
                for hh in range(2):
                    wab[hh * 64:(hh + 1) * 64, base + hh * 64:base + (hh + 1) * 64] = W[l, 2 * c + hh]
        wdt[:, l * 64:(l + 1) * 64] = w_in[l][:, C_DT:C_DT + 8].reshape(8, 128, 8).transpose(1, 0, 2).reshape(128, 64)
    pfm[:, PF_FN:PF_FN + 8] = _col(f(final_norm_w), 8)
    s_rc, s_rh, s_sc, s_ss, s_hg = f(state_rglru_conv), f(state_rglru_h), f(state_ssd_conv), f(state_ssd), f(state_hgrn)

    in_maps = []
    for core in range(ncores):
        sconv = np.zeros((128, DEPTH, nsamp, 10, 3), np.float32)
        shA = np.zeros((128, DEPTH, nsamp, 2), np.float32)
        sssd = np.zeros((128, DEPTH, nsamp, 512), np.float32)
        shg = np.zeros((128, DEPTH, nsamp, 128), np.float32)
        for l in range(DEPTH):
            for si in range(nsamp):
                b = core * nsamp + si
                sconv[:, l, si, 0:2, :] = s_rc[l, b].reshape(3, 2, 128).transpose(2, 1, 0)
                sconv[:, l, si, 2:10, :] = s_sc[l, b].reshape(3, 8, 128).transpose(2, 1, 0)
                shA[:, l, si, :] = _col(s_rh[l, b], 2)
                sssd[:, l, si, :] = s_ss[l, b].transpose(2, 0, 1).reshape(128, 512)
                shg[:, l, si, :] = s_hg[l, b].reshape(2, 2, 64, 64).transpose(1, 2, 0, 3).reshape(128, 128)
        in_maps.append({
            "xT_p": np.ascontiguousarray(x_prompt[core].T),
            "xT_s": np.ascontiguousarray(x_sample[core * nsamp:(core + 1) * nsamp].reshape(TS, D).T),
            "wblk": wblk, "pfm": pfm, "pbc": pbc, "wab": wab, "wdt": wdt,
            "s_conv": sconv.reshape(128, -1), "s_hA": shA.reshape(128, -1),
            "s_ssd": sssd.reshape(128, -1), "s_hgrn": shg.reshape(128, -1),
        })

    with contextlib.ExitStack() as stack:
        nc = bass.Bass("TRN2", target_bir_lowering=False)
        build_program(nc, stack, seq, nsamp, lsamp)
        res = run_bass_kernel_spmd(nc, in_maps, core_ids=list(range(ncores)))
    R = res.results

    B, SB = ncores, ncores * nsamp
    y_p = np.zeros((B, seq, D), np.float32)
    y_s = np.zeros((SB, lsamp, D), np.float32)
    outs = {}
    for pre, nb in (("p", B), ("s", SB)):
        outs[pre + "_rc"] = np.zeros((DEPTH, nb, 3, 256), np.float32)
        outs[pre + "_rh"] = np.zeros((DEPTH, nb, 256), np.float32)
        outs[pre + "_sc"] = np.zeros((DEPTH, nb, 3, 1024), np.float32)
        outs[pre + "_ss"] = np.zeros((DEPTH, nb, 8, 64, 128), np.float32)
        outs[pre + "_hg"] = np.zeros((DEPTH, nb, 4, 64, 64), np.float32)
    for core in range(ncores):
        r = R[core]
        y_p[core] = r["yT_p"].T
        ys = r["yT_s"].T.reshape(nsamp, lsamp, D)
        y_s[core * nsamp:(core + 1) * nsamp] = ys
        oc_ = r["o_conv"].reshape(128, 1 + nsamp, DEPTH, 10, 3)
        oh_ = r["o_hA"].reshape(128, 1 + nsamp, DEPTH, 2)
        os_ = r["o_ssd"].reshape(128, 1 + nsamp, DEPTH, 8, 64)
        og_ = r["o_hgrn"].reshape(2, 64, 1 + nsamp, DEPTH, 2, 64)
        for qi in range(1 + nsamp):
            pre, b = ("p", core) if qi == 0 else ("s", core * nsamp + qi - 1)
            for l in range(DEPTH):
                outs[pre + "_rc"][l, b] = oc_[:, qi, l, 0:2, :].transpose(2, 1, 0).reshape(3, 256)
                outs[pre + "_sc"][l, b] = oc_[:, qi, l, 2:10, :].transpose(2, 1, 0).reshape(3, 1024)
                outs[pre + "_rh"][l, b] = oh_[:, qi, l, :].T.reshape(256)
                outs[pre + "_ss"][l, b] = os_[:, qi, l].transpose(1, 2, 0)
                outs[pre + "_hg"][l, b] = og_[:, :, qi, l].transpose(2, 0, 1, 3).reshape(4, 64, 64)
    return (y_p, y_s, outs["p_rc"], outs["p_rh"], outs["p_sc"], outs["p_ss"], outs["p_hg"],
            outs["s_rc"], outs["s_rh"], outs["s_sc"], outs["s_ss"], outs["s_hg"])
```

```python
import contextlib
import numpy as np
import concourse.bass as bass
import concourse.mybir as mybir
from concourse.bass_utils import run_bass_kernel_spmd

F32 = mybir.dt.float32
BF16 = mybir.dt.bfloat16
AF = mybir.ActivationFunctionType
ALU = mybir.AluOpType

D = 1024
DEPTH = 2
DFF = 2816
EPS = 1e-6
WB = 2048
NSLOT = 4
NBLK_L = 59
BLK_SIZES = [1536, 1536, 2048] + [1536] * 8 + [2048, 2048] + [2048] * 4 + [2048] * 4 + [2048] * 22 + [1408] * 16
TILE = 512
NCORES = 8

C_XA, C_GA, C_Z, C_XS, C_B, C_C, C_DT, C_Q, C_F, C_I, C_G = 0, 256, 512, 1024, 1536, 1792, 2048, 2056, 2312, 2568, 2824

PF_N1, PF_N2, PF_CW, PF_CB, PF_BA, PF_BX, PF_LAM = 0, 8, 16, 56, 66, 68, 70
PF_SNW, PF_HNW = 72, 76
PF_L = 78
PF_FN = 2 * PF_L
PF_LB = PF_FN + 8
PF_TOT = PF_LB + 4
PB_DTB, PB_ALOG, PB_DSK, PB_SNW, PB_HNW = 0, 8, 16, 24, 536
PB_L = 792
PB_TOT = 2 * PB_L


class Buf:
    __slots__ = ("name", "lo", "hi", "space", "writers", "readers", "alias")

    def __init__(self, name, space, lo, hi):
        self.name, self.space, self.lo, self.hi = name, space, lo, hi
        self.writers = {}
        self.readers = {}
        self.alias = []


class V:
    __slots__ = ("ap", "bufs")

    def __init__(self, ap, bufs):
        self.ap = ap
        self.bufs = list(bufs)

    def __getitem__(self, k):
        return V(self.ap[k], self.bufs)

    def re(self, pattern_, **kw):
        return V(self.ap.rearrange(pattern_, **kw), self.bufs)

    def bc(self, shape):
        return V(self.ap.broadcast_to(list(shape)), self.bufs)

    def un(self, axis):
        return V(self.ap.unsqueeze(axis), self.bufs)

    def cast(self, dt):
        return V(self.ap.bitcast(dt), self.bufs)

    def only(self, i):
        return V(self.ap, [self.bufs[i]])


EPOCH = 1000
import os as _os
_STOP = [False]


def stage(n):
    if int(_os.environ.get("KSTOP", "-1")) == n:
        _STOP[0] = True
        print(f"[kernel] debug: stopping emission at stage {n}")


NDMASEM = 24


class Prog:
    ENGS = ("pe", "act", "dve", "pool", "sp")

    def __init__(self, nc, stack):
        self.nc = nc
        self.stack = stack
        self.ops = {e: [] for e in self.ENGS}
        self.cnt = {e: 0 for e in self.ENGS}
        self.esems = {e: [] for e in self.ENGS}
        self.seen = {e: {} for e in self.ENGS}
        self.dsems = [stack.enter_context(nc.semaphore(f"dma{i}")) for i in range(NDMASEM)]
        self.dval = [0] * NDMASEM
        self.dnext = 0
        self.sem_of = {}
        for i, s in enumerate(self.dsems):
            self.sem_of[("d", i)] = s
        self.nops = 0

    def _esem(self, eng):
        ep = self.cnt[eng] // EPOCH
        while len(self.esems[eng]) <= ep:
            s = self.stack.enter_context(self.nc.semaphore(f"{eng}{len(self.esems[eng])}"))
            self.sem_of[(eng, len(self.esems[eng]))] = s
            self.esems[eng].append(s)
        return (eng, ep), self.cnt[eng] % EPOCH + 1

    def add(self, eng, emits, reads=(), writes=(), dma=False):
        if _STOP[0]:
            return
        if not isinstance(emits, (list, tuple)):
            emits = [emits]
        preads = [b for b in reads if b.space == "psum"]
        if preads:
            reads = [b for b in reads if b.space != "psum"]
            writes = list(writes) + [b for b in preads if b not in writes]
        need = {}

        def want(k, v):
            if need.get(k, 0) < v:
                need[k] = v

        for b in reads:
            for bb in [b] + b.alias:
                for k, v in bb.writers.items():
                    want(k, v)
        for b in writes:
            for bb in [b] + b.alias:
                for k, v in bb.writers.items():
                    want(k, v)
                for k, v in bb.readers.items():
                    want(k, v)
        if dma:
            j = self.dnext
            self.dnext = (self.dnext + 1) % NDMASEM
            key = ("d", j)
            if self.dval[j] > 0:
                want(key, self.dval[j])
            self.dval[j] += 16
            val = self.dval[j]
            inc = (key, 16)
        else:
            key, val = self._esem(eng)
            self.cnt[eng] += 1
            inc = (key, 1)
        waits = []
        seen = self.seen[eng]
        for k, v in need.items():
            if k[0] == "pe" and eng == "pe":
                continue
            if seen.get(k, 0) >= v:
                continue
            seen[k] = v
            waits.append((k, v))
        self.ops[eng].append((waits, emits, inc))
        for b in writes:
            b.writers = {key: val}
            b.readers = {}
        for b in reads:
            if b.readers.get(key, 0) < val:
                b.readers[key] = val
        self.nops += len(emits)

    def replay(self, eng, e):
        for waits, emits, inc in self.ops[eng]:
            for k, v in waits:
                e.wait_ge(self.sem_of[k], v)
            last = None
            for f in emits:
                last = f(e)
            last.then_inc(self.sem_of[inc[0]], inc[1])

    def finish(self, eng, e):
        for j in range(NDMASEM):
            if self.dval[j] > 0:
                e.wait_ge(self.dsems[j], self.dval[j])


class KB:
    def __init__(self, nc, stack, seq, nsamp=2, lsamp=16, debug=False):
        self.nc = nc
        self.stack = stack
        self.pg = Prog(nc, stack)
        self.seq = seq
        self.nt = seq // TILE
        self.nsamp = nsamp
        self.lsamp = lsamp
        self.debug = debug
        self.sbufs = []
        self.off = 0
        self.maxoff = 0
        self.CAP = 212000
        self.arena = stack.enter_context(nc.sbuf_tensor("arena", [128, self.CAP // 4], F32))
        self.psum = stack.enter_context(nc.psum_tensor("psum", [128, 4096], F32))
        self.pbufs = [Buf(f"bank{i}", "psum", i, i + 1) for i in range(8)]
        self.pnext = 0
        self.ntiles_total = self.nt + 1
        self.wissued = 0
        self.wtotal = self.ntiles_total * DEPTH * NBLK_L
        self.wused = 0

    def alloc(self, name, cols, dt=F32, nbufs=1):
        esz = 4 if dt == F32 else 2
        nbytes = (cols * esz + 3) // 4 * 4
        lo = self.off
        self.off += nbytes
        self.maxoff = max(self.maxoff, self.off)
        assert self.off <= self.CAP, f"SBUF overflow at {name}: {self.off}"
        ap = self.arena[:, lo // 4:(lo + nbytes) // 4]
        if dt != F32:
            ap = ap.bitcast(dt)
        ap = ap[:, 0:cols]
        bufs = []
        per = nbytes // nbufs
        for i in range(nbufs):
            b = Buf(f"{name}{i}", "sb", lo + i * per, lo + (i + 1) * per if i < nbufs - 1 else lo + nbytes)
            for o in self.sbufs:
                if o.lo < b.hi and b.lo < o.hi:
                    o.alias.append(b)
                    b.alias.append(o)
            self.sbufs.append(b)
            bufs.append(b)
        return V(ap, bufs)

    def ps(self, n=1):
        if n == 2 and self.pnext % 2 == 1:
            self.pnext = (self.pnext + 1) % 8
        i = self.pnext
        self.pnext = (self.pnext + n) % 8
        return V(self.psum[:, i * 512:(i + n) * 512], self.pbufs[i:i + n])

    @staticmethod
    def _bufs(*vs):
        out = []
        for v in vs:
            if isinstance(v, V):
                for b in v.bufs:
                    if b not in out:
                        out.append(b)
        return out

    @staticmethod
    def _a(v):
        return v.ap if isinstance(v, V) else v

    def mm(self, out, pairs, extra_reads=(), fp32=False):
        n = len(pairs)
        emits = []
        for i, (l, r) in enumerate(pairs):
            emits.append(lambda e, l=l, r=r, i=i: e.matmul(out.ap, l.ap, r.ap, start=(i == 0), stop=(i == n - 1)))
        rd = self._bufs(*[p[0] for p in pairs], *[p[1] for p in pairs], *extra_reads)
        self.pg.add("pe", emits, reads=rd, writes=out.bufs)

    def mms(self, items):
        emits = []
        rd, wr = [], []
        for (o, l, r, st, sp) in items:
            emits.append(lambda e, o=o, l=l, r=r, st=st, sp=sp: e.matmul(o.ap, l.ap, r.ap, start=st, stop=sp))
            rd += self._bufs(l, r)
            wr += self._bufs(o)
        self.pg.add("pe", emits, reads=list(dict.fromkeys(rd)), writes=list(dict.fromkeys(wr)))

    def trs(self, items):
        emits = []
        rd, wr = [], []
        for (o, i, idn) in items:
            emits.append(lambda e, o=o, i=i, idn=idn: e.transpose(o.ap, i.ap, idn.ap))
            rd += self._bufs(i, idn)
            wr += self._bufs(o)
        self.pg.add("pe", emits, reads=list(dict.fromkeys(rd)), writes=list(dict.fromkeys(wr)))

    def act(self, out, in_, func, bias=None, scale=None, accum=None, eng="act"):
        kw = {}
        if bias is not None:
            kw["bias"] = self._a(bias)
        if scale is not None:
            kw["scale"] = self._a(scale)
        if accum is not None:
            kw["accum_out"] = accum.ap
        self.pg.add("act", lambda e: e.activation(out.ap, in_.ap, func, **kw),
                    reads=self._bufs(in_, bias, scale), writes=self._bufs(out, accum))

    def tt(self, eng, out, a, b, op):
        self.pg.add(eng, lambda e: e.tensor_tensor(out.ap, a.ap, b.ap, op),
                    reads=self._bufs(a, b), writes=out.bufs)

    def ts(self, eng, out, a, s1, s2, op0, op1=None):
        if op1 is None:
            f = lambda e: e.tensor_scalar(out.ap, a.ap, self._a(s1), None, op0)
        else:
            f = lambda e: e.tensor_scalar(out.ap, a.ap, self._a(s1), self._a(s2), op0, op1)
        self.pg.add(eng, f, reads=self._bufs(a, s1, s2), writes=out.bufs)

    def stt(self, out, a, s, b, op0, op1):
        self.pg.add("dve", lambda e: e.scalar_tensor_tensor(out.ap, a.ap, self._a(s), b.ap, op0, op1),
                    reads=self._bufs(a, s, b), writes=out.bufs)

    def copy(self, eng, out, in_):
        if eng == "act":
            self.pg.add("act", lambda e: e.copy(out.ap, in_.ap), reads=in_.bufs, writes=out.bufs)
        else:
            self.pg.add(eng, lambda e: e.tensor_copy(out.ap, in_.ap), reads=in_.bufs, writes=out.bufs)

    def scan(self, out, d0, d1, init):
        self.pg.add("dve", lambda e: e.tensor_tensor_scan(out.ap, d0.ap, d1.ap, self._a(init), ALU.mult, ALU.add),
                    reads=self._bufs(d0, d1, init), writes=out.bufs)

    def recip(self, out, in_):
        self.pg.add("dve", lambda e: e.reciprocal(out.ap, in_.ap), reads=in_.bufs, writes=out.bufs)

    def memset(self, eng, out, val):
        self.pg.add(eng, lambda e: e.memset(out.ap, val), reads=(), writes=out.bufs)

    def dma(self, q, out, in_, **kw):
        self.pg.add(q, lambda e: e.dma_start(out=out.ap, in_=in_.ap, **kw),
                    reads=self._bufs(in_), writes=self._bufs(out), dma=True)


class TileCfg:
    def __init__(self, T, P, NSEG, L, CH):
        self.T, self.P, self.NSEG, self.L, self.CH = T, P, NSEG, L, CH
        self.NB = T // P
        self.NCH = T // CH
        self.NCB = P // CH
        self.BPS = L // P


def build_program(nc, stack, seq, nsamp=2, lsamp=16):
    kb = KB(nc, stack, seq, nsamp, lsamp)
    pg = kb.pg
    NT = seq // TILE
    TS = nsamp * lsamp
    NSEQ = 1 + nsamp
    def din(name, shape, dt=F32):
        return V(nc.dram_tensor(name, shape, dt, kind="ExternalInput").ap(), [])

    def dout(name, shape):
        return V(nc.dram_tensor(name, shape, F32, kind="ExternalOutput").ap(), [Buf(name, "dram", 0, 1)])

    xT_p = din("xT_p", [D, seq])
    xT_s = din("xT_s", [D, TS])
    wblk = din("wblk", [DEPTH * NBLK_L, 128, WB])
    pfm_d = din("pfm", [128, PF_TOT])
    pbc_d = din("pbc", [1, PB_TOT])
    wab_d = din("wab", [128, DEPTH * 4 * 128])
    wdt_d = din("wdt", [128, DEPTH * 64])
    sconv_d = din("s_conv", [128, DEPTH * nsamp * 30])
    shA_d = din("s_hA", [128, DEPTH * nsamp * 2])
    sssd_d = din("s_ssd", [128, DEPTH * nsamp * 512])
    shg_d = din("s_hgrn", [128, DEPTH * nsamp * 128])
    yT_p = dout("yT_p", [D, seq])
    yT_s = dout("yT_s", [D, TS])
    o_conv = dout("o_conv", [128, NSEQ * DEPTH * 30])
    o_hA = dout("o_hA", [128, NSEQ * DEPTH * 2])
    o_ssd = dout("o_ssd", [128, NSEQ * DEPTH * 512])
    o_hg = dout("o_hgrn", [128, NSEQ * DEPTH * 128])
    wscr_t = nc.dram_tensor("wscr", [DEPTH * NBLK_L, 128, WB], BF16, kind="Internal").ap()
    wscr = [V(wscr_t[i], [Buf(f"wscr{i}", "dram", 0, 1)]) for i in range(DEPTH * NBLK_L)]

    A = kb.alloc
    x_fm = A("x_fm", 8 * TILE, nbufs=8)
    h_bf = A("h_bf", 8 * TILE, BF16, nbufs=8)
    wslots = [A(f"wslot{i}", WB, BF16) for i in range(NSLOT)]
    U = A("U", 128)
    SL = A("SL", 128)
    MBD = A("MBD", 128)
    IDN = A("IDN", 128)
    ONF = A("ONF", 128)
    IDB = A("IDB", 128, BF16)
    ONB = A("ONB", 128, BF16)
    RMK = A("RMK", TILE)
    pfm = A("pfm", PF_TOT)
    pbc = A("pbc", PB_TOT)
    drv = A("drv", 64)
    aneg = A("aneg", 2 * 8)
    rowm = A("rowm", 2)
    wab = A("wab", DEPTH * 4 * 128, BF16)
    wdt = A("wdt", DEPTH * 64, BF16)
    NSLOTS = max(1, nsamp)
    convst = [[A(f"convst{l}_{s}", 30) for s in range(NSLOTS)] for l in range(DEPTH)]
    hAst = [[A(f"hA{l}_{s}", 2) for s in range(NSLOTS)] for l in range(DEPTH)]
    STf = [[A(f"ST{l}_{s}", 512) for s in range(NSLOTS)] for l in range(DEPTH)]
    STb = [[A(f"STb{l}_{s}", 512, BF16) for s in range(NSLOTS)] for l in range(DEPTH)]
    Scur = [[A(f"Sc{l}_{s}", 128) for s in range(NSLOTS)] for l in range(DEPTH)]
    rstd = A("rstd", TILE)
    sqt = [A(f"sqt{i}", TILE, BF16) for i in range(2)]

    DV_HBA, DV_HBX, DV_CA, DV_HCA, DV_LBH, DV_HOML = 0, 2, 4, 6, 8, 10
    DVL = 16

    ov0 = kb.off
    xpre = [A(f"xpre{i}", 4 + TILE, BF16) for i in range(3)]
    xa_c = A("xa_c", 2 * TILE, nbufs=2)
    xa_b = A("xa_b", 2 * TILE, BF16, nbufs=2)
    thr_b = A("thr_b", 2 * TILE, nbufs=2)
    thi_b = A("thi_b", 2 * TILE, nbufs=2)
    xs_f = A("xs_f", 4 * TILE, nbufs=4)
    B_b = A("B_b", 2 * TILE, BF16, nbufs=2)
    C_b = A("C_b", 2 * TILE, BF16, nbufs=2)
    xg = A("xg", 2 * TILE, nbufs=2)
    q_s = A("q_s", 2 * TILE, nbufs=2)
    f_f = A("f_f", 2 * TILE, nbufs=2)
    kk = A("kk", 2 * TILE, nbufs=2)
    lf = A("lf", 2 * TILE, nbufs=2)
    bcm = A("bcm", 2 * TILE, nbufs=2)
    tmpA = [A(f"tmpA{i}", 2 * TILE, nbufs=2) for i in range(4)]
    qk6 = [A(f"qk{i}", 2 * TILE, BF16, nbufs=2) for i in range(6)]
    ebl = A("ebl", 2 * 8)
    zs = A("zs", 4 * 512)
    v_tm = A("v_tm", 4 * 256, BF16)
    v_msk = A("v_msk", 2 * 256, BF16)
    gs = A("gs", 4 * 256)
    dtp = A("dtp", 32)
    dtv = A("dtv", 32)
    da = A("da", 8)
    dtt = A("dtt", 8)
    eat = A("eat", 16)
    elast = A("elast", 8)
    xs_tm = A("xs_tm", 512)
    B_tm = A("B_tm", 256, BF16)
    kh_tm = A("kh_tm", 256, BF16)
    Lm = A("Lm", 1024)
    Em = A("Em", 1024)
    cbm = A("cbm", 256)
    Mm = A("Mm", 1024, BF16)
    xdt = A("xdt", 512, BF16)
    xdtt = A("xdtt", 512, BF16)
    y1 = A("y1", 512)
    y2s = [A("y2a", 512), A("y2b", 512)]
    y3 = A("y3", 512)
    junk = A("junk", 768, BF16)
    ssS = A("ssS", 2)
    rsS = A("rsS", 2)
    ssH = A("ssH", 2)
    rsH = A("rsH", 2)
    att_m = A("att_m", 512, BF16)
    Sbf = A("Sbf", 8 * 128, BF16)
    oc = A("oc", 256)
    ov1 = kb.off
    kb.off = ov0
    act_b = A("act_b", 22 * TILE, BF16, nbufs=22)
    sil = [A(f"sil{i}", TILE) for i in range(2)]
    y_out = A("y_out", 8 * TILE, nbufs=8)
    pre32 = [A(f"pre32_{i}", WB) for i in range(4)]
    pre16 = [A(f"pre16_{i}", WB, BF16) for i in range(4)]
    kb.off = max(kb.off, ov1)
    print(f"[kernel] SBUF bytes/partition used: {kb.maxoff} (overlay1 {ov1 - ov0})")

    kb.dma("sp", pfm, pfm_d)
    kb.dma("sp", pbc, V(pbc_d.ap[0:1, :].partition_broadcast(128) if False else pbc_d.ap.broadcast_to([128, PB_TOT]), []))
    wab32 = xa_c[:, 0:DEPTH * 4 * 128]
    kb.dma("sp", wab32, wab_d)
    wdt32 = xg[:, 0:DEPTH * 64]
    kb.dma("sp", wdt32, wdt_d)
    kb.copy("dve", wab, wab32)
    kb.copy("dve", wdt, wdt32)
    kb.memset("pool", ONF, 1.0)
    kb.memset("pool", ONB, 1.0)
    kb.memset("pool", RMK, 1.0)
    kb.memset("pool", RMK.re("p (j c) -> p j c", c=64)[:, :, 0:1], 0.0)
    pg.add("pool", lambda e: e.affine_select(U.ap, ONF.ap, [[1, 128]], ALU.is_ge, 0.0, base=0, channel_multiplier=-1),
           reads=ONF.bufs, writes=U.bufs)
    pg.add("pool", lambda e: e.affine_select(SL.ap, ONF.ap, [[-1, 128]], ALU.is_ge, 0.0, base=-1, channel_multiplier=1),
           reads=ONF.bufs, writes=SL.bufs)
    pg.add("pool", lambda e: e.affine_select(IDN.ap, ONF.ap, [[1, 128]], ALU.is_equal, 0.0, base=0, channel_multiplier=-1),
           reads=ONF.bufs, writes=IDN.bufs)
    kb.copy("pool", IDB, IDN)
    kb.copy("pool", MBD, U)
    kb.memset("pool", MBD[0:64, 64:128], 0.0)
    kb.memset("pool", rowm, 1.0)
    kb.memset("pool", rowm[64:128, 0:1], 0.0)
    kb.memset("pool", rowm[0:64, 1:2], 0.0)
    for l in range(DEPTH):
        b = DVL * l
        pl = PF_L * l
        kb.ts("dve", drv[:, b + DV_HBA:b + DV_HBA + 2], pfm[:, pl + PF_BA:pl + PF_BA + 2], 0.5, None, ALU.mult)
        kb.ts("dve", drv[:, b + DV_HBX:b + DV_HBX + 2], pfm[:, pl + PF_BX:pl + PF_BX + 2], 0.5, None, ALU.mult)
        t = drv[:, 60:62]
        kb.act(t, pfm[:, pl + PF_LAM:pl + PF_LAM + 2], AF.Exp, scale=-1.0)
        kb.act(t, t, AF.Ln, bias=1.0)
        kb.ts("dve", drv[:, b + DV_CA:b + DV_CA + 2], t, -8.0, None, ALU.mult)
        kb.ts("dve", drv[:, b + DV_HCA:b + DV_HCA + 2], t, -4.0, None, ALU.mult)
        lbv = drv[:, 62:64]
        if l == 0:
            kb.memset("dve", lbv, 0.0)
        else:
            kb.tt("dve", lbv, pfm[:, PF_LB:PF_LB + 2], pfm[:, PF_LB + 2:PF_LB + 4], ALU.subtract)
            kb.act(lbv, lbv, AF.Exp)
            kb.ts("dve", lbv, lbv, 1.0, None, ALU.add)
            kb.recip(lbv, lbv)
        kb.ts("dve", drv[:, b + DV_HOML:b + DV_HOML + 2], lbv, -0.5, 0.5, ALU.mult, ALU.add)
        kb.tt("dve", drv[:, b + DV_LBH:b + DV_LBH + 2], lbv, drv[:, b + DV_HOML:b + DV_HOML + 2], ALU.add)
        kb.act(aneg[:, 8 * l:8 * l + 8], pbc[:, PB_L * l + PB_ALOG:PB_L * l + PB_ALOG + 8], AF.Exp)
        kb.ts("dve", aneg[:, 8 * l:8 * l + 8], aneg[:, 8 * l:8 * l + 8], -1.0, None, ALU.mult)

    stage(0)
    NB_ALL = DEPTH * NBLK_L
    cast_engs = ["act", "dve"]
    NPRE = len(pre32)
    stage(1)
    def blk_size(j):
        return BLK_SIZES[j % NBLK_L]

    def wissue():
        g = kb.wissued
        kb.wissued += 1
        j = g % NB_ALL
        n = blk_size(j)
        slot = wslots[g % NSLOT]
        if g < NB_ALL:
            s32 = pre32[g % NPRE]
            kb.dma("sp", s32[:, 0:n], V(wblk.ap[j][:, 0:n], []))
            kb.copy(cast_engs[g % 2], slot[:, 0:n], s32[:, 0:n])
            kb.dma("pool", V(wscr[j].ap[:, 0:n], wscr[j].bufs), slot[:, 0:n])
        else:
            kb.dma("sp", slot[:, 0:n], V(wscr[j].ap[:, 0:n], wscr[j].bufs))

    def wget():
        g = kb.wused
        kb.wused += 1
        while kb.wissued < min(g + NSLOT - 1, kb.wtotal):
            wissue()
        return wslots[g % NSLOT]

    def rmsnorm(T, wcol0, out_is_h=True, out_v=None):
        p = kb.ps()
        for k in range(8):
            s = sqt[k % 2]
            kb.act(s[:, 0:T], xk(k, T), AF.Square)
            pg.add("pe", lambda e, k=k, s=s: e.matmul(p.ap[:, 0:T], ONB.ap, s.ap[:, 0:T], start=(k == 0), stop=(k == 7)),
                   reads=ONB.bufs + s.bufs, writes=p.bufs)
        kb.act(rstd[:, 0:T], p[:, 0:T], AF.Ln, bias=EPS, scale=1.0 / D)
        kb.act(rstd[:, 0:T], rstd[:, 0:T], AF.Exp, scale=-0.5)
        for k in range(8):
            if out_is_h:
                o = V(h_bf.ap[:, k * TILE:k * TILE + T], [h_bf.bufs[k]])
            else:
                o = V(out_v.ap[:, k * TILE:k * TILE + T], [out_v.bufs[k]])
            kb.stt(o, xk(k, T), pfm[:, wcol0 + k:wcol0 + k + 1], rstd[:, 0:T], ALU.mult, ALU.mult)

    def xk(k, T):
        return V(x_fm.ap[:, k * TILE:k * TILE + T], [x_fm.bufs[k]])

    def hk(k, c0, n):
        return V(h_bf.ap[:, k * TILE + c0:k * TILE + c0 + n], [h_bf.bufs[k]])

    def mixw(k, c0, n):
        return hk(k, c0, n)

    evq = [0]

    def ev_eng():
        evq[0] += 1
        return "act" if evq[0] % 2 == 0 else "dve"

    class BankPool:
        def __init__(self, banks):
            self.banks = banks
            self.i = 0

        def get(self, n=1):
            b = self.banks[self.i % len(self.banks)]
            self.i += 1
            return V(kb.psum[:, b * 512:(b + n) * 512], kb.pbufs[b:b + n])

    def interleave(*gens):
        active = [g for g in gens if g is not None]
        while active:
            for g in list(active):
                if g not in active:
                    continue
                try:
                    tok = next(g)
                except StopIteration:
                    active.remove(g)
                    continue
                if tok == "drain":
                    for o in list(active):
                        if o is not g:
                            for _ in o:
                                pass
                            active.remove(o)
                yield

    def pipelined(n, front, back):
        prev = None
        for b in range(n):
            f = front(b)
            if prev is None:
                yield from f
            else:
                yield from interleave(f, prev)
            prev = back(b)
        yield from prev

    def chunkv(v, c, T):
        return V(v.ap[:, c * TILE:c * TILE + T], [v.bufs[c]])

    def layer(l, cfg, slots):
        T, P, NB, NSEG, L, CH = cfg.T, cfg.P, cfg.NB, cfg.NSEG, cfg.L, cfg.CH
        pl = PF_L * l
        pb = PB_L * l
        dv = DVL * l
        NCH = cfg.NCH

        def seg3(v):
            return v.re("p (s l) -> p s l", s=NSEG)

        rmsnorm(T, pl + PF_N1)

        stage(2)
        conv_i = [0]

        def a_post(c):
            cs = slice(c * TILE, c * TILE + T)
            thr, thi = chunkv(thr_b, c, T), chunkv(thi_b, c, T)
            av, sv = chunkv(tmpA[2], c, T), chunkv(tmpA[3], c, T)
            cA = drv[:, dv + DV_CA + c:dv + DV_CA + c + 1]
            hcA = drv[:, dv + DV_HCA + c:dv + DV_HCA + c + 1]
            kb.act(av, thr, AF.Exp, bias=hcA, scale=hcA)
            kb.act(sv, thr, AF.Exp, bias=cA, scale=cA)
            kb.stt(thi, thi, 1.0, chunkv(xa_c, c, T), ALU.add, ALU.mult)
            yield
            kb.act(sv, sv, AF.Ln, bias=1.0, scale=-1.0)
            kb.act(sv, sv, AF.Exp, scale=0.5)
            yield
            kb.stt(thi, thi, 0.5, sv, ALU.mult, ALU.mult)
            yield
            for s in range(NSEG):
                ss_ = slice(c * TILE + s * L, c * TILE + s * L + L)
                hst = hAst[l][slots[s]][:, c:c + 1]
                kb.scan(V(thr_b.ap[:, ss_], [thr_b.bufs[c]]), V(tmpA[2].ap[:, ss_], [tmpA[2].bufs[c]]),
                        V(thi_b.ap[:, ss_], [thi_b.bufs[c]]), hst)
                kb.copy("act", hst, V(thr_b.ap[:, c * TILE + s * L + L - 1:c * TILE + s * L + L], [thr_b.bufs[c]]))
            yield

        def a_pre():
            for c in range(2):
                pr = kb.ps()
                pi = kb.ps()
                kb.mm(pr[:, 0:T], [(wab[:, (l * 4 + 0 + c) * 128:(l * 4 + 0 + c + 1) * 128], chunkv(xa_b, c, T))])
                kb.mm(pi[:, 0:T], [(wab[:, (l * 4 + 2 + c) * 128:(l * 4 + 2 + c + 1) * 128], chunkv(xa_b, c, T))])
                kb.act(chunkv(thr_b, c, T), pr[:, 0:T], AF.Tanh, bias=drv[:, dv + DV_HBA + c:dv + DV_HBA + c + 1], scale=0.5)
                kb.act(chunkv(thi_b, c, T), pi[:, 0:T], AF.Tanh, bias=drv[:, dv + DV_HBX + c:dv + DV_HBX + c + 1], scale=0.5)
            G = [chunkv(lf, c, T) for c in range(2)]
            X = [chunkv(xg, c, T) for c in range(2)]
            for c in range(2):
                kb.act(G[c], X[c], AF.Square)
            for c in range(2):
                kb.ts("dve", G[c], G[c], 0.044715, 1.0, ALU.mult, ALU.add)
            for c in range(2):
                kb.tt("dve", G[c], G[c], X[c], ALU.mult)
            for c in range(2):
                kb.act(G[c], G[c], AF.Tanh, scale=0.7978845608028654)
            for c in range(2):
                kb.stt(X[c], G[c], 1.0, X[c], ALU.add, ALU.mult)

        def proj_chunk(wv, off):
            p = kb.ps()
            pv = p[:, 0:T]
            kb.mm(pv, [(wv[:, off + kc * 128:off + (kc + 1) * 128], hk(kc, 0, T)) for kc in range(8)])
            return pv

        cbias = lambda ci: pfm[:, pl + PF_CB + ci:pl + PF_CB + ci + 1]

        def conv_evac(ci, pv):
            xp = xpre[conv_i[0] % 3]
            conv_i[0] += 1
            W = 4 + L
            xpv = xp[:, 0:NSEG * W].re("p (s w) -> p s w", s=NSEG)
            for s in range(NSEG):
                kb.copy("pool", xp[:, s * W:s * W + 3], convst[l][slots[s]][:, ci * 3:ci * 3 + 3])
            kb.copy("act", xpv[:, :, 3:3 + L], seg3(pv))
            for s in range(NSEG):
                kb.copy("dve", convst[l][slots[s]][:, ci * 3:ci * 3 + 3], pv[:, s * L + L - 3:s * L + L])
            return xp

        def conv_mm(wv, xp):
            W = 4 + L
            p2 = kb.ps()
            for s in range(NSEG):
                kb.mm(p2[:, s * L:(s + 1) * L],
                      [(wv[:, 1024 + k * 128:1024 + (k + 1) * 128], xp[:, s * W + k:s * W + k + L]) for k in range(4)])
            return p2[:, 0:T]

        units = []

        def mk_conv(ci, fin):
            st = {}

            def A_():
                st["wv"] = wget()
                st["xp"] = conv_evac(ci, proj_chunk(st["wv"], 0))

            def B_():
                fin(conv_mm(st["wv"], st["xp"]))
            return (A_, B_)

        def fin_xa(c):
            def f_(pc):
                kb.act(chunkv(xa_c, c, T), pc, AF.Identity, bias=cbias(c))
                kb.act(chunkv(xa_b, c, T), pc, AF.Identity, bias=cbias(c))
            return f_

        def fin_xbc(j):
            def f_(pc):
                if j < 4:
                    dst = V(xs_f.ap[:, j * TILE:j * TILE + T], [xs_f.bufs[j]])
                elif j < 6:
                    dst = chunkv(B_b, j - 4, T)
                else:
                    dst = chunkv(C_b, j - 6, T)
                kb.act(dst, pc, AF.Silu, bias=cbias(2 + j))
            return f_

        def mk_pair(kind):
            def A_():
                wv = wget()
                for j in range(2):
                    pv = proj_chunk(wv, j * 1024)
                    if kind == "ga":
                        kb.copy("act", chunkv(xg, j, T), pv)
                    elif kind == "q":
                        kb.act(chunkv(q_s, j, T), pv, AF.Silu)
                    else:
                        kb.act(chunkv(f_f, j, T), pv, AF.Tanh, scale=0.5)

            def B_():
                if kind == "ga":
                    a_pre_flag[0] = True
            return (A_, B_)

        a_pre_flag = [False]
        units.append(mk_conv(0, fin_xa(0)))
        units.append(mk_conv(1, fin_xa(1)))
        units.append(mk_pair("ga"))
        for j in range(8):
            units.append(mk_conv(2 + j, fin_xbc(j)))
        units.append(mk_pair("q"))
        units.append(mk_pair("f"))
        did_pre = False
        for i in range(len(units) + 1):
            if i < len(units):
                units[i][0]()
            if i > 0:
                units[i - 1][1]()
            if a_pre_flag[0] and not did_pre and i >= 4:
                a_pre()
                for _ in interleave(a_post(0), a_post(1)):
                    pass
                did_pre = True

        stage(4)
        pSF = BankPool([2, 3])
        pSD = BankPool([0])
        pTR = BankPool([4])
        pHF = BankPool([5, 6])
        pHO = BankPool([7])

        def fl(v):
            return V(v.ap.rearrange("p (c t) -> p c t", c=2)[:, :, 0:T], v.bufs)

        def ch4(v):
            return V(v.ap.rearrange("p (c t) -> p c t", c=2)[:, :, 0:T].rearrange("p c (j i) -> p c j i", i=CH), v.bufs)

        qt_e, qt_o, qh_e, qh_o, kt, kh = qk6
        d1, r1, d2, e3 = tmpA[0], tmpA[1], tmpA[2], tmpA[3]

        def c_prep():
            for c in range(2):
                kb.ts("dve", chunkv(f_f, c, T), chunkv(f_f, c, T), drv[:, dv + DV_HOML + c:dv + DV_HOML + c + 1],
                      drv[:, dv + DV_LBH + c:dv + DV_LBH + c + 1], ALU.mult, ALU.add)
                kb.ts("pool", chunkv(kk, c, T), chunkv(f_f, c, T), -1.0, 1.0, ALU.mult, ALU.add)
            for c in range(2):
                kb.act(chunkv(lf, c, T), chunkv(f_f, c, T), AF.Ln)
                for s in range(NSEG):
                    ss_ = slice(c * TILE + s * L, c * TILE + s * L + L)
                    kb.scan(V(bcm.ap[:, ss_], [bcm.bufs[c]]), RMK[:, 0:L], V(lf.ap[:, ss_], [lf.bufs[c]]), 0.0)
                yield
            b4 = ch4(bcm)
            kb.tt("dve", ch4(d1), b4, b4[:, :, :, CH // 2 - 1:CH // 2].bc([128, 2, NCH, CH]), ALU.subtract)
            kb.tt("pool", ch4(d2), b4[:, :, :, CH - 1:CH].bc([128, 2, NCH, CH]), b4, ALU.subtract)
            yield
            kb.act(fl(r1), fl(d1), AF.Exp, scale=-1.0)
            kb.act(fl(d1), fl(d1), AF.Exp)
            yield
            kb.act(fl(e3), fl(bcm), AF.Exp)
            kb.act(V(ebl.ap.rearrange("p (c j) -> p c j", c=2)[:, :, 0:NCH], ebl.bufs),
                   V(b4.ap[:, :, :, CH - 1], b4.bufs), AF.Exp)
            kb.tt("dve", fl(qt_o), fl(q_s), fl(d1), ALU.mult)
            yield
            kb.act(fl(d2), fl(d2), AF.Exp)
            kb.tt("dve", fl(qh_o), fl(q_s), fl(e3), ALU.mult)
            kb.tt("pool", fl(kt), fl(kk), fl(r1), ALU.mult)
            yield
            kb.act(fl(qt_e), fl(qt_o), AF.Copy, scale=rowm[:, 0:1])
            kb.act(fl(qt_o), fl(qt_o), AF.Copy, scale=rowm[:, 1:2])
            kb.tt("dve", fl(kh), fl(kk), fl(d2), ALU.mult)
            yield
            kb.act(fl(qh_e), fl(qh_o), AF.Copy, scale=rowm[:, 0:1])
            kb.act(fl(qh_o), fl(qh_o), AF.Copy, scale=rowm[:, 1:2])
            yield

        def ya_late(c):
            kb.stt(mixw(c, 0, T), chunkv(thr_b, c, T), 0.5, chunkv(xg, c, T), ALU.mult, ALU.mult)

        def ssd_front(b):
            s = b // cfg.BPS
            slot = slots[s]
            c0 = b * P
            y2 = y2s[b % 2]
            if b == 0:
                kb.act(dtv[0:P, 0:NB * 8], dtp[0:P, 0:NB * 8], AF.Exp)
                kb.act(dtv[0:P, 0:NB * 8], dtv[0:P, 0:NB * 8], AF.Ln, bias=1.0)
            dtb_ = dtv[0:P, b * 8:b * 8 + 8]
            kb.tt("dve", da[0:P, :], dtb_, aneg[0:P, 8 * l:8 * l + 8], ALU.mult)
            L3 = Lm[0:P, 0:8 * P].re("p (h s) -> p h s", h=8)
            kb.tt("pool", L3, SL[0:P, 0:P].un(1).bc([P, 8, P]), da[0:P, :].un(2).bc([P, 8, P]), ALU.mult)
            pT = pSF.get()
            kb.trs([(pT[0:P, j * 128:(j + 1) * 128], V(xs_f.ap[:, j * TILE + c0:j * TILE + c0 + P], [xs_f.bufs[j]]), IDN)
                    for j in range(4)])
            kb.copy("act", xs_tm[0:P, :], pT[0:P, 0:512])
            yield
            pD = pSD.get(2)
            kb.mms([(pD[0:P, h * P:(h + 1) * P], Lm[0:P, h * P:(h + 1) * P], U[0:P, 0:P], True, True) for h in range(8)])
            pS = pSF.get()
            kb.mms([(pS[0:P, 0:8], U[0:P, 0:P], da[0:P, :], True, True),
                    (pS[0:P, 8:16], SL[0:P, 0:P], da[0:P, :], True, True),
                    (pS[:, 16:24], ONF[0:P, :], da[0:P, :], True, True)])
            kb.act(eat[0:P, :], pS[0:P, 0:16], AF.Exp)
            kb.act(elast, pS[:, 16:24], AF.Exp)
            yield
            pT2 = pSF.get()
            pT2b = pT2.cast(BF16)
            kb.trs([(pT2b[0:P, g * 128:(g + 1) * 128], chunkv(B_b, g, T)[:, c0:c0 + P], IDB) for g in range(2)])
            kb.copy("act", B_tm[0:P, :], pT2b[0:P, 0:256])
            kb.act(Em[0:P, 0:8 * P], pD[0:P, 0:8 * P], AF.Exp)
            kb.tt("dve", dtt[0:P, :], dtb_, eat[0:P, 8:16], ALU.mult)
            yield
            pC = pSF.get()
            kb.mms([(pC[0:P, g * P:(g + 1) * P], chunkv(B_b, g, T)[:, c0:c0 + P], chunkv(C_b, g, T)[:, c0:c0 + P], True, True)
                    for g in range(2)])
            kb.tt("dve", cbm[0:P, 0:2 * P].re("p (g t) -> p g t", g=2), pC[0:P, 0:2 * P].re("p (g t) -> p g t", g=2),
                  U[0:P, 0:P].un(1).bc([P, 2, P]), ALU.mult)
            x3 = xs_tm[0:P, :].re("p (h d) -> p h d", h=8)
            kb.tt("dve", xdt[0:P, :].re("p (h d) -> p h d", h=8), x3, dtb_.un(2).bc([P, 8, 64]), ALU.mult)
            kb.tt("pool", xdtt[0:P, :].re("p (h d) -> p h d", h=8), x3, dtt[0:P, :].un(2).bc([P, 8, 64]), ALU.mult)
            yield
            kb.tt("dve", Mm[0:P, 0:8 * P].re("p (g h t) -> p g h t", g=2, h=4), Em[0:P, 0:8 * P].re("p (g h t) -> p g h t", g=2, h=4),
                  cbm[0:P, 0:2 * P].re("p (g t) -> p g t", g=2).un(2).bc([P, 2, 4, P]), ALU.mult)
            kb.tt("pool", y2[0:P, :].re("p (h d) -> p h d", h=8), x3, pbc[0:P, pb + PB_DSK:pb + PB_DSK + 8].un(2).bc([P, 8, 64]), ALU.mult)
            yield
            stage(7)
            pI = pSF.get()
            kb.mms([(pI[0:P, g * 256:(g + 1) * 256], chunkv(C_b, g, T)[:, c0:c0 + P], STb[l][slot][:, g * 256:(g + 1) * 256], True, True)
                    for g in range(2)])
            kb.tt("dve", y1[0:P, :].re("p (h d) -> p h d", h=8), pI[0:P, 0:512].re("p (h d) -> p h d", h=8),
                  eat[0:P, 0:8].un(2).bc([P, 8, 64]), ALU.mult)
            pY = pSF.get()
            kb.mms([(pY[0:P, h * 64:(h + 1) * 64], Mm[0:P, h * P:(h + 1) * P], xdt[0:P, h * 64:(h + 1) * 64], True, True) for h in range(8)])
            yield
            kb.tt("dve", y1[0:P, :], y1[0:P, :], pY[0:P, 0:512], ALU.add)
            pN = pSF.get()
            kb.mms([(pN[:, g * 256:(g + 1) * 256], B_tm[0:P, g * 128:(g + 1) * 128], xdtt[0:P, g * 256:(g + 1) * 256], True, True)
                    for g in range(2)])
            st = STf[l][slot]
            kb.tt("dve", st.re("p (h d) -> p h d", h=8), st.re("p (h d) -> p h d", h=8), elast.un(2).bc([128, 8, 64]), ALU.mult)
            yield
            kb.tt("dve", st, st, pN[:, 0:512], ALU.add)
            kb.copy("act", STb[l][slot], st)
            kb.tt("pool", y2[0:P, :], y2[0:P, :], y1[0:P, :], ALU.add)
            kb.tt("pool", y2[0:P, :], y2[0:P, :], zs[0:P, b * 512:(b + 1) * 512], ALU.mult)
            yield

        def ssd_back(b):
            c0 = b * P
            y2 = y2s[b % 2]
            for g in range(2):
                kb.act(junk[0:P, g * 256:(g + 1) * 256], y2[0:P, g * 256:(g + 1) * 256], AF.Square, accum=ssS[0:P, g:g + 1])
            kb.act(rsS[0:P, 0:2], ssS[0:P, 0:2], AF.Ln, bias=EPS, scale=1.0 / 256)
            kb.act(rsS[0:P, 0:2], rsS[0:P, 0:2], AF.Exp, scale=-0.5)
            yield
            kb.tt("dve", y3[0:P, :].re("p (g d) -> p g d", g=2), y2[0:P, :].re("p (g d) -> p g d", g=2),
                  rsS[0:P, 0:2].un(2).bc([P, 2, 256]), ALU.mult)
            yield
            stage(9)
            pM = pTR.get()
            kb.trs([(pM[:, j * P:(j + 1) * P], y3[0:P, j * 128:(j + 1) * 128], IDN[0:P, 0:P]) for j in range(4)])
            for j in range(4):
                kb.act(mixw(2 + j, c0, P), pM[:, j * P:(j + 1) * P], AF.Copy, scale=pfm[:, pl + PF_SNW + j:pl + PF_SNW + j + 1])
            yield

        def hg_front(b):
            s = b // cfg.BPS
            slot = slots[s]
            c0 = b * P
            stage(8)
            pT2 = pHF.get()
            pT2b = pT2.cast(BF16)
            kb.trs([(pT2b[0:P, c * 128:(c + 1) * 128], chunkv(kh, c, T)[:, c0:c0 + P], IDB) for c in range(2)])
            kb.copy("act", kh_tm[0:P, :], pT2b[0:P, 0:256])
            yield
            pA = pHF.get()
            items = []
            for h in range(4):
                hp, c = h % 2, h // 2
                qt_ = qt_e if hp == 0 else qt_o
                items.append((pA[0:P, (hp * 2 + c) * P:(hp * 2 + c + 1) * P], chunkv(kt, c, T)[:, c0:c0 + P],
                              chunkv(qt_, c, T)[:, c0:c0 + P], True, True))
            kb.mms(items)
            kb.tt("dve", att_m[0:P, 0:4 * P].re("p (q t) -> p q t", q=4), pA[0:P, 0:4 * P].re("p (q t) -> p q t", q=4),
                  MBD[0:P, 0:P].un(1).bc([P, 4, P]), ALU.mult)
            yield
            sc = Scur[l][slot]
            pDs = pHF.get()
            for jj in range(cfg.NCB):
                if cfg.NCB == 1:
                    vm = v_tm[0:P, b * 256:(b + 1) * 256]
                else:
                    vm = v_msk[0:P, jj * 256:(jj + 1) * 256]
                    kb.act(vm, v_tm[0:P, b * 256:(b + 1) * 256], AF.Copy, scale=rowm[0:P, jj:jj + 1])
                kb.mms([(pDs[hp * 64:(hp + 1) * 64, (jj * 2 + c) * 64:(jj * 2 + c + 1) * 64],
                         kh_tm[0:P, c * 128 + hp * 64:c * 128 + (hp + 1) * 64],
                         vm[:, c * 128 + hp * 64:c * 128 + (hp + 1) * 64], True, True) for c in range(2) for hp in range(2)])
            yield
            for jj in range(cfg.NCB):
                j = b * cfg.NCB + jj
                kb.copy("act", Sbf[:, jj * 128:(jj + 1) * 128], sc)
                for c in range(2):
                    kb.stt(sc[:, c * 64:(c + 1) * 64], sc[:, c * 64:(c + 1) * 64], ebl[:, c * 8 + j:c * 8 + j + 1],
                           pDs[:, (jj * 2 + c) * 64:(jj * 2 + c + 1) * 64], ALU.mult, ALU.add)
                yield
            if b > 0:
                yield "drain"
            pO = pHO.get()
            for h in range(4):
                hp, c = h % 2, h // 2
                qh_ = qh_e if hp == 0 else qh_o
                items = [(pO[0:P, h * 64:(h + 1) * 64], att_m[0:P, (hp * 2 + c) * P:(hp * 2 + c + 1) * P],
                          v_tm[0:P, b * 256 + h * 64:b * 256 + (h + 1) * 64], True, False)]
                for jj in range(cfg.NCB):
                    items.append((pO[jj * CH:(jj + 1) * CH, h * 64:(h + 1) * 64],
                                  chunkv(qh_, c, T)[:, c0 + jj * CH:c0 + (jj + 1) * CH],
                                  Sbf[:, jj * 128 + c * 64:jj * 128 + (c + 1) * 64], False, jj == cfg.NCB - 1))
                kb.mms(items)
            hg_po[0] = pO
            yield

        hg_po = [None]

        def hg_back(b):
            c0 = b * P
            pO = hg_po[0]
            kb.act(junk[0:P, 512:768], pO[0:P, 0:256], AF.Square, accum=ssH[0:P, 0:1])
            kb.act(rsH[0:P, 0:1], ssH[0:P, 0:1], AF.Ln, bias=EPS, scale=1.0 / 256)
            kb.act(rsH[0:P, 0:1], rsH[0:P, 0:1], AF.Exp, scale=-0.5)
            yield
            kb.act(oc[0:P, :], pO[0:P, 0:256], AF.Copy, scale=rsH[0:P, 0:1])
            kb.tt("pool", oc[0:P, :], oc[0:P, :], gs[0:P, b * 256:(b + 1) * 256], ALU.mult)
            yield
            pM2 = pTR.get()
            kb.trs([(pM2[:, j * P:(j + 1) * P], oc[0:P, j * 128:(j + 1) * 128], IDN[0:P, 0:P]) for j in range(2)])
            for j in range(2):
                kb.act(mixw(6 + j, c0, P), pM2[:, j * P:(j + 1) * P], AF.Copy, scale=pfm[:, pl + PF_HNW + j:pl + PF_HNW + j + 1])
            yield

        stage(3)
        cg = c_prep()
        wz = [wget(), wget()]
        for b in range(NB):
            p = kb.ps()
            kb.mm(p[0:P, 0:512], [(hk(kc, b * P, P), wz[kc // 4][:, (kc % 4) * 512:(kc % 4 + 1) * 512]) for kc in range(8)])
            kb.act(zs[0:P, b * 512:(b + 1) * 512], p[0:P, 0:512], AF.Silu)
        stage(31)
        for _ in cg:
            pass
        wg = [wget(), wget()]
        for b in range(NB):
            p = kb.ps()
            kb.mm(p[0:P, 0:512], [(hk(kc, b * P, P), wg[kc // 4][:, (kc % 4) * 512:(kc % 4 + 1) * 512]) for kc in range(8)])
            kb.copy("dve", v_tm[0:P, b * 256:(b + 1) * 256], p[0:P, 0:256])
            kb.act(gs[0:P, b * 256:(b + 1) * 256], p[0:P, 256:512], AF.Silu)
        stage(32)
        for b in range(NB):
            p = kb.ps()
            kb.mm(p[0:P, 0:8], [(hk(kc, b * P, P), wdt[:, l * 64 + kc * 8:l * 64 + kc * 8 + 8]) for kc in range(8)])
            kb.tt("dve", dtp[0:P, b * 8:b * 8 + 8], p[0:P, 0:8], pbc[0:P, pb + PB_DTB:pb + PB_DTB + 8], ALU.add)
        for _ in cg:
            pass

        def chain_x():
            ya_late(0)
            ya_late(1)
            yield
            yield from pipelined(NB, hg_front, hg_back)

        def chain_y():
            yield from pipelined(NB, ssd_front, ssd_back)

        for _ in interleave(chain_y(), chain_x()):
            pass

        stage(10)
        for bi in range(4):
            wv = wget()
            for j in range(2):
                dc = 2 * bi + j
                p = kb.ps()
                kb.mm(p[:, 0:T], [(wv[:, (j * 8 + ec) * 128:(j * 8 + ec + 1) * 128], mixw(ec, 0, T)) for ec in range(8)])
                kb.tt("dve", xk(dc, T), xk(dc, T), p[:, 0:T], ALU.add)

        stage(11)
        rmsnorm(T, pl + PF_N2)
        for fc in range(22):
            wv = wget()
            pg_ = kb.ps()
            pu_ = kb.ps()
            kb.mm(pg_[:, 0:T], [(wv[:, kc * 128:(kc + 1) * 128], hk(kc, 0, T)) for kc in range(8)])
            kb.mm(pu_[:, 0:T], [(wv[:, 1024 + kc * 128:1024 + (kc + 1) * 128], hk(kc, 0, T)) for kc in range(8)])
            s_ = sil[fc % 2]
            kb.act(s_[:, 0:T], pg_[:, 0:T], AF.Silu)
            kb.tt("dve", V(act_b.ap[:, fc * TILE:fc * TILE + T], [act_b.bufs[fc]]), s_[:, 0:T], pu_[:, 0:T], ALU.mult)
        for dc in range(8):
            w0 = wget()
            w1 = wget()
            p = kb.ps()
            pairs = []
            for fc in range(22):
                wv = w0 if fc < 11 else w1
                o = (fc % 11) * 128
                pairs.append((wv[:, o:o + 128], V(act_b.ap[:, fc * TILE:fc * TILE + T], [act_b.bufs[fc]])))
            kb.mm(p[:, 0:T], pairs)
            kb.tt("dve", xk(dc, T), xk(dc, T), p[:, 0:T], ALU.add)
    def zero_states():
        for l in range(DEPTH):
            kb.memset("pool", convst[l][0], 0.0)
            kb.memset("pool", hAst[l][0], 0.0)
            kb.memset("pool", STf[l][0], 0.0)
            kb.memset("pool", STb[l][0], 0.0)
            kb.memset("pool", Scur[l][0], 0.0)

    def store_states(seq_i, slot):
        for l in range(DEPTH):
            base = (seq_i * DEPTH + l)
            kb.dma("pool", o_conv[:, base * 30:(base + 1) * 30], convst[l][slot])
            kb.dma("pool", o_hA[:, base * 2:(base + 1) * 2], hAst[l][slot])
            kb.dma("pool", o_ssd[:, base * 512:(base + 1) * 512], STf[l][slot])
            kb.dma("pool", o_hg[:, base * 128:(base + 1) * 128], Scur[l][slot])

    def load_states(si, slot):
        for l in range(DEPTH):
            base = (l * nsamp + si)
            kb.dma("pool", convst[l][slot], sconv_d[:, base * 30:(base + 1) * 30])
            kb.dma("pool", hAst[l][slot], shA_d[:, base * 2:(base + 1) * 2])
            kb.dma("pool", STf[l][slot], sssd_d[:, base * 512:(base + 1) * 512])
            kb.copy("pool", STb[l][slot], STf[l][slot])
            kb.dma("pool", Scur[l][slot], shg_d[:, base * 128:(base + 1) * 128])

    def load_x(cfg, x_src):
        T = cfg.T
        for k in range(8):
            kb.dma("pool", xk(k, T), V(x_src.ap[:, k, :], []))

    def run_tile(cfg, y_dst, slots, nxt=None):
        T = cfg.T
        for l in range(DEPTH):
            layer(l, cfg, slots)
        rmsnorm(T, PF_FN, out_is_h=False, out_v=y_out)
        if nxt is not None:
            load_x(*nxt)
        for k in range(8):
            kb.dma("pool", V(y_dst.ap[:, k, :], y_dst.bufs), V(y_out.ap[:, k * TILE:k * TILE + T], [y_out.bufs[k]]))

    cfgP = TileCfg(TILE, 128, 1, TILE, 64)
    xTp3 = xT_p.ap.rearrange("(k p) t -> p k t", p=128)
    yTp3 = yT_p.ap.rearrange("(k p) t -> p k t", p=128)
    cfgS = TileCfg(TS, lsamp, nsamp, lsamp, lsamp) if nsamp > 0 else None
    xTs3 = xT_s.ap.rearrange("(k p) t -> p k t", p=128)
    yTs3 = yT_s.ap.rearrange("(k p) t -> p k t", p=128)

    def xsrc(ti):
        return V(xTp3[:, :, ti * TILE:(ti + 1) * TILE], [])

    assert nsamp > 0, "the first (sample) tile carries the weight conversion"
    load_x(cfgS, V(xTs3, []))
    for si in range(nsamp):
        load_states(si, si)
    run_tile(cfgS, V(yTs3, yT_s.bufs), list(range(nsamp)), (cfgP, xsrc(0)) if NT > 0 else None)
    for si in range(nsamp):
        store_states(1 + si, si)
    if NT > 0:
        zero_states()
    for ti in range(NT):
        nxt = (cfgP, xsrc(ti + 1)) if ti + 1 < NT else None
        run_tile(cfgP, V(yTp3[:, :, ti * TILE:(ti + 1) * TILE], yT_p.bufs), [0], nxt)
    if NT > 0:
        store_states(0, 0)

    with nc.Block() as block:
        @block.tensor
        def _(e):
            pg.replay("pe", e)

        @block.scalar
        def _(e):
            pg.replay("act", e)

        @block.vector
        def _(e):
            pg.replay("dve", e)

        @block.gpsimd
        def _(e):
            pg.replay("pool", e)

        @block.sync
        def _(e):
            pg.replay("sp", e)
            pg.finish("sp", e)
    print(f"[kernel] ops: { {k: len(v) for k, v in pg.ops.items()} } instrs~{pg.nops}")
    return nc


def _fm_chunk(W, c0):
    sub = W[:, c0:c0 + 128]
    K = sub.shape[0]
    return sub.reshape(K // 128, 128, 128).transpose(1, 0, 2).reshape(128, -1)


def _layer_blocks(w_in, w_out, w_g, w_u, w_d, cw):
    blocks = []

    def conv_blk(c0, ci):
        blk = np.zeros((128, WB), np.float32)
        blk[:, 0:1024] = _fm_chunk(w_in, c0)
        for k in range(4):
            blk[np.arange(128), 1024 + k * 128 + np.arange(128)] = cw[ci, k]
        return blk

    def pair_blk(c0):
        return np.concatenate([_fm_chunk(w_in, c0), _fm_chunk(w_in, c0 + 128)], axis=1)

    blocks.append(conv_blk(C_XA, 0))
    blocks.append(conv_blk(C_XA + 128, 1))
    blocks.append(pair_blk(C_GA))
    for j in range(8):
        blocks.append(conv_blk(C_XS + j * 128, 2 + j))
    blocks.append(pair_blk(C_Q))
    blocks.append(pair_blk(C_F))
    for c0 in (C_Z, C_I):
        r = w_in[:, c0:c0 + 512].reshape(8, 128, 512).transpose(1, 0, 2)
        blocks.append(r[:, 0:4].reshape(128, -1))
        blocks.append(r[:, 4:8].reshape(128, -1))
    for i in range(0, 8, 2):
        blocks.append(np.concatenate([_fm_chunk(w_out, i * 128), _fm_chunk(w_out, (i + 1) * 128)], axis=1))
    for fc in range(22):
        blocks.append(np.concatenate([_fm_chunk(w_g, fc * 128), _fm_chunk(w_u, fc * 128)], axis=1))
    for dc in range(8):
        r = _fm_chunk(w_d, dc * 128).reshape(128, 22, 128)
        for half in range(2):
            blk = np.zeros((128, WB), np.float32)
            blk[:, 0:1408] = r[:, half * 11:(half + 1) * 11].reshape(128, -1)
            blocks.append(blk)
    assert len(blocks) == NBLK_L
    return blocks


def _col(v, n):
    return np.asarray(v).reshape(n, 128).T


_PROG_CACHE = {}


def kernel(x_prompt, x_sample, state_rglru_conv, state_rglru_h, state_ssd_conv, state_ssd, state_hgrn,
           norm1_w, w_in, rglru_conv_w, rglru_conv_b, rglru_wa, rglru_ba, rglru_wx, rglru_bx, rglru_lambda,
           ssd_conv_w, ssd_conv_b, ssd_dt_bias, ssd_a_log, ssd_d, ssd_norm_w, hgrn_lb, hgrn_norm_w,
           w_out, norm2_w, w_ffn_gate, w_ffn_up, w_ffn_down, final_norm_w):
    f = lambda a: np.ascontiguousarray(np.asarray(a, dtype=np.float32))
    x_prompt, x_sample = f(x_prompt), f(x_sample)
    ncores, seq, _ = x_prompt.shape
    nsamp = x_sample.shape[0] // ncores
    lsamp = x_sample.shape[1]
    TS = nsamp * lsamp
    w_in, w_out, w_g, w_u, w_d = f(w_in), f(w_out), f(w_ffn_gate), f(w_ffn_up), f(w_ffn_down)
    blocks = []
    pfm = np.zeros((128, PF_TOT), np.float32)
    pbc = np.zeros((1, PB_TOT), np.float32)
    wab = np.zeros((128, DEPTH * 4 * 128), np.float32)
    wdt = np.zeros((128, DEPTH * 64), np.float32)
    rcw, rcb, scw, scb = f(rglru_conv_w), f(rglru_conv_b), f(ssd_conv_w), f(ssd_conv_b)
    for l in range(DEPTH):
        cwl = np.concatenate([rcw[l].reshape(4, 2, 128), scw[l].reshape(4, 8, 128)], axis=1).transpose(1, 0, 2)
        blocks += _layer_blocks(w_in[l], w_out[l], w_g[l], w_u[l], w_d[l], cwl)
    wblk = np.ascontiguousarray(np.stack(blocks, 0))
    wa, wx = f(rglru_wa), f(rglru_wx)
    for l in range(DEPTH):
        o = PF_L * l
        pfm[:, o + PF_N1:o + PF_N1 + 8] = _col(f(norm1_w)[l], 8)
        pfm[:, o + PF_N2:o + PF_N2 + 8] = _col(f(norm2_w)[l], 8)
        for ci in range(10):
            for k in range(4):
                src = rcw[l, k, ci * 128:(ci + 1) * 128] if ci < 2 else scw[l, k, (ci - 2) * 128:(ci - 1) * 128]
                pfm[:, o + PF_CW + ci * 4 + k] = src
            pfm[:, o + PF_CB + ci] = rcb[l, ci * 128:(ci + 1) * 128] if ci < 2 else scb[l, (ci - 2) * 128:(ci - 1) * 128]
        pfm[:, o + PF_BA:o + PF_BA + 2] = _col(f(rglru_ba)[l], 2)
        pfm[:, o + PF_BX:o + PF_BX + 2] = _col(f(rglru_bx)[l], 2)
        pfm[:, o + PF_LAM:o + PF_LAM + 2] = _col(f(rglru_lambda)[l], 2)
        pfm[:, o + PF_SNW:o + PF_SNW + 4] = _col(f(ssd_norm_w)[l], 4)
        pfm[:, o + PF_HNW:o + PF_HNW + 2] = _col(f(hgrn_norm_w)[l], 2)
        pfm[:, PF_LB + 2 * l:PF_LB + 2 * l + 2] = _col(f(hgrn_lb)[l], 2)
        ob = PB_L * l
        pbc[0, ob + PB_DTB:ob + PB_DTB + 8] = f(ssd_dt_bias)[l]
        pbc[0, ob + PB_ALOG:ob + PB_ALOG + 8] = f(ssd_a_log)[l]
        pbc[0, ob + PB_DSK:ob + PB_DSK + 8] = f(ssd_d)[l]
        pbc[0, ob + PB_SNW:ob + PB_SNW + 512] = f(ssd_norm_w)[l]
        pbc[0, ob + PB_HNW:ob + PB_HNW + 256] = f(hgrn_norm_w)[l]
        for which, W in enumerate((wa, wx)):
            for c in range(2):
                base = (l * 4 + which * 2 + c) * 128
                for hh in range(2):
                    wab[hh * 64:(hh + 1) * 64, base + hh * 64:base + (hh + 1) * 64] = W[l, 2 * c + hh]
        wdt[:, l * 64:(l + 1) * 64] = w_in[l][:, C_DT:C_DT + 8].reshape(8, 128, 8).transpose(1, 0, 2).reshape(128, 64)
    pfm[:, PF_FN:PF_FN + 8] = _col(f(final_norm_w), 8)
    s_rc, s_rh, s_sc, s_ss, s_hg = f(state_rglru_conv), f(state_rglru_h), f(state_ssd_conv), f(state_ssd), f(state_hgrn)

    in_maps = []
    for core in range(ncores):
        sconv = np.zeros((128, DEPTH, nsamp, 10, 3), np.float32)
        shA = np.zeros((128, DEPTH, nsamp, 2), np.float32)
        sssd = np.zeros((128, DEPTH, nsamp, 512), np.float32)
        shg = np.zeros((128, DEPTH, nsamp, 128), np.float32)
        for l in range(DEPTH):
            for si in range(nsamp):
                b = core * nsamp + si
                sconv[:, l, si, 0:2, :] = s_rc[l, b].reshape(3, 2, 128).transpose(2, 1, 0)
                sconv[:, l, si, 2:10, :] = s_sc[l, b].reshape(3, 8, 128).transpose(2, 1, 0)
                shA[:, l, si, :] = _col(s_rh[l, b], 2)
                sssd[:, l, si, :] = s_ss[l, b].transpose(2, 0, 1).reshape(128, 512)
                shg[:, l, si, :] = s_hg[l, b].reshape(2, 2, 64, 64).transpose(1, 2, 0, 3).reshape(128, 128)
        in_maps.append({
            "xT_p": np.ascontiguousarray(x_prompt[core].T),
            "xT_s": np.ascontiguousarray(x_sample[core * nsamp:(core + 1) * nsamp].reshape(TS, D).T),
            "wblk": wblk, "pfm": pfm, "pbc": pbc, "wab": wab, "wdt": wdt,
            "s_conv": sconv.reshape(128, -1), "s_hA": shA.reshape(128, -1),
            "s_ssd": sssd.reshape(128, -1), "s_hgrn": shg.reshape(128, -1),
        })

    with contextlib.ExitStack() as stack:
        nc = bass.Bass("TRN2", target_bir_lowering=False)
        build_program(nc, stack, seq, nsamp, lsamp)
        res = run_bass_kernel_spmd(nc, in_maps, core_ids=list(range(ncores)))
    R = res.results

    B, SB = ncores, ncores * nsamp
    y_p = np.zeros((B, seq, D), np.float32)
    y_s = np.zeros((SB, lsamp, D), np.float32)
    outs = {}
    for pre, nb in (("p", B), ("s", SB)):
        outs[pre + "_rc"] = np.zeros((DEPTH, nb, 3, 256), np.float32)
        outs[pre + "_rh"] = np.zeros((DEPTH, nb, 256), np.float32)
        outs[pre + "_sc"] = np.zeros((DEPTH, nb, 3, 1024), np.float32)
        outs[pre + "_ss"] = np.zeros((DEPTH, nb, 8, 64, 128), np.float32)
        outs[pre + "_hg"] = np.zeros((DEPTH, nb, 4, 64, 64), np.float32)
    for core in range(ncores):
        r = R[core]
        y_p[core] = r["yT_p"].T
        ys = r["yT_s"].T.reshape(nsamp, lsamp, D)
        y_s[core * nsamp:(core + 1) * nsamp] = ys
        oc_ = r["o_conv"].reshape(128, 1 + nsamp, DEPTH, 10, 3)
        oh_ = r["o_hA"].reshape(128, 1 + nsamp, DEPTH, 2)
        os_ = r["o_ssd"].reshape(128, 1 + nsamp, DEPTH, 8, 64)
        og_ = r["o_hgrn"].reshape(2, 64, 1 + nsamp, DEPTH, 2, 64)
        for qi in range(1 + nsamp):
            pre, b = ("p", core) if qi == 0 else ("s", core * nsamp + qi - 1)
            for l in range(DEPTH):
                outs[pre + "_rc"][l, b] = oc_[:, qi, l, 0:2, :].transpose(2, 1, 0).reshape(3, 256)
                outs[pre + "_sc"][l, b] = oc_[:, qi, l, 2:10, :].transpose(2, 1, 0).reshape(3, 1024)
                outs[pre + "_rh"][l, b] = oh_[:, qi, l, :].T.reshape(256)
                outs[pre + "_ss"][l, b] = os_[:, qi, l].transpose(1, 2, 0)
                outs[pre + "_hg"][l, b] = og_[:, :, qi, l].transpose(2, 0, 1, 3).reshape(4, 64, 64)
    return (y_p, y_s, outs["p_rc"], outs["p_rh"], outs["p_sc"], outs["p_ss"], outs["p_hg"],
            outs["s_rc"], outs["s_rh"], outs["s_sc"], outs["s_ss"], outs["s_hg"])
```

```python
import contextlib
import numpy as np
import concourse.bass as bass
import concourse.mybir as mybir
from concourse.bass_utils import run_bass_kernel_spmd

F32 = mybir.dt.float32
BF16 = mybir.dt.bfloat16
AF = mybir.ActivationFunctionType
ALU = mybir.AluOpType

D = 1024
DEPTH = 2
DFF = 2816
EPS = 1e-6
WB = 2048
NSLOT = 4
NBLK_L = 59
BLK_SIZES = [1536, 1536, 2048] + [1536] * 8 + [2048, 2048] + [2048] * 4 + [2048] * 4 + [2048] * 22 + [1408] * 16
TILE = 512
NCORES = 8

C_XA, C_GA, C_Z, C_XS, C_B, C_C, C_DT, C_Q, C_F, C_I, C_G = 0, 256, 512, 1024, 1536, 1792, 2048, 2056, 2312, 2568, 2824

PF_N1, PF_N2, PF_CW, PF_CB, PF_BA, PF_BX, PF_LAM = 0, 8, 16, 56, 66, 68, 70
PF_L = 72
PF_FN = 2 * PF_L
PF_LB = PF_FN + 8
PF_TOT = PF_LB + 4
PB_DTB, PB_ALOG, PB_DSK, PB_SNW, PB_HNW = 0, 8, 16, 24, 536
PB_L = 792
PB_TOT = 2 * PB_L


class Buf:
    __slots__ = ("name", "lo", "hi", "space", "writers", "readers", "alias")

    def __init__(self, name, space, lo, hi):
        self.name, self.space, self.lo, self.hi = name, space, lo, hi
        self.writers = {}
        self.readers = {}
        self.alias = []


class V:
    __slots__ = ("ap", "bufs")

    def __init__(self, ap, bufs):
        self.ap = ap
        self.bufs = list(bufs)

    def __getitem__(self, k):
        return V(self.ap[k], self.bufs)

    def re(self, pattern_, **kw):
        return V(self.ap.rearrange(pattern_, **kw), self.bufs)

    def bc(self, shape):
        return V(self.ap.broadcast_to(list(shape)), self.bufs)

    def un(self, axis):
        return V(self.ap.unsqueeze(axis), self.bufs)

    def cast(self, dt):
        return V(self.ap.bitcast(dt), self.bufs)

    def only(self, i):
        return V(self.ap, [self.bufs[i]])


EPOCH = 1000
import os as _os
_STOP = [False]


def stage(n):
    if int(_os.environ.get("KSTOP", "-1")) == n:
        _STOP[0] = True
        print(f"[kernel] debug: stopping emission at stage {n}")


NDMASEM = 24
NDMASEM_SW = 8


class Prog:
    ENGS = ("pe", "act", "dve", "pool", "sp")

    def __init__(self, nc, stack):
        self.nc = nc
        self.stack = stack
        self.ops = {e: [] for e in self.ENGS}
        self.cnt = {e: 0 for e in self.ENGS}
        self.esems = {e: [] for e in self.ENGS}
        self.seen = {e: {} for e in self.ENGS}
        self.dsems = [stack.enter_context(nc.semaphore(f"dma{i}")) for i in range(NDMASEM)]
        self.dval = [0] * NDMASEM
        self.dnext = 0
        self.dnext_sw = 0
        self.sem_of = {}
        for i, s in enumerate(self.dsems):
            self.sem_of[("d", i)] = s
        self.nops = 0

    def _esem(self, eng):
        ep = self.cnt[eng] // EPOCH
        while len(self.esems[eng]) <= ep:
            s = self.stack.enter_context(self.nc.semaphore(f"{eng}{len(self.esems[eng])}"))
            self.sem_of[(eng, len(self.esems[eng]))] = s
            self.esems[eng].append(s)
        return (eng, ep), self.cnt[eng] % EPOCH + 1

    def add(self, eng, emits, reads=(), writes=(), dma=False):
        if _STOP[0]:
            return
        if not isinstance(emits, (list, tuple)):
            emits = [emits]
        preads = [b for b in reads if b.space == "psum"]
        if preads:
            reads = [b for b in reads if b.space != "psum"]
            writes = list(writes) + [b for b in preads if b not in writes]
        need = {}

        def want(k, v):
            if need.get(k, 0) < v:
                need[k] = v

        for b in reads:
            for bb in [b] + b.alias:
                for k, v in bb.writers.items():
                    want(k, v)
        for b in writes:
            for bb in [b] + b.alias:
                for k, v in bb.writers.items():
                    want(k, v)
                for k, v in bb.readers.items():
                    want(k, v)
        if dma:
            if eng == "pool":
                j = self.dnext_sw % NDMASEM_SW
                self.dnext_sw += 1
            else:
                j = NDMASEM_SW + self.dnext % (NDMASEM - NDMASEM_SW)
                self.dnext += 1
            key = ("d", j)
            if self.dval[j] > 0:
                want(key, self.dval[j])
            self.dval[j] += 16
            val = self.dval[j]
            inc = (key, 16)
        else:
            key, val = self._esem(eng)
            self.cnt[eng] += 1
            inc = (key, 1)
        waits = []
        seen = self.seen[eng]
        for k, v in need.items():
            if k[0] == "pe" and eng == "pe":
                continue
            if seen.get(k, 0) >= v:
                continue
            seen[k] = v
            waits.append((k, v))
        self.ops[eng].append((waits, emits, inc))
        for b in writes:
            b.writers = {key: val}
            b.readers = {}
        for b in reads:
            if b.readers.get(key, 0) < val:
                b.readers[key] = val
        self.nops += len(emits)

    def replay(self, eng, e):
        for waits, emits, inc in self.ops[eng]:
            for k, v in waits:
                e.wait_ge(self.sem_of[k], v)
            last = None
            for f in emits:
                last = f(e)
            last.then_inc(self.sem_of[inc[0]], inc[1])

    def finish(self, eng, e):
        for j in range(NDMASEM):
            if self.dval[j] > 0:
                e.wait_ge(self.dsems[j], self.dval[j])


class KB:
    def __init__(self, nc, stack, seq, nsamp=2, lsamp=16, debug=False):
        self.nc = nc
        self.stack = stack
        self.pg = Prog(nc, stack)
        self.seq = seq
        self.nt = seq // TILE
        self.nsamp = nsamp
        self.lsamp = lsamp
        self.debug = debug
        self.sbufs = []
        self.off = 0
        self.maxoff = 0
        self.CAP = 212000
        self.arena = stack.enter_context(nc.sbuf_tensor("arena", [128, self.CAP // 4], F32))
        self.psum = stack.enter_context(nc.psum_tensor("psum", [128, 4096], F32))
        self.pbufs = [Buf(f"bank{i}", "psum", i, i + 1) for i in range(8)]
        self.pnext = 0
        self.ntiles_total = self.nt + 1
        self.wissued = 0
        self.wtotal = self.ntiles_total * DEPTH * NBLK_L
        self.wused = 0

    def alloc(self, name, cols, dt=F32, nbufs=1):
        esz = 4 if dt == F32 else 2
        nbytes = (cols * esz + 3) // 4 * 4
        lo = self.off
        self.off += nbytes
        self.maxoff = max(self.maxoff, self.off)
        assert self.off <= self.CAP, f"SBUF overflow at {name}: {self.off}"
        ap = self.arena[:, lo // 4:(lo + nbytes) // 4]
        if dt != F32:
            ap = ap.bitcast(dt)
        ap = ap[:, 0:cols]
        bufs = []
        per = nbytes // nbufs
        for i in range(nbufs):
            b = Buf(f"{name}{i}", "sb", lo + i * per, lo + (i + 1) * per if i < nbufs - 1 else lo + nbytes)
            for o in self.sbufs:
                if o.lo < b.hi and b.lo < o.hi:
                    o.alias.append(b)
                    b.alias.append(o)
            self.sbufs.append(b)
            bufs.append(b)
        return V(ap, bufs)

    def ps(self, n=1):
        if n == 2 and self.pnext % 2 == 1:
            self.pnext = (self.pnext + 1) % 8
        i = self.pnext
        self.pnext = (self.pnext + n) % 8
        return V(self.psum[:, i * 512:(i + n) * 512], self.pbufs[i:i + n])

    @staticmethod
    def _bufs(*vs):
        out = []
        for v in vs:
            if isinstance(v, V):
                for b in v.bufs:
                    if b not in out:
                        out.append(b)
        return out

    @staticmethod
    def _a(v):
        return v.ap if isinstance(v, V) else v

    def mm(self, out, pairs, extra_reads=(), fp32=False):
        n = len(pairs)
        emits = []
        for i, (l, r) in enumerate(pairs):
            emits.append(lambda e, l=l, r=r, i=i: e.matmul(out.ap, l.ap, r.ap, start=(i == 0), stop=(i == n - 1)))
        rd = self._bufs(*[p[0] for p in pairs], *[p[1] for p in pairs], *extra_reads)
        self.pg.add("pe", emits, reads=rd, writes=out.bufs)

    def mms(self, items):
        emits = []
        rd, wr = [], []
        for (o, l, r, st, sp) in items:
            emits.append(lambda e, o=o, l=l, r=r, st=st, sp=sp: e.matmul(o.ap, l.ap, r.ap, start=st, stop=sp))
            rd += self._bufs(l, r)
            wr += self._bufs(o)
        self.pg.add("pe", emits, reads=list(dict.fromkeys(rd)), writes=list(dict.fromkeys(wr)))

    def trs(self, items):
        emits = []
        rd, wr = [], []
        for (o, i, idn) in items:
            emits.append(lambda e, o=o, i=i, idn=idn: e.transpose(o.ap, i.ap, idn.ap))
            rd += self._bufs(i, idn)
            wr += self._bufs(o)
        self.pg.add("pe", emits, reads=list(dict.fromkeys(rd)), writes=list(dict.fromkeys(wr)))

    def act(self, out, in_, func, bias=None, scale=None, accum=None, eng="act"):
        kw = {}
        if bias is not None:
            kw["bias"] = self._a(bias)
        if scale is not None:
            kw["scale"] = self._a(scale)
        if accum is not None:
            kw["accum_out"] = accum.ap
        self.pg.add("act", lambda e: e.activation(out.ap, in_.ap, func, **kw),
                    reads=self._bufs(in_, bias, scale), writes=self._bufs(out, accum))

    def tt(self, eng, out, a, b, op):
        self.pg.add(eng, lambda e: e.tensor_tensor(out.ap, a.ap, b.ap, op),
                    reads=self._bufs(a, b), writes=out.bufs)

    def ts(self, eng, out, a, s1, s2, op0, op1=None):
        if op1 is None:
            f = lambda e: e.tensor_scalar(out.ap, a.ap, self._a(s1), None, op0)
        else:
            f = lambda e: e.tensor_scalar(out.ap, a.ap, self._a(s1), self._a(s2), op0, op1)
        self.pg.add(eng, f, reads=self._bufs(a, s1, s2), writes=out.bufs)

    def stt(self, out, a, s, b, op0, op1):
        self.pg.add("dve", lambda e: e.scalar_tensor_tensor(out.ap, a.ap, self._a(s), b.ap, op0, op1),
                    reads=self._bufs(a, s, b), writes=out.bufs)

    def copy(self, eng, out, in_):
        if eng == "act":
            self.pg.add("act", lambda e: e.copy(out.ap, in_.ap), reads=in_.bufs, writes=out.bufs)
        else:
            self.pg.add(eng, lambda e: e.tensor_copy(out.ap, in_.ap), reads=in_.bufs, writes=out.bufs)

    def scan(self, out, d0, d1, init):
        self.pg.add("dve", lambda e: e.tensor_tensor_scan(out.ap, d0.ap, d1.ap, self._a(init), ALU.mult, ALU.add),
                    reads=self._bufs(d0, d1, init), writes=out.bufs)

    def recip(self, out, in_):
        self.pg.add("dve", lambda e: e.reciprocal(out.ap, in_.ap), reads=in_.bufs, writes=out.bufs)

    def memset(self, eng, out, val):
        self.pg.add(eng, lambda e: e.memset(out.ap, val), reads=(), writes=out.bufs)

    def dma(self, q, out, in_, **kw):
        self.pg.add(q, lambda e: e.dma_start(out=out.ap, in_=in_.ap, **kw),
                    reads=self._bufs(in_), writes=self._bufs(out), dma=True)


class TileCfg:
    def __init__(self, T, P, NSEG, L, CH):
        self.T, self.P, self.NSEG, self.L, self.CH = T, P, NSEG, L, CH
        self.NB = T // P
        self.NCH = T // CH
        self.NCB = P // CH
        self.BPS = L // P


def build_program(nc, stack, seq, nsamp=2, lsamp=16):
    kb = KB(nc, stack, seq, nsamp, lsamp)
    pg = kb.pg
    NT = seq // TILE
    TS = nsamp * lsamp
    NSEQ = 1 + nsamp
    def din(name, shape, dt=F32):
        return V(nc.dram_tensor(name, shape, dt, kind="ExternalInput").ap(), [])

    def dout(name, shape):
        return V(nc.dram_tensor(name, shape, F32, kind="ExternalOutput").ap(), [Buf(name, "dram", 0, 1)])

    xT_p = din("xT_p", [D, seq])
    xT_s = din("xT_s", [D, TS])
    wblk = din("wblk", [DEPTH * NBLK_L, 128, WB])
    pfm_d = din("pfm", [128, PF_TOT])
    pbc_d = din("pbc", [1, PB_TOT])
    wab_d = din("wab", [128, DEPTH * 4 * 128])
    wdt_d = din("wdt", [128, DEPTH * 64])
    sconv_d = din("s_conv", [128, DEPTH * nsamp * 30])
    shA_d = din("s_hA", [128, DEPTH * nsamp * 2])
    sssd_d = din("s_ssd", [128, DEPTH * nsamp * 512])
    shg_d = din("s_hgrn", [128, DEPTH * nsamp * 128])
    yT_p = dout("yT_p", [D, seq])
    yT_s = dout("yT_s", [D, TS])
    o_conv = dout("o_conv", [128, NSEQ * DEPTH * 30])
    o_hA = dout("o_hA", [128, NSEQ * DEPTH * 2])
    o_ssd = dout("o_ssd", [128, NSEQ * DEPTH * 512])
    o_hg = dout("o_hgrn", [128, NSEQ * DEPTH * 128])
    wscr_t = nc.dram_tensor("wscr", [DEPTH * NBLK_L, 128, WB], BF16, kind="Internal").ap()
    wscr = [V(wscr_t[i], [Buf(f"wscr{i}", "dram", 0, 1)]) for i in range(DEPTH * NBLK_L)]

    A = kb.alloc
    x_fm = A("x_fm", 8 * TILE, nbufs=8)
    h_bf = A("h_bf", 8 * TILE, BF16, nbufs=8)
    wslots = [A(f"wslot{i}", WB, BF16) for i in range(NSLOT)]
    U = A("U", 128)
    SL = A("SL", 128)
    MBD = A("MBD", 128)
    IDN = A("IDN", 128)
    ONF = A("ONF", 128)
    IDB = A("IDB", 128, BF16)
    ONB = A("ONB", 128, BF16)
    RMK = A("RMK", TILE)
    pfm = A("pfm", PF_TOT)
    pbc = A("pbc", PB_TOT)
    drv = A("drv", 64)
    aneg = A("aneg", 2 * 8)
    rowm = A("rowm", 2)
    wab = A("wab", DEPTH * 4 * 128, BF16)
    wdt = A("wdt", DEPTH * 64, BF16)
    NSLOTS = max(1, nsamp)
    convst = [[A(f"convst{l}_{s}", 30) for s in range(NSLOTS)] for l in range(DEPTH)]
    hAst = [[A(f"hA{l}_{s}", 2) for s in range(NSLOTS)] for l in range(DEPTH)]
    STf = [[A(f"ST{l}_{s}", 512) for s in range(NSLOTS)] for l in range(DEPTH)]
    STb = [[A(f"STb{l}_{s}", 512, BF16) for s in range(NSLOTS)] for l in range(DEPTH)]
    Scur = [[A(f"Sc{l}_{s}", 128) for s in range(NSLOTS)] for l in range(DEPTH)]
    rstd = A("rstd", TILE)
    sqt = [A(f"sqt{i}", TILE, BF16) for i in range(2)]

    DV_HBA, DV_HBX, DV_CA, DV_HCA, DV_LBH, DV_HOML = 0, 2, 4, 6, 8, 10
    DVL = 16

    ov0 = kb.off
    xpre = [A(f"xpre{i}", 4 + TILE, BF16) for i in range(3)]
    xa_c = A("xa_c", 2 * TILE, nbufs=2)
    xa_b = A("xa_b", 2 * TILE, BF16, nbufs=2)
    thr_b = A("thr_b", 2 * TILE, nbufs=2)
    thi_b = A("thi_b", 2 * TILE, nbufs=2)
    xs_f = A("xs_f", 4 * TILE, nbufs=4)
    B_b = A("B_b", 2 * TILE, BF16, nbufs=2)
    C_b = A("C_b", 2 * TILE, BF16, nbufs=2)
    xg = A("xg", 2 * TILE, nbufs=2)
    q_s = A("q_s", 2 * TILE, nbufs=2)
    f_f = A("f_f", 2 * TILE, nbufs=2)
    kk = A("kk", 2 * TILE, nbufs=2)
    lf = A("lf", 2 * TILE, nbufs=2)
    bcm = A("bcm", 2 * TILE, nbufs=2)
    tmpA = [A(f"tmpA{i}", 2 * TILE, nbufs=2) for i in range(4)]
    qk6 = [A(f"qk{i}", 2 * TILE, BF16, nbufs=2) for i in range(6)]
    ebl = A("ebl", 2 * 8)
    zs = A("zs", 4 * 512)
    v_tm = A("v_tm", 4 * 256, BF16)
    v_msk = A("v_msk", 2 * 256, BF16)
    gs = A("gs", 4 * 256)
    dtp = A("dtp", 32)
    dtv = A("dtv", 32)
    da = A("da", 8)
    dtt = A("dtt", 8)
    eat = A("eat", 16)
    elast = A("elast", 8)
    xs_tm = A("xs_tm", 512)
    B_tm = A("B_tm", 256, BF16)
    kh_tm = A("kh_tm", 256, BF16)
    Lm = A("Lm", 1024)
    Em = A("Em", 1024)
    cbm = A("cbm", 256)
    Mm = A("Mm", 1024, BF16)
    xdt = A("xdt", 512, BF16)
    xdtt = A("xdtt", 512, BF16)
    y1 = A("y1", 512)
    y2s = [A("y2a", 512), A("y2b", 512)]
    y3 = A("y3", 512)
    junk = A("junk", 768, BF16)
    ssS = A("ssS", 2)
    rsS = A("rsS", 2)
    ssH = A("ssH", 2)
    rsH = A("rsH", 2)
    att_m = A("att_m", 512, BF16)
    Sbf = A("Sbf", 8 * 128, BF16)
    oc = A("oc", 256)
    ov1 = kb.off
    kb.off = ov0
    act_b = A("act_b", 22 * TILE, BF16, nbufs=22)
    sil = [A(f"sil{i}", TILE) for i in range(2)]
    y_out = A("y_out", 8 * TILE, nbufs=8)
    pre32 = [A(f"pre32_{i}", WB) for i in range(4)]
    pre16 = [A(f"pre16_{i}", WB, BF16) for i in range(4)]
    kb.off = max(kb.off, ov1)
    print(f"[kernel] SBUF bytes/partition used: {kb.maxoff} (overlay1 {ov1 - ov0})")

    kb.dma("sp", pfm, pfm_d)
    kb.dma("sp", pbc, V(pbc_d.ap[0:1, :].partition_broadcast(128) if False else pbc_d.ap.broadcast_to([128, PB_TOT]), []))
    wab32 = xa_c[:, 0:DEPTH * 4 * 128]
    kb.dma("sp", wab32, wab_d)
    wdt32 = xg[:, 0:DEPTH * 64]
    kb.dma("sp", wdt32, wdt_d)
    kb.copy("dve", wab, wab32)
    kb.copy("dve", wdt, wdt32)
    kb.memset("pool", ONF, 1.0)
    kb.memset("pool", ONB, 1.0)
    kb.memset("pool", RMK, 1.0)
    kb.memset("pool", RMK.re("p (j c) -> p j c", c=64)[:, :, 0:1], 0.0)
    pg.add("pool", lambda e: e.affine_select(U.ap, ONF.ap, [[1, 128]], ALU.is_ge, 0.0, base=0, channel_multiplier=-1),
           reads=ONF.bufs, writes=U.bufs)
    pg.add("pool", lambda e: e.affine_select(SL.ap, ONF.ap, [[-1, 128]], ALU.is_ge, 0.0, base=-1, channel_multiplier=1),
           reads=ONF.bufs, writes=SL.bufs)
    pg.add("pool", lambda e: e.affine_select(IDN.ap, ONF.ap, [[1, 128]], ALU.is_equal, 0.0, base=0, channel_multiplier=-1),
           reads=ONF.bufs, writes=IDN.bufs)
    kb.copy("pool", IDB, IDN)
    kb.copy("pool", MBD, U)
    kb.memset("pool", MBD[0:64, 64:128], 0.0)
    kb.memset("pool", rowm, 1.0)
    kb.memset("pool", rowm[64:128, 0:1], 0.0)
    kb.memset("pool", rowm[0:64, 1:2], 0.0)
    for l in range(DEPTH):
        b = DVL * l
        pl = PF_L * l
        kb.ts("dve", drv[:, b + DV_HBA:b + DV_HBA + 2], pfm[:, pl + PF_BA:pl + PF_BA + 2], 0.5, None, ALU.mult)
        kb.ts("dve", drv[:, b + DV_HBX:b + DV_HBX + 2], pfm[:, pl + PF_BX:pl + PF_BX + 2], 0.5, None, ALU.mult)
        t = drv[:, 60:62]
        kb.act(t, pfm[:, pl + PF_LAM:pl + PF_LAM + 2], AF.Exp, scale=-1.0)
        kb.act(t, t, AF.Ln, bias=1.0)
        kb.ts("dve", drv[:, b + DV_CA:b + DV_CA + 2], t, -8.0, None, ALU.mult)
        kb.ts("dve", drv[:, b + DV_HCA:b + DV_HCA + 2], t, -4.0, None, ALU.mult)
        lbv = drv[:, 62:64]
        if l == 0:
            kb.memset("dve", lbv, 0.0)
        else:
            kb.tt("dve", lbv, pfm[:, PF_LB:PF_LB + 2], pfm[:, PF_LB + 2:PF_LB + 4], ALU.subtract)
            kb.act(lbv, lbv, AF.Exp)
            kb.ts("dve", lbv, lbv, 1.0, None, ALU.add)
            kb.recip(lbv, lbv)
        kb.ts("dve", drv[:, b + DV_HOML:b + DV_HOML + 2], lbv, -0.5, 0.5, ALU.mult, ALU.add)
        kb.tt("dve", drv[:, b + DV_LBH:b + DV_LBH + 2], lbv, drv[:, b + DV_HOML:b + DV_HOML + 2], ALU.add)
        kb.act(aneg[:, 8 * l:8 * l + 8], pbc[:, PB_L * l + PB_ALOG:PB_L * l + PB_ALOG + 8], AF.Exp)
        kb.ts("dve", aneg[:, 8 * l:8 * l + 8], aneg[:, 8 * l:8 * l + 8], -1.0, None, ALU.mult)

    stage(0)
    NB_ALL = DEPTH * NBLK_L
    cast_engs = ["act", "dve"]
    NPRE = len(pre32)
    stage(1)
    def blk_size(j):
        return BLK_SIZES[j % NBLK_L]

    def wissue():
        g = kb.wissued
        kb.wissued += 1
        j = g % NB_ALL
        n = blk_size(j)
        slot = wslots[g % NSLOT]
        if g < NB_ALL:
            s32 = pre32[g % NPRE]
            kb.dma("sp", s32[:, 0:n], V(wblk.ap[j][:, 0:n], []))
            kb.copy(cast_engs[g % 2], slot[:, 0:n], s32[:, 0:n])
            kb.dma("pool", V(wscr[j].ap[:, 0:n], wscr[j].bufs), slot[:, 0:n])
        else:
            kb.dma("sp", slot[:, 0:n], V(wscr[j].ap[:, 0:n], wscr[j].bufs))

    def wget():
        g = kb.wused
        kb.wused += 1
        while kb.wissued < min(g + NSLOT - 1, kb.wtotal):
            wissue()
        return wslots[g % NSLOT]

    def rmsnorm(T, wcol0, out_is_h=True, out_v=None):
        p = kb.ps()
        for k in range(8):
            s = sqt[k % 2]
            kb.act(s[:, 0:T], xk(k, T), AF.Square)
            pg.add("pe", lambda e, k=k, s=s: e.matmul(p.ap[:, 0:T], ONB.ap, s.ap[:, 0:T], start=(k == 0), stop=(k == 7)),
                   reads=ONB.bufs + s.bufs, writes=p.bufs)
        kb.act(rstd[:, 0:T], p[:, 0:T], AF.Ln, bias=EPS, scale=1.0 / D)
        kb.act(rstd[:, 0:T], rstd[:, 0:T], AF.Exp, scale=-0.5)
        for k in range(8):
            if out_is_h:
                o = V(h_bf.ap[:, k * TILE:k * TILE + T], [h_bf.bufs[k]])
            else:
                o = V(out_v.ap[:, k * TILE:k * TILE + T], [out_v.bufs[k]])
            kb.stt(o, xk(k, T), pfm[:, wcol0 + k:wcol0 + k + 1], rstd[:, 0:T], ALU.mult, ALU.mult)

    def xk(k, T):
        return V(x_fm.ap[:, k * TILE:k * TILE + T], [x_fm.bufs[k]])

    def hk(k, c0, n):
        return V(h_bf.ap[:, k * TILE + c0:k * TILE + c0 + n], [h_bf.bufs[k]])

    def mixw(k, c0, n):
        return hk(k, c0, n)

    evq = [0]

    def ev_eng():
        evq[0] += 1
        return "act" if evq[0] % 2 == 0 else "dve"

    class BankPool:
        def __init__(self, banks):
            self.banks = banks
            self.i = 0

        def get(self, n=1):
            b = self.banks[self.i % len(self.banks)]
            self.i += 1
            return V(kb.psum[:, b * 512:(b + n) * 512], kb.pbufs[b:b + n])

    def interleave(*gens):
        active = [g for g in gens if g is not None]
        while active:
            for g in list(active):
                if g not in active:
                    continue
                try:
                    tok = next(g)
                except StopIteration:
                    active.remove(g)
                    continue
                if tok == "drain":
                    for o in list(active):
                        if o is not g:
                            for _ in o:
                                pass
                            active.remove(o)
                yield

    def pipelined(n, front, back):
        prev = None
        for b in range(n):
            f = front(b)
            if prev is None:
                yield from f
            else:
                yield from interleave(f, prev)
            prev = back(b)
        yield from prev

    def chunkv(v, c, T):
        return V(v.ap[:, c * TILE:c * TILE + T], [v.bufs[c]])

    def layer(l, cfg, slots):
        T, P, NB, NSEG, L, CH = cfg.T, cfg.P, cfg.NB, cfg.NSEG, cfg.L, cfg.CH
        pl = PF_L * l
        pb = PB_L * l
        dv = DVL * l
        NCH = cfg.NCH

        def seg3(v):
            return v.re("p (s l) -> p s l", s=NSEG)

        rmsnorm(T, pl + PF_N1)

        stage(2)
        conv_i = [0]

        def a_pre():
            for c in range(2):
                pr = kb.ps()
                pi = kb.ps()
                kb.mm(pr[:, 0:T], [(wab[:, (l * 4 + 0 + c) * 128:(l * 4 + 0 + c + 1) * 128], chunkv(xa_b, c, T))])
                kb.mm(pi[:, 0:T], [(wab[:, (l * 4 + 2 + c) * 128:(l * 4 + 2 + c + 1) * 128], chunkv(xa_b, c, T))])
                kb.act(chunkv(thr_b, c, T), pr[:, 0:T], AF.Tanh, bias=drv[:, dv + DV_HBA + c:dv + DV_HBA + c + 1], scale=0.5)
                kb.act(chunkv(thi_b, c, T), pi[:, 0:T], AF.Tanh, bias=drv[:, dv + DV_HBX + c:dv + DV_HBX + c + 1], scale=0.5)
            G = [chunkv(lf, c, T) for c in range(2)]
            X = [chunkv(xg, c, T) for c in range(2)]
            for c in range(2):
                kb.act(G[c], X[c], AF.Square)
            for c in range(2):
                kb.ts("dve", G[c], G[c], 0.044715, 1.0, ALU.mult, ALU.add)
            for c in range(2):
                kb.tt("dve", G[c], G[c], X[c], ALU.mult)
            for c in range(2):
                kb.act(G[c], G[c], AF.Tanh, scale=0.7978845608028654)
            for c in range(2):
                kb.stt(X[c], G[c], 1.0, X[c], ALU.add, ALU.mult)

        def proj_chunk(wv, off):
            p = kb.ps()
            pv = p[:, 0:T]
            kb.mm(pv, [(wv[:, off + kc * 128:off + (kc + 1) * 128], hk(kc, 0, T)) for kc in range(8)])
            return pv

        cbias = lambda ci: pfm[:, pl + PF_CB + ci:pl + PF_CB + ci + 1]

        def conv_evac(ci, pv):
            xp = xpre[conv_i[0] % 3]
            conv_i[0] += 1
            W = 4 + L
            xpv = xp[:, 0:NSEG * W].re("p (s w) -> p s w", s=NSEG)
            for s in range(NSEG):
                kb.copy("pool", xp[:, s * W:s * W + 3], convst[l][slots[s]][:, ci * 3:ci * 3 + 3])
            kb.copy("act", xpv[:, :, 3:3 + L], seg3(pv))
            for s in range(NSEG):
                kb.copy("dve", convst[l][slots[s]][:, ci * 3:ci * 3 + 3], pv[:, s * L + L - 3:s * L + L])
            return xp

        def conv_mm(wv, xp):
            W = 4 + L
            p2 = kb.ps()
            for s in range(NSEG):
                kb.mm(p2[:, s * L:(s + 1) * L],
                      [(wv[:, 1024 + k * 128:1024 + (k + 1) * 128], xp[:, s * W + k:s * W + k + L]) for k in range(4)])
            return p2[:, 0:T]

        units = []

        def mk_conv(ci, fin):
            st = {}

            def A_():
                st["wv"] = wget()
                st["xp"] = conv_evac(ci, proj_chunk(st["wv"], 0))

            def B_():
                fin(conv_mm(st["wv"], st["xp"]))
            return (A_, B_)

        def fin_xa(c):
            def f_(pc):
                kb.act(chunkv(xa_c, c, T), pc, AF.Identity, bias=cbias(c))
                kb.act(chunkv(xa_b, c, T), pc, AF.Identity, bias=cbias(c))
            return f_

        def fin_xbc(j):
            def f_(pc):
                if j < 4:
                    dst = V(xs_f.ap[:, j * TILE:j * TILE + T], [xs_f.bufs[j]])
                elif j < 6:
                    dst = chunkv(B_b, j - 4, T)
                else:
                    dst = chunkv(C_b, j - 6, T)
                kb.act(dst, pc, AF.Silu, bias=cbias(2 + j))
            return f_

        def mk_pair(kind):
            def A_():
                wv = wget()
                for j in range(2):
                    pv = proj_chunk(wv, j * 1024)
                    if kind == "ga":
                        kb.copy("act", chunkv(xg, j, T), pv)
                    elif kind == "q":
                        kb.act(chunkv(q_s, j, T), pv, AF.Silu)
                    else:
                        kb.act(chunkv(f_f, j, T), pv, AF.Tanh, scale=0.5)

            def B_():
                if kind == "ga":
                    a_pre_flag[0] = True
            return (A_, B_)

        a_pre_flag = [False]
        units.append(mk_conv(0, fin_xa(0)))
        units.append(mk_conv(1, fin_xa(1)))
        units.append(mk_pair("ga"))
        for j in range(8):
            units.append(mk_conv(2 + j, fin_xbc(j)))
        units.append(mk_pair("q"))
        units.append(mk_pair("f"))
        did_pre = False
        for i in range(len(units) + 1):
            if i < len(units):
                units[i][0]()
            if i > 0:
                units[i - 1][1]()
            if a_pre_flag[0] and not did_pre and i >= 4:
                a_pre()
                did_pre = True

        stage(4)
        pSF = BankPool([2, 3])
        pSD = BankPool([0])
        pTR = BankPool([4])
        pHF = BankPool([5, 6])
        pHO = BankPool([7])

        def fl(v):
            return V(v.ap.rearrange("p (c t) -> p c t", c=2)[:, :, 0:T], v.bufs)

        def ch4(v):
            return V(v.ap.rearrange("p (c t) -> p c t", c=2)[:, :, 0:T].rearrange("p c (j i) -> p c j i", i=CH), v.bufs)

        qt_e, qt_o, qh_e, qh_o, kt, kh = qk6
        d1, r1, d2, e3 = tmpA[0], tmpA[1], tmpA[2], tmpA[3]

        def c_prep():
            for c in range(2):
                kb.ts("dve", chunkv(f_f, c, T), chunkv(f_f, c, T), drv[:, dv + DV_HOML + c:dv + DV_HOML + c + 1],
                      drv[:, dv + DV_LBH + c:dv + DV_LBH + c + 1], ALU.mult, ALU.add)
                kb.ts("pool", chunkv(kk, c, T), chunkv(f_f, c, T), -1.0, 1.0, ALU.mult, ALU.add)
            for c in range(2):
                kb.act(chunkv(lf, c, T), chunkv(f_f, c, T), AF.Ln)
                for s in range(NSEG):
                    ss_ = slice(c * TILE + s * L, c * TILE + s * L + L)
                    kb.scan(V(bcm.ap[:, ss_], [bcm.bufs[c]]), RMK[:, 0:L], V(lf.ap[:, ss_], [lf.bufs[c]]), 0.0)
                yield
            b4 = ch4(bcm)
            kb.tt("dve", ch4(d1), b4, b4[:, :, :, CH // 2 - 1:CH // 2].bc([128, 2, NCH, CH]), ALU.subtract)
            kb.tt("pool", ch4(d2), b4[:, :, :, CH - 1:CH].bc([128, 2, NCH, CH]), b4, ALU.subtract)
            yield
            kb.act(fl(r1), fl(d1), AF.Exp, scale=-1.0)
            kb.act(fl(d1), fl(d1), AF.Exp)
            yield
            kb.act(fl(e3), fl(bcm), AF.Exp)
            kb.act(V(ebl.ap.rearrange("p (c j) -> p c j", c=2)[:, :, 0:NCH], ebl.bufs),
                   V(b4.ap[:, :, :, CH - 1], b4.bufs), AF.Exp)
            kb.tt("dve", fl(qt_o), fl(q_s), fl(d1), ALU.mult)
            yield
            kb.act(fl(d2), fl(d2), AF.Exp)
            kb.tt("dve", fl(qh_o), fl(q_s), fl(e3), ALU.mult)
            kb.tt("pool", fl(kt), fl(kk), fl(r1), ALU.mult)
            yield
            kb.act(fl(qt_e), fl(qt_o), AF.Copy, scale=rowm[:, 0:1])
            kb.act(fl(qt_o), fl(qt_o), AF.Copy, scale=rowm[:, 1:2])
            kb.tt("dve", fl(kh), fl(kk), fl(d2), ALU.mult)
            yield
            kb.act(fl(qh_e), fl(qh_o), AF.Copy, scale=rowm[:, 0:1])
            kb.act(fl(qh_o), fl(qh_o), AF.Copy, scale=rowm[:, 1:2])
            yield

        def a_post(c):
            cs = slice(c * TILE, c * TILE + T)
            thr, thi = chunkv(thr_b, c, T), chunkv(thi_b, c, T)
            av, sv = chunkv(tmpA[2], c, T), chunkv(tmpA[3], c, T)
            cA = drv[:, dv + DV_CA + c:dv + DV_CA + c + 1]
            hcA = drv[:, dv + DV_HCA + c:dv + DV_HCA + c + 1]
            kb.act(av, thr, AF.Exp, bias=hcA, scale=hcA)
            kb.act(sv, thr, AF.Exp, bias=cA, scale=cA)
            kb.stt(thi, thi, 1.0, chunkv(xa_c, c, T), ALU.add, ALU.mult)
            yield
            kb.act(sv, sv, AF.Ln, bias=1.0, scale=-1.0)
            kb.act(sv, sv, AF.Exp, scale=0.5)
            yield
            kb.stt(thi, thi, 0.5, sv, ALU.mult, ALU.mult)
            yield
            for s in range(NSEG):
                ss_ = slice(c * TILE + s * L, c * TILE + s * L + L)
                hst = hAst[l][slots[s]][:, c:c + 1]
                kb.scan(V(thr_b.ap[:, ss_], [thr_b.bufs[c]]), V(tmpA[2].ap[:, ss_], [tmpA[2].bufs[c]]),
                        V(thi_b.ap[:, ss_], [thi_b.bufs[c]]), hst)
                kb.copy("act", hst, V(thr_b.ap[:, c * TILE + s * L + L - 1:c * TILE + s * L + L], [thr_b.bufs[c]]))
            yield
            kb.stt(mixw(c, 0, T), thr, 0.5, chunkv(xg, c, T), ALU.mult, ALU.mult)
            yield

        def ssd_front(b):
            s = b // cfg.BPS
            slot = slots[s]
            c0 = b * P
            y2 = y2s[b % 2]
            if b == 0:
                kb.act(dtv[0:P, 0:NB * 8], dtp[0:P, 0:NB * 8], AF.Exp)
                kb.act(dtv[0:P, 0:NB * 8], dtv[0:P, 0:NB * 8], AF.Ln, bias=1.0)
            dtb_ = dtv[0:P, b * 8:b * 8 + 8]
            kb.tt("dve", da[0:P, :], dtb_, aneg[0:P, 8 * l:8 * l + 8], ALU.mult)
            L3 = Lm[0:P, 0:8 * P].re("p (h s) -> p h s", h=8)
            kb.tt("dve", L3, SL[0:P, 0:P].un(1).bc([P, 8, P]), da[0:P, :].un(2).bc([P, 8, P]), ALU.mult)
            pT = pSF.get()
            kb.trs([(pT[0:P, j * 128:(j + 1) * 128], V(xs_f.ap[:, j * TILE + c0:j * TILE + c0 + P], [xs_f.bufs[j]]), IDN)
                    for j in range(4)])
            kb.copy("act", xs_tm[0:P, :], pT[0:P, 0:512])
            yield
            pD = pSD.get(2)
            kb.mms([(pD[0:P, h * P:(h + 1) * P], Lm[0:P, h * P:(h + 1) * P], U[0:P, 0:P], True, True) for h in range(8)])
            pS = pSF.get()
            kb.mms([(pS[0:P, 0:8], U[0:P, 0:P], da[0:P, :], True, True),
                    (pS[0:P, 8:16], SL[0:P, 0:P], da[0:P, :], True, True),
                    (pS[:, 16:24], ONF[0:P, :], da[0:P, :], True, True)])
            kb.act(eat[0:P, :], pS[0:P, 0:16], AF.Exp)
            kb.act(elast, pS[:, 16:24], AF.Exp)
            yield
            pT2 = pSF.get()
            pT2b = pT2.cast(BF16)
            kb.trs([(pT2b[0:P, g * 128:(g + 1) * 128], chunkv(B_b, g, T)[:, c0:c0 + P], IDB) for g in range(2)])
            kb.copy("act", B_tm[0:P, :], pT2b[0:P, 0:256])
            kb.act(Em[0:P, 0:8 * P], pD[0:P, 0:8 * P], AF.Exp)
            kb.tt("dve", dtt[0:P, :], dtb_, eat[0:P, 8:16], ALU.mult)
            yield
            pC = pSF.get()
            kb.mms([(pC[0:P, g * P:(g + 1) * P], chunkv(B_b, g, T)[:, c0:c0 + P], chunkv(C_b, g, T)[:, c0:c0 + P], True, True)
                    for g in range(2)])
            kb.tt("dve", cbm[0:P, 0:2 * P].re("p (g t) -> p g t", g=2), pC[0:P, 0:2 * P].re("p (g t) -> p g t", g=2),
                  U[0:P, 0:P].un(1).bc([P, 2, P]), ALU.mult)
            x3 = xs_tm[0:P, :].re("p (h d) -> p h d", h=8)
            kb.tt("dve", xdt[0:P, :].re("p (h d) -> p h d", h=8), x3, dtb_.un(2).bc([P, 8, 64]), ALU.mult)
            kb.tt("pool", xdtt[0:P, :].re("p (h d) -> p h d", h=8), x3, dtt[0:P, :].un(2).bc([P, 8, 64]), ALU.mult)
            yield
            kb.tt("dve", Mm[0:P, 0:8 * P].re("p (g h t) -> p g h t", g=2, h=4), Em[0:P, 0:8 * P].re("p (g h t) -> p g h t", g=2, h=4),
                  cbm[0:P, 0:2 * P].re("p (g t) -> p g t", g=2).un(2).bc([P, 2, 4, P]), ALU.mult)
            kb.tt("pool", y2[0:P, :].re("p (h d) -> p h d", h=8), x3, pbc[0:P, pb + PB_DSK:pb + PB_DSK + 8].un(2).bc([P, 8, 64]), ALU.mult)
            yield
            stage(7)
            pI = pSF.get()
            kb.mms([(pI[0:P, g * 256:(g + 1) * 256], chunkv(C_b, g, T)[:, c0:c0 + P], STb[l][slot][:, g * 256:(g + 1) * 256], True, True)
                    for g in range(2)])
            kb.tt("dve", y1[0:P, :].re("p (h d) -> p h d", h=8), pI[0:P, 0:512].re("p (h d) -> p h d", h=8),
                  eat[0:P, 0:8].un(2).bc([P, 8, 64]), ALU.mult)
            pY = pSF.get()
            kb.mms([(pY[0:P, h * 64:(h + 1) * 64], Mm[0:P, h * P:(h + 1) * P], xdt[0:P, h * 64:(h + 1) * 64], True, True) for h in range(8)])
            yield
            kb.tt("dve", y1[0:P, :], y1[0:P, :], pY[0:P, 0:512], ALU.add)
            pN = pSF.get()
            kb.mms([(pN[:, g * 256:(g + 1) * 256], B_tm[0:P, g * 128:(g + 1) * 128], xdtt[0:P, g * 256:(g + 1) * 256], True, True)
                    for g in range(2)])
            st = STf[l][slot]
            kb.tt("dve", st.re("p (h d) -> p h d", h=8), st.re("p (h d) -> p h d", h=8), elast.un(2).bc([128, 8, 64]), ALU.mult)
            yield
            kb.tt("dve", st, st, pN[:, 0:512], ALU.add)
            kb.copy("act", STb[l][slot], st)
            kb.tt("pool", y2[0:P, :], y2[0:P, :], y1[0:P, :], ALU.add)
            kb.tt("pool", y2[0:P, :], y2[0:P, :], zs[0:P, b * 512:(b + 1) * 512], ALU.mult)
            yield

        def ssd_back(b):
            c0 = b * P
            y2 = y2s[b % 2]
            for g in range(2):
                kb.act(junk[0:P, g * 256:(g + 1) * 256], y2[0:P, g * 256:(g + 1) * 256], AF.Square, accum=ssS[0:P, g:g + 1])
            kb.act(rsS[0:P, 0:2], ssS[0:P, 0:2], AF.Ln, bias=EPS, scale=1.0 / 256)
            kb.act(rsS[0:P, 0:2], rsS[0:P, 0:2], AF.Exp, scale=-0.5)
            yield
            for g in range(2):
                kb.stt(y3[0:P, g * 256:(g + 1) * 256], y2[0:P, g * 256:(g + 1) * 256], rsS[0:P, g:g + 1],
                       pbc[0:P, pb + PB_SNW + g * 256:pb + PB_SNW + (g + 1) * 256], ALU.mult, ALU.mult)
            yield
            stage(9)
            pM = pTR.get()
            kb.trs([(pM[:, j * P:(j + 1) * P], y3[0:P, j * 128:(j + 1) * 128], IDN[0:P, 0:P]) for j in range(4)])
            kb.copy("act", V(h_bf.ap.rearrange("p (k t) -> p k t", k=8)[:, 2:6, c0:c0 + P], h_bf.bufs[2:6]),
                    pM[:, 0:4 * P].re("p (j t) -> p j t", j=4))
            yield

        def hg_front(b):
            s = b // cfg.BPS
            slot = slots[s]
            c0 = b * P
            stage(8)
            pT2 = pHF.get()
            pT2b = pT2.cast(BF16)
            kb.trs([(pT2b[0:P, c * 128:(c + 1) * 128], chunkv(kh, c, T)[:, c0:c0 + P], IDB) for c in range(2)])
            kb.copy("act", kh_tm[0:P, :], pT2b[0:P, 0:256])
            yield
            pA = [pHF.get(), pHF.get()]
            items = []
            for h in range(4):
                hp, c = h % 2, h // 2
                qt_ = qt_e if hp == 0 else qt_o
                items.append((pA[hp][0:P, c * P:(c + 1) * P], chunkv(kt, c, T)[:, c0:c0 + P],
                              chunkv(qt_, c, T)[:, c0:c0 + P], True, True))
            kb.mms(items)
            for hp in range(2):
                kb.tt("dve", att_m[0:P, hp * 2 * P:(hp + 1) * 2 * P].re("p (c t) -> p c t", c=2),
                      pA[hp][0:P, 0:2 * P].re("p (c t) -> p c t", c=2), MBD[0:P, 0:P].un(1).bc([P, 2, P]), ALU.mult)
            yield
            sc = Scur[l][slot]
            pDs = pHF.get()
            for jj in range(cfg.NCB):
                if cfg.NCB == 1:
                    vm = v_tm[0:P, b * 256:(b + 1) * 256]
                else:
                    vm = v_msk[0:P, jj * 256:(jj + 1) * 256]
                    kb.act(vm, v_tm[0:P, b * 256:(b + 1) * 256], AF.Copy, scale=rowm[0:P, jj:jj + 1])
                kb.mms([(pDs[:, (jj * 2 + c) * 128:(jj * 2 + c + 1) * 128], kh_tm[0:P, c * 128:(c + 1) * 128],
                         vm[:, c * 128:(c + 1) * 128], True, True) for c in range(2)])
            yield
            for jj in range(cfg.NCB):
                j = b * cfg.NCB + jj
                kb.copy("act", Sbf[:, jj * 128:(jj + 1) * 128], sc)
                for c in range(2):
                    for hp in range(2):
                        rows = slice(hp * 64, hp * 64 + 64)
                        kb.stt(sc[rows, c * 64:(c + 1) * 64], sc[rows, c * 64:(c + 1) * 64], ebl[rows, c * 8 + j:c * 8 + j + 1],
                               pDs[rows, (jj * 2 + c) * 128 + hp * 64:(jj * 2 + c) * 128 + hp * 64 + 64], ALU.mult, ALU.add)
                yield
            if b > 0:
                yield "drain"
            pO = pHO.get()
            for h in range(4):
                hp, c = h % 2, h // 2
                qh_ = qh_e if hp == 0 else qh_o
                items = [(pO[0:P, h * 64:(h + 1) * 64], att_m[0:P, (hp * 2 + c) * P:(hp * 2 + c + 1) * P],
                          v_tm[0:P, b * 256 + h * 64:b * 256 + (h + 1) * 64], True, False)]
                for jj in range(cfg.NCB):
                    items.append((pO[jj * CH:(jj + 1) * CH, h * 64:(h + 1) * 64],
                                  chunkv(qh_, c, T)[:, c0 + jj * CH:c0 + (jj + 1) * CH],
                                  Sbf[:, jj * 128 + c * 64:jj * 128 + (c + 1) * 64], False, jj == cfg.NCB - 1))
                kb.mms(items)
            hg_po[0] = pO
            yield

        hg_po = [None]

        def hg_back(b):
            c0 = b * P
            pO = hg_po[0]
            kb.act(junk[0:P, 512:768], pO[0:P, 0:256], AF.Square, accum=ssH[0:P, 0:1])
            kb.act(rsH[0:P, 0:1], ssH[0:P, 0:1], AF.Ln, bias=EPS, scale=1.0 / 256)
            kb.act(rsH[0:P, 0:1], rsH[0:P, 0:1], AF.Exp, scale=-0.5)
            yield
            kb.stt(oc[0:P, :], pO[0:P, 0:256], rsH[0:P, 0:1], pbc[0:P, pb + PB_HNW:pb + PB_HNW + 256], ALU.mult, ALU.mult)
            kb.tt("pool", oc[0:P, :], oc[0:P, :], gs[0:P, b * 256:(b + 1) * 256], ALU.mult)
            yield
            pM2 = pTR.get()
            kb.trs([(pM2[:, j * P:(j + 1) * P], oc[0:P, j * 128:(j + 1) * 128], IDN[0:P, 0:P]) for j in range(2)])
            kb.copy("act", V(h_bf.ap.rearrange("p (k t) -> p k t", k=8)[:, 6:8, c0:c0 + P], h_bf.bufs[6:8]),
                    pM2[:, 0:2 * P].re("p (j t) -> p j t", j=2))
            yield

        stage(3)
        cg = c_prep()
        wz = [wget(), wget()]
        for b in range(NB):
            p = kb.ps()
            kb.mm(p[0:P, 0:512], [(hk(kc, b * P, P), wz[kc // 4][:, (kc % 4) * 512:(kc % 4 + 1) * 512]) for kc in range(8)])
            kb.act(zs[0:P, b * 512:(b + 1) * 512], p[0:P, 0:512], AF.Silu)
        stage(31)
        for _ in cg:
            pass
        wg = [wget(), wget()]
        for b in range(NB):
            p = kb.ps()
            kb.mm(p[0:P, 0:512], [(hk(kc, b * P, P), wg[kc // 4][:, (kc % 4) * 512:(kc % 4 + 1) * 512]) for kc in range(8)])
            kb.copy("dve", v_tm[0:P, b * 256:(b + 1) * 256], p[0:P, 0:256])
            kb.act(gs[0:P, b * 256:(b + 1) * 256], p[0:P, 256:512], AF.Silu)
        stage(32)
        for b in range(NB):
            p = kb.ps()
            kb.mm(p[0:P, 0:8], [(hk(kc, b * P, P), wdt[:, l * 64 + kc * 8:l * 64 + kc * 8 + 8]) for kc in range(8)])
            kb.tt("dve", dtp[0:P, b * 8:b * 8 + 8], p[0:P, 0:8], pbc[0:P, pb + PB_DTB:pb + PB_DTB + 8], ALU.add)
        for _ in cg:
            pass

        def chain_x():
            yield from interleave(pipelined(NB, hg_front, hg_back), a_post(0), a_post(1))

        def chain_y():
            yield from pipelined(NB, ssd_front, ssd_back)

        for _ in interleave(chain_y(), chain_x()):
            pass

        stage(10)
        for bi in range(4):
            wv = wget()
            for j in range(2):
                dc = 2 * bi + j
                p = kb.ps()
                kb.mm(p[:, 0:T], [(wv[:, (j * 8 + ec) * 128:(j * 8 + ec + 1) * 128], mixw(ec, 0, T)) for ec in range(8)])
                kb.tt("dve", xk(dc, T), xk(dc, T), p[:, 0:T], ALU.add)

        stage(11)
        rmsnorm(T, pl + PF_N2)
        for fc in range(22):
            wv = wget()
            pg_ = kb.ps()
            pu_ = kb.ps()
            kb.mm(pg_[:, 0:T], [(wv[:, kc * 128:(kc + 1) * 128], hk(kc, 0, T)) for kc in range(8)])
            kb.mm(pu_[:, 0:T], [(wv[:, 1024 + kc * 128:1024 + (kc + 1) * 128], hk(kc, 0, T)) for kc in range(8)])
            s_ = sil[fc % 2]
            kb.act(s_[:, 0:T], pg_[:, 0:T], AF.Silu)
            kb.tt("dve", V(act_b.ap[:, fc * TILE:fc * TILE + T], [act_b.bufs[fc]]), s_[:, 0:T], pu_[:, 0:T], ALU.mult)
        for dc in range(8):
            w0 = wget()
            w1 = wget()
            p = kb.ps()
            pairs = []
            for fc in range(22):
                wv = w0 if fc < 11 else w1
                o = (fc % 11) * 128
                pairs.append((wv[:, o:o + 128], V(act_b.ap[:, fc * TILE:fc * TILE + T], [act_b.bufs[fc]])))
            kb.mm(p[:, 0:T], pairs)
            kb.tt("dve", xk(dc, T), xk(dc, T), p[:, 0:T], ALU.add)
    def zero_states():
        for l in range(DEPTH):
            kb.memset("pool", convst[l][0], 0.0)
            kb.memset("pool", hAst[l][0], 0.0)
            kb.memset("pool", STf[l][0], 0.0)
            kb.memset("pool", STb[l][0], 0.0)
            kb.memset("pool", Scur[l][0], 0.0)

    def store_states(seq_i, slot):
        for l in range(DEPTH):
            base = (seq_i * DEPTH + l)
            kb.dma("pool", o_conv[:, base * 30:(base + 1) * 30], convst[l][slot])
            kb.dma("pool", o_hA[:, base * 2:(base + 1) * 2], hAst[l][slot])
            kb.dma("pool", o_ssd[:, base * 512:(base + 1) * 512], STf[l][slot])
            kb.dma("pool", o_hg[:, base * 128:(base + 1) * 128], Scur[l][slot])

    def load_states(si, slot):
        for l in range(DEPTH):
            base = (l * nsamp + si)
            kb.dma("pool", convst[l][slot], sconv_d[:, base * 30:(base + 1) * 30])
            kb.dma("pool", hAst[l][slot], shA_d[:, base * 2:(base + 1) * 2])
            kb.dma("pool", STf[l][slot], sssd_d[:, base * 512:(base + 1) * 512])
            kb.copy("pool", STb[l][slot], STf[l][slot])
            kb.dma("pool", Scur[l][slot], shg_d[:, base * 128:(base + 1) * 128])

    def load_x(cfg, x_src):
        T = cfg.T
        for k in range(8):
            kb.dma("pool", xk(k, T), V(x_src.ap[:, k, :], []))

    def run_tile(cfg, y_dst, slots, nxt=None):
        T = cfg.T
        for l in range(DEPTH):
            layer(l, cfg, slots)
        rmsnorm(T, PF_FN, out_is_h=False, out_v=y_out)
        if nxt is not None:
            load_x(*nxt)
        for k in range(8):
            kb.dma("pool", V(y_dst.ap[:, k, :], y_dst.bufs), V(y_out.ap[:, k * TILE:k * TILE + T], [y_out.bufs[k]]))

    cfgP = TileCfg(TILE, 128, 1, TILE, 64)
    xTp3 = xT_p.ap.rearrange("(k p) t -> p k t", p=128)
    yTp3 = yT_p.ap.rearrange("(k p) t -> p k t", p=128)
    cfgS = TileCfg(TS, lsamp, nsamp, lsamp, lsamp) if nsamp > 0 else None
    xTs3 = xT_s.ap.rearrange("(k p) t -> p k t", p=128)
    yTs3 = yT_s.ap.rearrange("(k p) t -> p k t", p=128)

    def xsrc(ti):
        return V(xTp3[:, :, ti * TILE:(ti + 1) * TILE], [])

    assert nsamp > 0, "the first (sample) tile carries the weight conversion"
    load_x(cfgS, V(xTs3, []))
    for si in range(nsamp):
        load_states(si, si)
    run_tile(cfgS, V(yTs3, yT_s.bufs), list(range(nsamp)), (cfgP, xsrc(0)) if NT > 0 else None)
    for si in range(nsamp):
        store_states(1 + si, si)
    if NT > 0:
        zero_states()
    for ti in range(NT):
        nxt = (cfgP, xsrc(ti + 1)) if ti + 1 < NT else None
        run_tile(cfgP, V(yTp3[:, :, ti * TILE:(ti + 1) * TILE], yT_p.bufs), [0], nxt)
    if NT > 0:
        store_states(0, 0)

    with nc.Block() as block:
        @block.tensor
        def _(e):
            pg.replay("pe", e)

        @block.scalar
        def _(e):
            pg.replay("act", e)

        @block.vector
        def _(e):
            pg.replay("dve", e)

        @block.gpsimd
        def _(e):
            pg.replay("pool", e)

        @block.sync
        def _(e):
            pg.replay("sp", e)
            pg.finish("sp", e)
    print(f"[kernel] ops: { {k: len(v) for k, v in pg.ops.items()} } instrs~{pg.nops}")
    return nc


def _fm_chunk(W, c0):
    sub = W[:, c0:c0 + 128]
    K = sub.shape[0]
    return sub.reshape(K // 128, 128, 128).transpose(1, 0, 2).reshape(128, -1)


def _layer_blocks(w_in, w_out, w_g, w_u, w_d, cw):
    blocks = []

    def conv_blk(c0, ci):
        blk = np.zeros((128, WB), np.float32)
        blk[:, 0:1024] = _fm_chunk(w_in, c0)
        for k in range(4):
            blk[np.arange(128), 1024 + k * 128 + np.arange(128)] = cw[ci, k]
        return blk

    def pair_blk(c0):
        return np.concatenate([_fm_chunk(w_in, c0), _fm_chunk(w_in, c0 + 128)], axis=1)

    blocks.append(conv_blk(C_XA, 0))
    blocks.append(conv_blk(C_XA + 128, 1))
    blocks.append(pair_blk(C_GA))
    for j in range(8):
        blocks.append(conv_blk(C_XS + j * 128, 2 + j))
    blocks.append(pair_blk(C_Q))
    blocks.append(pair_blk(C_F))
    for c0 in (C_Z, C_I):
        r = w_in[:, c0:c0 + 512].reshape(8, 128, 512).transpose(1, 0, 2)
        blocks.append(r[:, 0:4].reshape(128, -1))
        blocks.append(r[:, 4:8].reshape(128, -1))
    for i in range(0, 8, 2):
        blocks.append(np.concatenate([_fm_chunk(w_out, i * 128), _fm_chunk(w_out, (i + 1) * 128)], axis=1))
    for fc in range(22):
        blocks.append(np.concatenate([_fm_chunk(w_g, fc * 128), _fm_chunk(w_u, fc * 128)], axis=1))
    for dc in range(8):
        r = _fm_chunk(w_d, dc * 128).reshape(128, 22, 128)
        for half in range(2):
            blk = np.zeros((128, WB), np.float32)
            blk[:, 0:1408] = r[:, half * 11:(half + 1) * 11].reshape(128, -1)
            blocks.append(blk)
    assert len(blocks) == NBLK_L
    return blocks


def _col(v, n):
    return np.asarray(v).reshape(n, 128).T


_PROG_CACHE = {}


def kernel(x_prompt, x_sample, state_rglru_conv, state_rglru_h, state_ssd_conv, state_ssd, state_hgrn,
           norm1_w, w_in, rglru_conv_w, rglru_conv_b, rglru_wa, rglru_ba, rglru_wx, rglru_bx, rglru_lambda,
           ssd_conv_w, ssd_conv_b, ssd_dt_bias, ssd_a_log, ssd_d, ssd_norm_w, hgrn_lb, hgrn_norm_w,
           w_out, norm2_w, w_ffn_gate, w_ffn_up, w_ffn_down, final_norm_w):
    f = lambda a: np.ascontiguousarray(np.asarray(a, dtype=np.float32))
    x_prompt, x_sample = f(x_prompt), f(x_sample)
    ncores, seq, _ = x_prompt.shape
    nsamp = x_sample.shape[0] // ncores
    lsamp = x_sample.shape[1]
    TS = nsamp * lsamp
    w_in, w_out, w_g, w_u, w_d = f(w_in), f(w_out), f(w_ffn_gate), f(w_ffn_up), f(w_ffn_down)
    blocks = []
    pfm = np.zeros((128, PF_TOT), np.float32)
    pbc = np.zeros((1, PB_TOT), np.float32)
    wab = np.zeros((128, DEPTH * 4 * 128), np.float32)
    wdt = np.zeros((128, DEPTH * 64), np.float32)
    rcw, rcb, scw, scb = f(rglru_conv_w), f(rglru_conv_b), f(ssd_conv_w), f(ssd_conv_b)
    for l in range(DEPTH):
        cwl = np.concatenate([rcw[l].reshape(4, 2, 128), scw[l].reshape(4, 8, 128)], axis=1).transpose(1, 0, 2)
        blocks += _layer_blocks(w_in[l], w_out[l], w_g[l], w_u[l], w_d[l], cwl)
    wblk = np.ascontiguousarray(np.stack(blocks, 0))
    wa, wx = f(rglru_wa), f(rglru_wx)
    for l in range(DEPTH):
        o = PF_L * l
        pfm[:, o + PF_N1:o + PF_N1 + 8] = _col(f(norm1_w)[l], 8)
        pfm[:, o + PF_N2:o + PF_N2 + 8] = _col(f(norm2_w)[l], 8)
        for ci in range(10):
            for k in range(4):
                src = rcw[l, k, ci * 128:(ci + 1) * 128] if ci < 2 else scw[l, k, (ci - 2) * 128:(ci - 1) * 128]
                pfm[:, o + PF_CW + ci * 4 + k] = src
            pfm[:, o + PF_CB + ci] = rcb[l, ci * 128:(ci + 1) * 128] if ci < 2 else scb[l, (ci - 2) * 128:(ci - 1) * 128]
        pfm[:, o + PF_BA:o + PF_BA + 2] = _col(f(rglru_ba)[l], 2)
        pfm[:, o + PF_BX:o + PF_BX + 2] = _col(f(rglru_bx)[l], 2)
        pfm[:, o + PF_LAM:o + PF_LAM + 2] = _col(f(rglru_lambda)[l], 2)
        pfm[:, PF_LB + 2 * l:PF_LB + 2 * l + 2] = _col(f(hgrn_lb)[l], 2)
        ob = PB_L * l
        pbc[0, ob + PB_DTB:ob + PB_DTB + 8] = f(ssd_dt_bias)[l]
        pbc[0, ob + PB_ALOG:ob + PB_ALOG + 8] = f(ssd_a_log)[l]
        pbc[0, ob + PB_DSK:ob + PB_DSK + 8] = f(ssd_d)[l]
        pbc[0, ob + PB_SNW:ob + PB_SNW + 512] = f(ssd_norm_w)[l]
        pbc[0, ob + PB_HNW:ob + PB_HNW + 256] = f(hgrn_norm_w)[l]
        for which, W in enumerate((wa, wx)):
            for c in range(2):
                base = (l * 4 + which * 2 + c) * 128
                for hh in range(2):
                    wab[hh * 64:(hh + 1) * 64, base + hh * 64:base + (hh + 1) * 64] = W[l, 2 * c + hh]
        wdt[:, l * 64:(l + 1) * 64] = w_in[l][:, C_DT:C_DT + 8].reshape(8, 128, 8).transpose(1, 0, 2).reshape(128, 64)
    pfm[:, PF_FN:PF_FN + 8] = _col(f(final_norm_w), 8)
    s_rc, s_rh, s_sc, s_ss, s_hg = f(state_rglru_conv), f(state_rglru_h), f(state_ssd_conv), f(state_ssd), f(state_hgrn)

    in_maps = []
    for core in range(ncores):
        sconv = np.zeros((128, DEPTH, nsamp, 10, 3), np.float32)
        shA = np.zeros((128, DEPTH, nsamp, 2), np.float32)
        sssd = np.zeros((128, DEPTH, nsamp, 512), np.float32)
        shg = np.zeros((128, DEPTH, nsamp, 128), np.float32)
        for l in range(DEPTH):
            for si in range(nsamp):
                b = core * nsamp + si
                sconv[:, l, si, 0:2, :] = s_rc[l, b].reshape(3, 2, 128).transpose(2, 1, 0)
                sconv[:, l, si, 2:10, :] = s_sc[l, b].reshape(3, 8, 128).transpose(2, 1, 0)
                shA[:, l, si, :] = _col(s_rh[l, b], 2)
                sssd[:, l, si, :] = s_ss[l, b].transpose(2, 0, 1).reshape(128, 512)
                shg[:, l, si, :] = s_hg[l, b].reshape(2, 2, 64, 64).transpose(1, 2, 0, 3).reshape(128, 128)
        in_maps.append({
            "xT_p": np.ascontiguousarray(x_prompt[core].T),
            "xT_s": np.ascontiguousarray(x_sample[core * nsamp:(core + 1) * nsamp].reshape(TS, D).T),
            "wblk": wblk, "pfm": pfm, "pbc": pbc, "wab": wab, "wdt": wdt,
            "s_conv": sconv.reshape(128, -1), "s_hA": shA.reshape(128, -1),
            "s_ssd": sssd.reshape(128, -1), "s_hgrn": shg.reshape(128, -1),
        })

    with contextlib.ExitStack() as stack:
        nc = bass.Bass("TRN2", target_bir_lowering=False)
        build_program(nc, stack, seq, nsamp, lsamp)
        res = run_bass_kernel_spmd(nc, in_maps, core_ids=list(range(ncores)))
    R = res.results

    B, SB = ncores, ncores * nsamp
    y_p = np.zeros((B, seq, D), np.float32)
    y_s = np.zeros((SB, lsamp, D), np.float32)
    outs = {}
    for pre, nb in (("p", B), ("s", SB)):
        outs[pre + "_rc"] = np.zeros((DEPTH, nb, 3, 256), np.float32)
        outs[pre + "_rh"] = np.zeros((DEPTH, nb, 256), np.float32)
        outs[pre + "_sc"] = np.zeros((DEPTH, nb, 3, 1024), np.float32)
        outs[pre + "_ss"] = np.zeros((DEPTH, nb, 8, 64, 128), np.float32)
        outs[pre + "_hg"] = np.zeros((DEPTH, nb, 4, 64, 64), np.float32)
    for core in range(ncores):
        r = R[core]
        y_p[core] = r["yT_p"].T
        ys = r["yT_s"].T.reshape(nsamp, lsamp, D)
        y_s[core * nsamp:(core + 1) * nsamp] = ys
        oc_ = r["o_conv"].reshape(128, 1 + nsamp, DEPTH, 10, 3)
        oh_ = r["o_hA"].reshape(128, 1 + nsamp, DEPTH, 2)
        os_ = r["o_ssd"].reshape(128, 1 + nsamp, DEPTH, 8, 64)
        og_ = r["o_hgrn"].reshape(2, 64, 1 + nsamp, DEPTH, 2, 64)
        for qi in range(1 + nsamp):
            pre, b = ("p", core) if qi == 0 else ("s", core * nsamp + qi - 1)
            for l in range(DEPTH):
                outs[pre + "_rc"][l, b] = oc_[:, qi, l, 0:2, :].transpose(2, 1, 0).reshape(3, 256)
                outs[pre + "_sc"][l, b] = oc_[:, qi, l, 2:10, :].transpose(2, 1, 0).reshape(3, 1024)
                outs[pre + "_rh"][l, b] = oh_[:, qi, l, :].T.reshape(256)
                outs[pre + "_ss"][l, b] = os_[:, qi, l].transpose(1, 2, 0)
                outs[pre + "_hg"][l, b] = og_[:, :, qi, l].transpose(2, 0, 1, 3).reshape(4, 64, 64)
    return (y_p, y_s, outs["p_rc"], outs["p_rh"], outs["p_sc"], outs["p_ss"], outs["p_hg"],
            outs["s_rc"], outs["s_rh"], outs["s_sc"], outs["s_ss"], outs["s_hg"])
```
